# Optimizing a Trainium2 kernel written in Bass

```python
import jax, jax.numpy as jnp
from jax import lax
import numpy as np

D_MODEL = 2048
BATCH = 4
SEQ = 4096
DEPTH = 2

HEAD_DIM = 64
RWKV_WIDTH = 3 * D_MODEL // 8
POOL_WIDTH = D_MODEL // 4
SB_WIDTH = D_MODEL - RWKV_WIDTH - POOL_WIDTH
D_MIX = RWKV_WIDTH + POOL_WIDTH + SB_WIDTH
RWKV_HEADS = RWKV_WIDTH // HEAD_DIM
SB_HEADS = SB_WIDTH // HEAD_DIM
DECAY_RANK = 64
ICLR_RANK = 64
POOL_WINDOWS = (2, 4, 8, 16)
POOL_GROUPS = len(POOL_WINDOWS)
POOL_GROUP_WIDTH = POOL_WIDTH // POOL_GROUPS
SB_BLOCK = 128
RMS_EPS = 1e-6
GN_EPS = 64e-5
A_COLS = 4 * RWKV_WIDTH + DECAY_RANK + ICLR_RANK
B_COLS = 2 * POOL_WIDTH
C_COLS = 4 * SB_WIDTH
IN_COLS = A_COLS + B_COLS + C_COLS

kernel_name = "hymba_rwkv7_pool_stickbreak"


def _rms(x, g):
    xf = x.astype(jnp.float32)
    y = xf * lax.rsqrt(jnp.mean(jnp.square(xf), axis=-1, keepdims=True) + RMS_EPS)
    return (y * g.astype(jnp.float32)).astype(x.dtype)


def _split(u, sizes):
    cuts = [int(c) for c in np.cumsum(sizes)[:-1]]
    return jnp.split(u, cuts, axis=-1)


def _token_shift(u, mu):
    prev = jnp.pad(u, ((0, 0), (1, 0), (0, 0)))[:, :-1]
    return u + (prev - u) * mu


def _rwkv7(r, k, v, wd, ad, w_up, w0, a_up, a0, k_k, k_a, r_k, gn_g, gn_b):
    f32 = jnp.float32
    B, S, C = r.shape
    H, N = C // HEAD_DIM, HEAD_DIM
    r, k, v, wd, ad = (t.astype(f32) for t in (r, k, v, wd, ad))
    logw = -jax.nn.softplus(-(w0 + jnp.tanh(wd) @ w_up)) - 0.5
    decay = jnp.exp(-jnp.exp(logw))
    a = jax.nn.sigmoid(a0 + ad @ a_up)
    kk = (k * k_k).reshape(B, S, H, N)
    kk = kk / jnp.maximum(jnp.sqrt(jnp.sum(jnp.square(kk), axis=-1, keepdims=True)), 1e-12)
    k = k * (1.0 + (a - 1.0) * k_a)

    def heads_t(t):
        return t.reshape(B, S, H, N).transpose(1, 0, 2, 3)

    xs = (heads_t(r), heads_t(k), heads_t(v), heads_t(decay),
          kk.transpose(1, 0, 2, 3), heads_t(a))

    def step(state, inp):
        r_t, k_t, v_t, w_t, kk_t, a_t = inp
        sa = jnp.einsum('bhvk,bhk->bhv', state, -kk_t)
        state = (state * w_t[:, :, None, :]
                 + sa[..., None] * (kk_t * a_t)[:, :, None, :]
                 + v_t[..., None] * k_t[:, :, None, :])
        return state, jnp.einsum('bhvk,bhk->bhv', state, r_t)

    _, y = lax.scan(step, jnp.zeros((B, H, N, N), f32), xs)
    y = y.transpose(1, 0, 2, 3)
    mu = jnp.mean(y, axis=-1, keepdims=True)
    var = jnp.mean(jnp.square(y - mu), axis=-1, keepdims=True)
    y = ((y - mu) * lax.rsqrt(var + GN_EPS)).reshape(B, S, C) * gn_g + gn_b
    bonus = jnp.sum((r * k).reshape(B, S, H, N) * r_k, axis=-1, keepdims=True) * v.reshape(B, S, H, N)
    return y + bonus.reshape(B, S, C)


def _pool(u, pool_w, pool_scale):
    f32 = jnp.float32
    B, S, C = u.shape
    uf = u.astype(f32)
    cs = jnp.cumsum(uf, axis=1)
    pos = jnp.arange(1, S + 1, dtype=f32)[None, :, None]
    groups = []
    for gi, win in enumerate(POOL_WINDOWS):
        sl = slice(gi * POOL_GROUP_WIDTH, (gi + 1) * POOL_GROUP_WIDTH)
        c = cs[..., sl]
        lag = jnp.pad(c, ((0, 0), (win, 0), (0, 0)))[:, :S]
        mean = (c - lag) / jnp.minimum(pos, float(win))
        groups.append(mean - uf[..., sl])
    d = jnp.stack(groups, axis=2)
    y = jnp.einsum('bsgc,gcd->bsgd', d, pool_w.astype(f32)).reshape(B, S, C)
    return y * pool_scale


def _stick_breaking(q, k, v, qn_g, kn_g):
    f32 = jnp.float32
    B, S, C = q.shape
    H = C // HEAD_DIM

    def heads(t):
        return t.reshape(B, S, H, HEAD_DIM).transpose(0, 2, 1, 3)

    q = _rms(heads(q), qn_g).astype(f32) * (HEAD_DIM ** -0.5)
    k = _rms(heads(k), kn_g).astype(f32)
    v = heads(v).astype(f32)
    outs = []
    for q0 in range(0, S, SB_BLOCK):
        kend = q0 + SB_BLOCK
        z = jnp.einsum('bhqd,bhkd->bhqk', q[:, :, q0:kend], k[:, :, :kend])
        causal = jnp.arange(kend)[None, :] < (q0 + jnp.arange(SB_BLOCK))[:, None]
        log_1m = jnp.where(causal, jax.nn.log_sigmoid(-z), 0.0)
        after = lax.cumsum(log_1m, axis=3, reverse=True) - log_1m
        w = jnp.where(causal, jnp.exp(jax.nn.log_sigmoid(z) + after), 0.0)
        outs.append(jnp.einsum('bhqk,bhkd->bhqd', w, v[:, :, :kend]))
    o = jnp.concatenate(outs, axis=2)
    return o.transpose(0, 2, 1, 3).reshape(B, S, C)


def setup_inputs(seed: int = 0) -> dict:
    key = jax.random.key(seed)
    ks = jax.random.split(key, 20)
    f32 = jnp.float32
    nrm = lambda kk, shape, s: jax.random.normal(kk, shape, f32) * s
    L = DEPTH
    return {
        "x": nrm(ks[0], (BATCH, SEQ, D_MODEL), 1.0),
        "norm_g": 1.0 + nrm(ks[1], (L, D_MODEL), 0.02),
        "w_in": nrm(ks[2], (L, D_MODEL, IN_COLS), D_MODEL ** -0.5),
        "mu_a": jax.random.uniform(ks[3], (L, A_COLS), f32, 0.0, 1.0),
        "w_up": nrm(ks[4], (L, DECAY_RANK, RWKV_WIDTH), 0.1 * DECAY_RANK ** -0.5),
        "w0": jax.random.uniform(ks[5], (L, RWKV_WIDTH), f32, -6.0, 1.0),
        "a_up": nrm(ks[6], (L, ICLR_RANK, RWKV_WIDTH), 0.1 * ICLR_RANK ** -0.5),
        "a0": nrm(ks[7], (L, RWKV_WIDTH), 0.1),
        "k_k": 0.85 + nrm(ks[8], (L, RWKV_WIDTH), 0.02),
        "k_a": 1.0 + nrm(ks[9], (L, RWKV_WIDTH), 0.02),
        "r_k": nrm(ks[10], (L, RWKV_HEADS, HEAD_DIM), 0.1),
        "gn_g": 1.0 + nrm(ks[11], (L, RWKV_WIDTH), 0.02),
        "gn_b": nrm(ks[12], (L, RWKV_WIDTH), 0.02),
        "pool_w": nrm(ks[13], (L, POOL_GROUPS, POOL_GROUP_WIDTH, POOL_GROUP_WIDTH), POOL_GROUP_WIDTH ** -0.5),
        "pool_scale": 1.0 + nrm(ks[14], (L, POOL_WIDTH), 0.02),
        "qn_g": 1.0 + nrm(ks[15], (L, HEAD_DIM), 0.02),
        "kn_g": 1.0 + nrm(ks[16], (L, HEAD_DIM), 0.02),
        "w_out": nrm(ks[17], (L, D_MIX, D_MODEL), D_MIX ** -0.5),
    }


def reference(x, norm_g, w_in, mu_a, w_up, w0, a_up, a0, k_k, k_a, r_k, gn_g, gn_b,
              pool_w, pool_scale, qn_g, kn_g, w_out):
    f32 = jnp.float32
    for l in range(DEPTH):
        h = _rms(x, norm_g[l])
        proj = h @ w_in[l]
        pa, pb, pc = _split(proj, (A_COLS, B_COLS, C_COLS))
        pa = _token_shift(pa, mu_a[l])
        r, k, v, ga, wd, ad = _split(pa, (RWKV_WIDTH,) * 4 + (DECAY_RANK, ICLR_RANK))
        ya = _rwkv7(r, k, v, wd, ad, w_up[l], w0[l], a_up[l], a0[l], k_k[l], k_a[l],
                    r_k[l], gn_g[l], gn_b[l]) * jax.nn.silu(ga.astype(f32))
        pu, pg = _split(pb, (POOL_WIDTH, POOL_WIDTH))
        yb = _pool(pu, pool_w[l], pool_scale[l]) * jax.nn.silu(pg.astype(f32))
        qc, kc, vc, gc = _split(pc, (SB_WIDTH,) * 4)
        yc = _stick_breaking(qc, kc, vc, qn_g[l], kn_g[l]) * jax.nn.silu(gc.astype(f32))
        y = jnp.concatenate([ya, yb, yc], axis=-1).astype(x.dtype) @ w_out[l]
        x = x + y.astype(x.dtype)
    return x
```

```python
import contextlib
import math
import numpy as np
import concourse.bass as bass
import concourse.mybir as mybir
from concourse.bass_utils import run_bass_kernel_spmd

F32 = mybir.dt.float32
BF16 = mybir.dt.bfloat16
ALU = mybir.AluOpType
AF = mybir.ActivationFunctionType

ENGS = ("pe", "act", "dve", "pool", "sp")

D = 2048
RW = 768
PW = 512
SW = 768
A_COLS = 4 * RW + 128
B_COLS = 2 * PW
C_COLS = 4 * SW
IN_COLS = A_COLS + B_COLS + C_COLS
C0 = A_COLS + B_COLS
RMS_EPS = 1e-6
GN_EPS = 64e-5
CDEC = math.exp(-0.5)


class Buf:
    __slots__ = ("name", "w", "r", "rd", "excl")

    def __init__(self, name="", excl=False):
        self.name = name
        self.excl = excl
        self.w = None
        self.r = {}
        self.rd = []


class Op:
    __slots__ = ("eng", "fn", "dma", "deps", "seq", "needs_inc", "sem", "val", "prev_val", "xw")

    def __init__(self, eng, fn, dma):
        self.eng = eng
        self.fn = fn
        self.dma = dma
        self.deps = []
        self.needs_inc = False
        self.seq = None
        self.sem = None
        self.val = None
        self.prev_val = 0
        self.xw = None


class Tl:
    __slots__ = ("ap", "b")

    def __init__(self, ap, b):
        self.ap = ap
        self.b = b

    def __getitem__(self, k):
        return Tl(self.ap[k], self.b)

    def re(self, pat, **kw):
        return Tl(self.ap.rearrange(pat, **kw), self.b)

    def bc(self, shape):
        return Tl(self.ap.broadcast_to(shape), self.b)

    def wb(self, b):
        return Tl(self.ap, b)


def _ap(x):
    return x.ap if isinstance(x, Tl) else x


def _bufs(*xs):
    return [x.b for x in xs if isinstance(x, Tl) and x.b is not None]


class Prog:
    def __init__(self, nc, n_dma_sems=8):
        self.nc = nc
        self.ops = {e: [] for e in ENGS}
        self.n_dma_sems = n_dma_sems
        self.dma_rr = {e: 0 for e in ENGS}
        self.dma_cnt = {}

    def _dep(self, op, p, kind):
        if p is None or p is op:
            return
        if (not p.dma) and (not op.dma) and p.eng == op.eng:
            if kind != "raw" or op.eng == "pe":
                return
        op.deps.append(p)
        if not p.dma:
            p.needs_inc = True

    def op(self, eng, fn, reads=(), writes=(), dma=False):
        o = Op(eng, fn, dma)
        writes = list(writes) + [b for b in reads if b.excl and b not in writes]
        reads = [b for b in reads if not b.excl]
        for b in reads:
            self._dep(o, b.w, "raw")
        for b in writes:
            self._dep(o, b.w, "waw")
            for p in b.r.values():
                self._dep(o, p, "war")
            for p in b.rd:
                self._dep(o, p, "war")
        for b in reads:
            if dma:
                b.rd.append(o)
            else:
                b.r[eng] = o
        for b in writes:
            b.w = o
            b.r = {}
            b.rd = []
        if dma:
            k = (eng, self.dma_rr[eng] % self.n_dma_sems)
            self.dma_rr[eng] += 1
            o.sem = k
            o.prev_val = self.dma_cnt.get(k, 0)
            o.val = o.prev_val + 16
            self.dma_cnt[k] = o.val
        self.ops[eng].append(o)
        return o

    def barrier(self):
        lasts = []
        for e in ENGS:
            for o in reversed(self.ops[e]):
                if not o.dma and o.fn is not None:
                    lasts.append(o)
                    break
        snap = dict(self.dma_cnt)
        for e in ENGS:
            o = Op(e, None, False)
            for p in lasts:
                if p.eng != e:
                    o.deps.append(p)
                    p.needs_inc = True
            o.xw = snap
            self.ops[e].append(o)

    def dma(self, eng, out, in_, **kw):
        o_, i_ = _ap(out), _ap(in_)
        return self.op(eng, lambda e: e.dma_start(out=o_, in_=i_, **kw), _bufs(in_), _bufs(out), dma=True)

    def mm(self, out, lhsT, rhs, start=True, stop=True):
        o_, l_, r_ = _ap(out), _ap(lhsT), _ap(rhs)
        return self.op("pe", lambda e: e.matmul(o_, lhsT=l_, rhs=r_, start=start, stop=stop),
                       _bufs(lhsT, rhs), _bufs(out))

    def tr(self, out, in_, ident):
        o_, i_, d_ = _ap(out), _ap(in_), _ap(ident)
        return self.op("pe", lambda e: e.transpose(out=o_, in_=i_, identity=d_), _bufs(in_, ident), _bufs(out))

    def act(self, out, in_, func, bias=None, scale=1.0, accum=None):
        o_, i_ = _ap(out), _ap(in_)
        kw = {"scale": _ap(scale)}
        if bias is not None:
            kw["bias"] = _ap(bias)
        if accum is not None:
            kw["accum_out"] = _ap(accum)
        return self.op("act", lambda e: e.activation(out=o_, in_=i_, func=func, **kw),
                       _bufs(in_, bias, scale), _bufs(out, accum))

    def tt(self, eng, out, in0, in1, op):
        o_, a_, b_ = _ap(out), _ap(in0), _ap(in1)
        return self.op(eng, lambda e: e.tensor_tensor(out=o_, in0=a_, in1=b_, op=op), _bufs(in0, in1), _bufs(out))

    def ts(self, eng, out, in0, s1, s2, op0, op1=None):
        o_, a_, s1_, s2_ = _ap(out), _ap(in0), _ap(s1), _ap(s2)
        if op1 is None:
            fn = lambda e: e.tensor_scalar(out=o_, in0=a_, scalar1=s1_, scalar2=None, op0=op0)
        else:
            fn = lambda e: e.tensor_scalar(out=o_, in0=a_, scalar1=s1_, scalar2=s2_, op0=op0, op1=op1)
        return self.op(eng, fn, _bufs(in0, s1, s2), _bufs(out))

    def stt(self, out, in0, scalar, in1, op0, op1):
        o_, a_, s_, b_ = _ap(out), _ap(in0), _ap(scalar), _ap(in1)
        return self.op("dve", lambda e: e.scalar_tensor_tensor(out=o_, in0=a_, scalar=s_, in1=b_, op0=op0, op1=op1),
                       _bufs(in0, scalar, in1), _bufs(out))

    def copy(self, eng, out, in_):
        o_, i_ = _ap(out), _ap(in_)
        if eng == "act":
            fn = lambda e: e.copy(out=o_, in_=i_)
        else:
            fn = lambda e: e.tensor_copy(out=o_, in_=i_)
        return self.op(eng, fn, _bufs(in_), _bufs(out))

    def recip(self, out, in_):
        o_, i_ = _ap(out), _ap(in_)
        return self.op("dve", lambda e: e.reciprocal(out=o_, in_=i_), _bufs(in_), _bufs(out))

    def scan(self, out, d0, d1, init, op0, op1):
        o_, a_, b_, i_ = _ap(out), _ap(d0), _ap(d1), _ap(init)
        return self.op("dve", lambda e: e.tensor_tensor_scan(out=o_, data0=a_, data1=b_, initial=i_, op0=op0, op1=op1),
                       _bufs(d0, d1, init), _bufs(out))

    def memset(self, eng, out, val):
        o_ = _ap(out)
        return self.op(eng, lambda e: e.memset(o_, val), (), _bufs(out))

    def asel(self, out, in_, pattern, cmp, fill, base, cm):
        o_, i_ = _ap(out), _ap(in_)
        return self.op("pool", lambda e: e.affine_select(out=o_, in_=i_, pattern=pattern, compare_op=cmp, fill=fill,
                                                          base=base, channel_multiplier=cm), _bufs(in_), _bufs(out))

    def emit(self):
        nc = self.nc
        with contextlib.ExitStack() as st:
            esem = {e: st.enter_context(nc.semaphore("s_" + e)) for e in ENGS}
            dsem = {}
            for k in self.dma_cnt:
                dsem[k] = st.enter_context(nc.semaphore("d_%s_%d" % k))
            for e in ENGS:
                n = 0
                for o in self.ops[e]:
                    if o.needs_inc:
                        n += 1
                        o.seq = n
            final_waits = dict(self.dma_cnt)
            block = st.enter_context(nc.Block())

            def run(e, eng):
                waited = {}

                def wait(sem, key, val):
                    if waited.get(key, 0) >= val:
                        return
                    eng.wait_ge(sem, val)
                    waited[key] = val

                for o in self.ops[e]:
                    for p in o.deps:
                        if p.dma:
                            wait(dsem[p.sem], p.sem, p.val)
                        else:
                            wait(esem[p.eng], p.eng, p.seq)
                    if o.xw:
                        for k, v in o.xw.items():
                            wait(dsem[k], k, v)
                    if o.fn is None:
                        continue
                    if o.dma:
                        if o.prev_val:
                            wait(dsem[o.sem], o.sem, o.prev_val)
                        o.fn(eng).then_inc(dsem[o.sem], 16)
                    else:
                        ins = o.fn(eng)
                        if o.needs_inc:
                            ins.then_inc(esem[e], 1)
                if e == "sp":
                    for k, v in final_waits.items():
                        wait(dsem[k], k, v)

            @block.tensor
            def _(eng):
                run("pe", eng)

            @block.scalar
            def _(eng):
                run("act", eng)

            @block.vector
            def _(eng):
                run("dve", eng)

            @block.gpsimd
            def _(eng):
                run("pool", eng)

            @block.sync
            def _(eng):
                run("sp", eng)


class Arena:
    def __init__(self, t, nbytes):
        self.t = t
        self.nbytes = nbytes
        self.off = 0

    def alloc(self, cols, dtype=F32, name="", align=4):
        sz = cols * (2 if dtype == BF16 else 4)
        sz = (sz + 3) // 4 * 4
        self.off = (self.off + align - 1) // align * align
        a = self.off
        self.off += sz
        assert self.off <= self.nbytes, ("arena overflow", name, self.off, self.nbytes)
        ap = self.t[:, a // 4:(a + sz) // 4]
        if dtype == BF16:
            ap = ap.bitcast(BF16)[:, 0:cols]
        return Tl(ap, Buf(name))

    def bank(self, name=""):
        t = self.alloc(512, F32, name, align=2048)
        t.b.excl = True
        return t


class Rot:
    def __init__(self, items):
        self.items = items
        self.i = 0

    def next(self):
        x = self.items[self.i % len(self.items)]
        self.i += 1
        return x


def build(S, L, dbg=False):
    nc = bass.Bass("TRN2", target_bir_lowering=False)
    P = Prog(nc)
    NB = S // 128

    def din(name, shape):
        return nc.dram_tensor(name, shape, F32, kind="ExternalInput").ap()

    x_in = din("x", [S, D])
    norm_g = din("norm_g", [L, D])
    w_in = din("w_in", [L, D, IN_COLS])
    mu_a = din("mu_a", [L, A_COLS])
    w_up = din("w_up", [L, 64, RW])
    w0 = din("w0", [L, RW])
    a_up = din("a_up", [L, 64, RW])
    a0 = din("a0", [L, RW])
    k_k = din("k_k", [L, RW])
    k_a = din("k_a", [L, RW])
    r_k = din("r_k", [L, RW])
    gn_g = din("gn_g", [L, RW])
    gn_b = din("gn_b", [L, RW])
    pool_w = din("pool_w", [L, 4, 128, 128])
    pool_scale = din("pool_scale", [L, PW])
    qn_g = din("qn_g", [L, 64])
    kn_g = din("kn_g", [L, 64])
    w_out = din("w_out", [L, D, D])
    out = nc.dram_tensor("out", [S, D], F32, kind="ExternalOutput").ap()
    skind = "ExternalOutput" if dbg else "Internal"
    dbg_outs = {}

    def dump(name, tile, cols, dt=F32):
        if not dbg or name in dbg_outs:
            return
        dbg_outs[name] = nc.dram_tensor("dbg_" + name, [128, cols], dt, kind="ExternalOutput").ap()
        P.dma("sp", dbg_outs[name], tile)
    pj = nc.dram_tensor("pj", [IN_COLS, S], F32, kind=skind).ap()
    vtok = nc.dram_tensor("vtok", [S, SW], BF16, kind=skind).ap()
    ymix = nc.dram_tensor("ymix", [D, S], BF16, kind=skind).ap()

    with contextlib.ExitStack() as st:
        SB_BYTES = 206 * 1024
        sb_t = st.enter_context(nc.sbuf_tensor("arena", [128, SB_BYTES // 4], F32))
        ps_t = st.enter_context(nc.psum_tensor("psarena", [128, 4096], F32))
        sb = Arena(sb_t, SB_BYTES)
        ps = Arena(ps_t, 16384)

        identf = sb.alloc(128, F32, "identf")
        ident = sb.alloc(128, BF16, "ident")
        M2 = sb.alloc(256, F32, "M2")
        MLs = sb.alloc(128, F32, "MLs")
        MGE = sb.alloc(128, F32, "MGE")
        bones = sb.alloc(128, F32, "bones")
        zero1 = sb.alloc(1, F32, "zero1")
        epsr = sb.alloc(1, F32, "epsr")
        epsg = sb.alloc(1, F32, "epsg")
        TPB = min(512, S)
        reset = sb.alloc(TPB, F32, "reset")
        prm = sb.alloc(80, F32, "prm")
        qgs = sb.alloc(1, F32, "qgs")

        P.memset("pool", identf, 1.0)
        P.asel(identf, identf, [[-1, 128]], ALU.is_equal, 0.0, 0, 1)
        P.copy("dve", ident, identf)
        P.memset("pool", M2, 1.0)
        P.asel(M2[:, 0:128], M2[:, 0:128], [[1, 128]], ALU.is_gt, 0.0, 0, -1)
        P.asel(M2[:, 128:256], M2[:, 128:256], [[1, 128]], ALU.is_ge, 0.0, 0, -1)
        P.memset("pool", MLs, 1.0)
        P.asel(MLs, MLs, [[-1, 128]], ALU.is_gt, 0.0, 0, 1)
        P.ts("dve", MGE, MLs, -1.0, 1.0, ALU.mult, ALU.add)
        P.memset("dve", bones, 0.0)
        P.memset("dve", bones[0:64, 0:64], 1.0)
        P.memset("dve", bones[64:128, 64:128], 1.0)
        P.memset("dve", zero1, 0.0)
        P.memset("dve", epsr, RMS_EPS)
        P.memset("dve", epsg, GN_EPS)
        P.memset("dve", reset, 1.0)
        P.memset("dve", reset.re("p (c n) -> p c n", n=128)[:, :, 0:1], 0.0)
        const_off = sb.off

        PM_MU, PM_W0, PM_A0, PM_KK, PM_KA, PM_RK, PM_GG, PM_GB, PM_PS, PM_QN, PM_KN = 0, 25, 31, 37, 43, 49, 55, 61, 67, 71, 72

        def load_params(l):
            sb.off = const_off
            ps.off = 0
            stg = sb.alloc(128, F32, "prm_stage")
            P.memset("dve", stg, 0.0)
            stg_w = stg

            def ld(row0, src, n):
                P.dma("sp", stg_w[row0:row0 + n, :], src)
            ld(PM_MU, mu_a[l].rearrange("(t p) -> t p", p=128), 25)
            for r0, src in ((PM_W0, w0), (PM_A0, a0), (PM_KK, k_k), (PM_KA, k_a), (PM_RK, r_k), (PM_GG, gn_g),
                            (PM_GB, gn_b)):
                ld(r0, src[l].rearrange("(t p) -> t p", p=128), 6)
            ld(PM_PS, pool_scale[l].rearrange("(t p) -> t p", p=128), 4)
            P.dma("sp", stg_w[PM_QN:PM_QN + 1, 0:64], qn_g[l:l + 1, :])
            P.dma("sp", stg_w[PM_QN:PM_QN + 1, 64:128], qn_g[l:l + 1, :])
            P.dma("sp", stg_w[PM_KN:PM_KN + 1, 0:64], kn_g[l:l + 1, :])
            P.dma("sp", stg_w[PM_KN:PM_KN + 1, 64:128], kn_g[l:l + 1, :])
            pt = ps.bank("prm_ps")
            P.mm(pt[:, 0:73], lhsT=stg[0:73, :], rhs=identf[0:73, 0:73])
            P.copy("dve", prm[:, 0:73], pt[:, 0:73])
            P.ts("dve", qgs, prm[:, PM_QN:PM_QN + 1], 0.125, None, ALU.mult)

        def phase_A(l, xsrc):
            sb.off = const_off
            ps.off = 0
            TOKP = min(2048, S)
            npass = S // TOKP
            nb = TOKP // 128
            gbc = sb.alloc(D, F32, "gbc")
            P.dma("sp", gbc, norm_g[l].partition_broadcast(128))
            hT = sb.alloc(16 * TOKP, BF16, "hT")
            hT3 = hT.re("p (k t) -> p k t", t=TOKP)
            hbufs = [Buf("hT%d" % i) for i in range(TOKP // 512)]
            xrot = Rot([sb.alloc(D, F32, "x%d" % i) for i in range(2)])
            hrot = Rot([sb.alloc(D, BF16, "h%d" % i) for i in range(2)])
            junk = sb.alloc(D, BF16, "junk")
            ssr = Rot([sb.alloc(1, F32, "ss%d" % i) for i in range(4)])
            rsr = Rot([sb.alloc(1, F32, "rs%d" % i) for i in range(4)])
            CG = 384
            wfrot = Rot([sb.alloc(16 * CG, F32, "wf%d" % i) for i in range(2)])
            wbrot = Rot([sb.alloc(16 * CG, BF16, "wb%d" % i) for i in range(2)])
            ostrot = Rot([sb.alloc(512, F32, "ost%d" % i) for i in range(3)])
            vstrot = Rot([sb.alloc(CG, BF16, "vst%d" % i) for i in range(3)])
            trps = Rot([ps.bank("trp%d" % i) for i in range(2)])
            mops = Rot([ps.bank("mo%d" % i) for i in range(4)])
            for pa in range(npass):
                t0 = pa * TOKP
                for tb in range(nb):
                    xt = xrot.next()
                    P.dma("sp", xt, xsrc[t0 + tb * 128:t0 + (tb + 1) * 128, :])
                    ss = ssr.next()
                    rs = rsr.next()
                    P.memset("pool", ss, 0.0)
                    P.act(junk, xt, AF.Square, accum=ss)
                    P.act(rs, ss, AF.Sqrt, bias=epsr, scale=1.0 / D)
                    P.recip(rs, rs)
                    hb = hrot.next()
                    P.stt(hb, xt, rs, gbc, ALU.mult, ALU.mult)
                    hbuf = hbufs[tb // 4]
                    for q in range(4):
                        tp = trps.next()
                        tpb = Tl(tp.ap.bitcast(BF16)[:, 0:512], tp.b)
                        for j in range(4):
                            kc = 4 * q + j
                            P.tr(tpb[:, j * 128:(j + 1) * 128], hb[:, kc * 128:(kc + 1) * 128], ident)
                        dst = Tl(hT3.ap[:, 4 * q:4 * q + 4, tb * 128:(tb + 1) * 128], hbuf)
                        P.copy("act" if q % 2 == 0 else "dve", dst, tpb.re("p (j t) -> p j t", t=128))
                for cg in range(IN_COLS // CG):
                    wf = wfrot.next()
                    wsrc = w_in[l, :, cg * CG:(cg + 1) * CG].rearrange("(k p) c -> p k c", p=128)
                    wf3 = wf.re("p (k c) -> p k c", c=CG)
                    P.dma("sp", wf3[:, 0:8, :], wsrc[:, 0:8, :])
                    P.dma("pool", wf3[:, 8:16, :], wsrc[:, 8:16, :])
                    wb = wbrot.next()
                    wb3 = wb.re("p (k c) -> p k c", c=CG)
                    P.copy("pool", wb3[:, 0:6, :], wf3[:, 0:6, :])
                    P.copy("dve", wb3[:, 6:16, :], wf3[:, 6:16, :])
                    if cg not in (15, 16):
                        for ci in range(3):
                            ct = cg * 3 + ci
                            for tc in range(TOKP // 512):
                                po = mops.next()
                                for k in range(16):
                                    rhs = Tl(hT3.ap[:, k, tc * 512:(tc + 1) * 512], hbufs[tc])
                                    P.mm(po, lhsT=wb3[:, k, ci * 128:(ci + 1) * 128], rhs=rhs, start=(k == 0),
                                         stop=(k == 15))
                                ost = ostrot.next()
                                P.copy("act", ost, po)
                                P.dma("act", pj[ct * 128:(ct + 1) * 128, t0 + tc * 512:t0 + (tc + 1) * 512], ost)
                    else:
                        for tb in range(nb):
                            po = mops.next()
                            for k in range(16):
                                lhsT = Tl(hT3.ap[:, k, tb * 128:(tb + 1) * 128], hbufs[tb // 4])
                                P.mm(po[:, 0:CG], lhsT=lhsT, rhs=wb3[:, k, :], start=(k == 0), stop=(k == 15))
                            vst = vstrot.next()
                            P.copy("act", vst, po[:, 0:CG])
                            P.dma("act", vtok[t0 + tb * 128:t0 + (tb + 1) * 128, (cg - 15) * CG:(cg - 14) * CG], vst)

        def phase_B(l):
            sb.off = const_off
            ps.off = 0
            TP = TPB
            NCH = TP // 128
            npiece = S // TP
            wup = sb.alloc(RW, F32, "wup")
            P.dma("sp", wup[0:64, :], w_up[l])
            P.dma("sp", wup[64:128, :], a_up[l])
            Sf = [[sb.alloc(64, F32, "Sf%d_%d" % (hp, i)) for i in range(2)] for hp in range(6)]
            Sb = [[sb.alloc(64, BF16, "Sb%d_%d" % (hp, i)) for i in range(2)] for hp in range(6)]
            cur = [0] * 6
            for hp in range(6):
                P.memset("dve", Sf[hp][0], 0.0)
                P.memset("dve", Sb[hp][0], 0.0)

            def f32t(name, n=TP):
                return sb.alloc(n, F32, name)

            def bft(name, n=TP):
                return sb.alloc(n, BF16, name)
            lda, tmpA, wa, wa2 = f32t("lda", TP + 1), f32t("tmpA"), f32t("wa"), f32t("wa2")
            ldr = Rot([[f32t("ld%s%d" % (nm, i), TP + 1) for nm in "rkvg"] for i in range(2)])
            tmpd = f32t("tmpd")
            rs_, ks_, vs_, gs_ = f32t("rs"), f32t("ks"), f32t("vs"), f32t("gs")
            sig, a_, cs, W, Winv, csm, Wm, e2, E2 = (f32t(n) for n in
                                                     ("sig", "a", "cs", "W", "Winv", "csm", "Wm", "e2", "E2"))
            kk, kk2, nrm, kkn, tmpk, kp, b_ = (f32t(n) for n in ("kk", "kk2", "nrm", "kkn", "tmpk", "kp", "b"))
            rkr, bon, sg = f32t("rkr"), f32t("bon"), f32t("sg")
            ysb, yc, sq, rstd, yn = f32t("ysb"), f32t("yc"), f32t("sq"), f32t("rstd"), f32t("yn")
            AR = bft("AR", 2 * TP)
            AR4 = AR.re("p (c w n) -> p c w n", w=2, n=128)
            KH, BH, KB, BB, VB = bft("KH"), bft("BH"), bft("KB"), bft("BB"), bft("VB")
            yarot = Rot([bft("ya%d" % i) for i in range(2)])
            TOKr = Rot([bft("TOK%d" % i, 384) for i in range(2)])
            NIT = 4
            SA1 = [bft("SA1_%d" % i, 256) for i in range(NIT)]
            SA2 = [bft("SA2_%d" % i, 256) for i in range(NIT)]
            Xr = [Rot([bft("X%d_%d" % (i, j), 128) for j in range(2)]) for i in range(NIT)]
            XTr = [Rot([bft("XT%d_%d" % (i, j), 128) for j in range(2)]) for i in range(NIT)]
            Pr = [Rot([bft("P%d_%d" % (i, j), 128) for j in range(2)]) for i in range(NIT)]
            Zbr = Rot([bft("Zb%d" % i, 64) for i in range(2)])
            Ubr = Rot([bft("Ub%d" % i, 64) for i in range(2)])
            pgen = Rot([ps.bank("pgen%d" % i) for i in range(1)])
            psc = Rot([ps.bank("psc%d" % i) for i in range(2)])
            pinv = [ps.bank("pinv%d" % i) for i in range(4)]
            pinv_r = Rot(pinv)
            pmisc = ps.bank("pmisc")
            ptr = Tl(pmisc.ap.bitcast(BF16)[:, 0:384], pmisc.b)
            pzr = Rot([Tl(pmisc.ap[:, 192:256], pmisc.b), Tl(pmisc.ap[:, 256:320], pmisc.b)])
            pur = Rot([Tl(pmisc.ap[:, 320:384], pmisc.b), Tl(pmisc.ap[:, 384:448], pmisc.b)])
            psn = Tl(pmisc.ap[:, 448:512], pmisc.b)
            for pc in range(npiece):
                t0 = pc * TP

                def load_shift(dst, row0, eng="sp"):
                    if t0 > 0:
                        P.dma(eng, dst, pj[row0:row0 + 128, t0 - 1:t0 + TP])
                    else:
                        P.memset("pool", dst[:, 0:1], 0.0)
                        P.dma(eng, dst[:, 1:TP + 1], pj[row0:row0 + 128, 0:TP])

                def lerp(dst, ld, mucol):
                    P.tt("pool", tmpd, ld[:, 0:TP], ld[:, 1:TP + 1], ALU.subtract)
                    P.stt(dst, tmpd, prm[:, mucol:mucol + 1], ld[:, 1:TP + 1], ALU.mult, ALU.add)
                load_shift(lda, 3072)
                lerp(wa, lda, PM_MU + 24)
                P.act(wa2[0:64, :], wa[0:64, :], AF.Tanh)
                P.copy("pool", wa2[64:128, :], wa[64:128, :])
                for hp in range(6):
                    lds = ldr.next()
                    for i, (ld, r0) in enumerate(zip(lds, (0, 768, 1536, 2304))):
                        load_shift(ld, r0 + hp * 128, "sp" if i % 2 == 0 else "act")
                    lerp(rs_, lds[0], PM_MU + hp)
                    lerp(ks_, lds[1], PM_MU + 6 + hp)
                    lerp(vs_, lds[2], PM_MU + 12 + hp)
                    lerp(gs_, lds[3], PM_MU + 18 + hp)
                    pd = pgen.next()
                    P.mm(pd[:, 0:TP], lhsT=wup[0:64, hp * 128:(hp + 1) * 128], rhs=wa2[0:64, :])
                    P.act(sig, pd[:, 0:TP], AF.Sigmoid, bias=prm[:, PM_W0 + hp:PM_W0 + hp + 1])
                    pa = pgen.next()
                    P.mm(pa[:, 0:TP], lhsT=wup[64:128, hp * 128:(hp + 1) * 128], rhs=wa2[64:128, :])
                    P.act(a_, pa[:, 0:TP], AF.Sigmoid, bias=prm[:, PM_A0 + hp:PM_A0 + hp + 1])
                    P.scan(cs, reset, sig, 0.0, ALU.mult, ALU.add)
                    P.act(W, cs, AF.Exp, scale=-CDEC)
                    P.act(Winv, cs, AF.Exp, scale=CDEC)
                    P.tt("pool", csm, cs, sig, ALU.subtract)
                    P.act(Wm, csm, AF.Exp, scale=-CDEC)
                    cs3 = cs.re("p (c n) -> p c n", n=128)
                    P.tt("dve", e2.re("p (c n) -> p c n", n=128), cs3[:, :, 127:128].bc([128, NCH, 128]), cs3,
                         ALU.subtract)
                    P.act(E2, e2, AF.Exp, scale=-CDEC)
                    P.ts("dve", kk, ks_, prm[:, PM_KK + hp:PM_KK + hp + 1], None, ALU.mult)
                    P.act(kk2, kk, AF.Square)
                    pn = pgen.next()
                    P.mm(pn[:, 0:TP], lhsT=bones, rhs=kk2)
                    P.act(nrm, pn[:, 0:TP], AF.Sqrt)
                    P.ts("dve", nrm, nrm, 1e-12, None, ALU.max)
                    P.recip(nrm, nrm)
                    P.tt("pool", kkn, kk, nrm, ALU.mult)
                    P.ts("dve", tmpk, a_, -1.0, prm[:, PM_KA + hp:PM_KA + hp + 1], ALU.add, ALU.mult)
                    P.stt(kp, tmpk, 1.0, ks_, ALU.add, ALU.mult)
                    P.tt("pool", b_, kkn, a_, ALU.mult)
                    TPv = "p (c n) -> p c n"
                    P.stt(AR4[:, :, 0, :], kkn.re(TPv, n=128), -1.0, Wm.re(TPv, n=128), ALU.mult, ALU.mult)
                    P.tt("pool", AR4[:, :, 1, :], rs_.re(TPv, n=128), W.re(TPv, n=128), ALU.mult)
                    P.tt("dve", KH, kp, Winv, ALU.mult)
                    P.tt("pool", BH, b_, Winv, ALU.mult)
                    P.tt("dve", KB, kp, E2, ALU.mult)
                    P.tt("pool", BB, b_, E2, ALU.mult)
                    P.copy("act", VB, vs_)
                    P.stt(rkr, rs_, prm[:, PM_RK + hp:PM_RK + hp + 1], kp, ALU.mult, ALU.mult)
                    pb = pgen.next()
                    P.mm(pb[:, 0:TP], lhsT=bones, rhs=rkr)
                    P.tt("dve", bon, pb[:, 0:TP], vs_, ALU.mult)
                    P.act(sg, gs_, AF.Silu)
                    if hp == 0 and pc == 0:
                        for nm, t_ in (("rs", rs_), ("ks", ks_), ("vs", vs_), ("sig", sig), ("a", a_), ("kkn", kkn),
                                       ("kp", kp), ("b", b_), ("bon", bon), ("sg", sg), ("W", W), ("Wm", Wm),
                                       ("Winv", Winv), ("E2", E2)):
                            dump(nm, t_, TP)
                        dump("AR", AR, 2 * TP, BF16)
                        dump("KH", KH, TP, BF16)
                        dump("BH", BH, TP, BF16)
                    for cb in range(0, NCH, 2):
                        chunks = [c for c in (cb, cb + 1) if c < NCH]
                        items = [(c, h) for c in chunks for h in range(2)]
                        TT = {}
                        Xs, XTs, Ps = {}, {}, {}
                        for it, (c, h) in enumerate(items):
                            sl = slice(64 * h, 64 * h + 64)
                            csl = slice(c * 128, (c + 1) * 128)
                            pscb = psc.next()
                            ps1 = Tl(pscb.ap[:, 0:256], pscb.b)
                            ps2 = Tl(pscb.ap[:, 256:512], pscb.b)
                            arc = Tl(AR4.ap[sl, c].rearrange("p w n -> p (w n)"), AR.b)
                            P.mm(ps1, lhsT=BH[sl, csl], rhs=arc)
                            P.mm(ps2, lhsT=KH[sl, csl], rhs=arc)
                            ps3 = pinv[it][:, 384:512]
                            P.mm(ps3, lhsT=AR4[sl, c, 0, :], rhs=BH[sl, csl])
                            P.tt("dve", SA1[it], ps1, M2, ALU.mult)
                            P.tt("dve", SA2[it], ps2, M2, ALU.mult)
                            xt0 = XTr[it].next()
                            P.tt("dve", xt0, ps3, MLs, ALU.mult)
                            p0 = Pr[it].next()
                            P.tt("pool", p0, SA1[it][:, 0:128], ident, ALU.add)
                            Xs[it], XTs[it], Ps[it] = SA1[it][:, 0:128], xt0, p0
                        for k in range(7):
                            pend = []
                            for it in range(len(items)):
                                pbank = pinv[it]
                                pp = pbank[:, 0:128]
                                if k >= 1:
                                    P.mm(pp, lhsT=XTs[it], rhs=Ps[it])
                                if k < 6:
                                    px = pbank[:, 128:256]
                                    P.mm(px, lhsT=XTs[it], rhs=Xs[it])
                                    pxt = pbank[:, 256:384]
                                    P.mm(pxt, lhsT=Xs[it], rhs=XTs[it])
                                else:
                                    px = pxt = None
                                pend.append((pp, px, pxt))
                            for it in range(len(items)):
                                pp, px, pxt = pend[it]
                                if k >= 1:
                                    pn_ = Pr[it].next()
                                    P.tt("dve", pn_, pp, Ps[it], ALU.add)
                                    Ps[it] = pn_
                                if k < 6:
                                    xn = Xr[it].next()
                                    P.copy("act", xn, px)
                                    xtn = XTr[it].next()
                                    P.copy("act", xtn, pxt)
                                    Xs[it], XTs[it] = xn, xtn
                        if hp == 0 and pc == 0 and cb == 0:
                            for it_ in range(2):
                                dump("SA1_%d" % it_, SA1[it_], 256, BF16)
                                dump("SA2_%d" % it_, SA2[it_], 256, BF16)
                                dump("TT_%d" % it_, Ps[it_], 128, BF16)
                        for c in chunks:
                            csl = slice(c * 128, (c + 1) * 128)
                            TOK = TOKr.next()
                            P.tr(ptr[:, 0:128], VB[:, csl], ident)
                            P.tr(ptr[:, 128:256], KB[:, csl], ident)
                            P.tr(ptr[:, 256:384], BB[:, csl], ident)
                            P.copy("act", TOK, ptr)
                            pyb = pgen.next()
                            sfo, sbo = Sf[hp][cur[hp]], Sb[hp][cur[hp]]
                            sfn, sbn = Sf[hp][1 - cur[hp]], Sb[hp][1 - cur[hp]]
                            for h in range(2):
                                it = items.index((c, h))
                                sl = slice(64 * h, 64 * h + 64)
                                vt = TOK[:, 64 * h:64 * h + 64]
                                kbt = TOK[:, 128 + 64 * h:128 + 64 * h + 64]
                                bbt = TOK[:, 256 + 64 * h:256 + 64 * h + 64]
                                pz = pzr.next()
                                P.mm(pz, lhsT=AR4[sl, c, 0, :], rhs=sbo[sl, :], start=True, stop=False)
                                P.mm(pz, lhsT=SA2[it][:, 0:128], rhs=vt, start=False, stop=True)
                                zb = Zbr.next()
                                P.copy("act", zb, pz)
                                pu = pur.next()
                                P.mm(pu, lhsT=Ps[it], rhs=zb)
                                ub = Ubr.next()
                                P.copy("act", ub, pu)
                                if hp == 0 and pc == 0 and c == 0:
                                    dump("Zb_%d" % h, zb, 64, BF16)
                                    dump("Ub_%d" % h, ub, 64, BF16)
                                    dump("TOK", TOK, 384, BF16)
                                P.mm(pyb[sl, 0:128], lhsT=sbo[sl, :], rhs=AR4[sl, c, 1, :], start=True, stop=False)
                                P.mm(pyb[sl, 0:128], lhsT=ub, rhs=SA1[it][:, 128:256], start=False, stop=False)
                                P.mm(pyb[sl, 0:128], lhsT=vt, rhs=SA2[it][:, 128:256], start=False, stop=True)
                                P.mm(psn[sl, :], lhsT=bbt, rhs=ub, start=True, stop=False)
                                P.mm(psn[sl, :], lhsT=kbt, rhs=vt, start=False, stop=True)
                            P.copy("act", ysb[:, csl], pyb[:, 0:128])
                            wc = W[:, c * 128 + 127:c * 128 + 128]
                            P.stt(sfn, sfo, wc, psn, ALU.mult, ALU.add)
                            P.stt(sbn, sfo, wc, psn, ALU.mult, ALU.add)
                            cur[hp] = 1 - cur[hp]
                    pm = pgen.next()
                    P.mm(pm[:, 0:TP], lhsT=bones, rhs=ysb)
                    P.stt(yc, pm[:, 0:TP], -1.0 / 64, ysb, ALU.mult, ALU.add)
                    P.act(sq, yc, AF.Square)
                    pv = pgen.next()
                    P.mm(pv[:, 0:TP], lhsT=bones, rhs=sq)
                    P.act(rstd, pv[:, 0:TP], AF.Sqrt, bias=epsg, scale=1.0 / 64)
                    P.recip(rstd, rstd)
                    P.tt("pool", yn, yc, rstd, ALU.mult)
                    P.ts("dve", yn, yn, prm[:, PM_GG + hp:PM_GG + hp + 1], prm[:, PM_GB + hp:PM_GB + hp + 1], ALU.mult,
                         ALU.add)
                    P.tt("pool", yn, yn, bon, ALU.add)
                    if hp == 0 and pc == 0:
                        dump("ysb", ysb, TP)
                        dump("yn", yn, TP)
                    ya = yarot.next()
                    P.tt("dve", ya, yn, sg, ALU.mult)
                    P.dma("sp", ymix[hp * 128:(hp + 1) * 128, t0:t0 + TP], ya)

        def phase_C(l):
            sb.off = const_off
            ps.off = 0
            TP = min(1024, S)
            npiece = S // TP
            H = 16
            pwf = sb.alloc(512, F32, "pwf")
            pwb = sb.alloc(512, BF16, "pwb")
            P.dma("sp", pwf.re("p (g d) -> p g d", d=128), pool_w[l].rearrange("g c d -> c g d"))
            P.copy("dve", pwb, pwf)
            corr = sb.alloc(4 * 16, F32, "corr")
            for g, win in enumerate((2, 4, 8, 16)):
                for t in range(16):
                    P.memset("pool", corr[:, g * 16 + t:g * 16 + t + 1], float(win) / min(t + 1, win))
            urot = Rot([sb.alloc(TP + H, F32, "pu%d" % i) for i in range(2)])
            grot = Rot([sb.alloc(TP, F32, "pg%d" % i) for i in range(2)])
            sA = sb.alloc(TP + H, F32, "sA")
            sB = sb.alloc(TP + H, F32, "sB")
            dbf = sb.alloc(TP, BF16, "dbf")
            sgp = sb.alloc(TP, F32, "sgp")
            ybr = Rot([sb.alloc(TP, BF16, "yb%d" % i) for i in range(2)])
            pyr = Rot([ps.bank("pc%d" % i) for i in range(2)])
            for pc in range(npiece):
                t0 = pc * TP
                for g, win in enumerate((2, 4, 8, 16)):
                    u = urot.next()
                    row = A_COLS + g * 128
                    if t0 > 0:
                        P.dma("sp", u, pj[row:row + 128, t0 - H:t0 + TP])
                    else:
                        P.memset("pool", u[:, 0:H], 0.0)
                        P.dma("sp", u[:, H:H + TP], pj[row:row + 128, 0:TP])
                    pg_ = grot.next()
                    P.dma("act", pg_, pj[A_COLS + PW + g * 128:A_COLS + PW + (g + 1) * 128, t0:t0 + TP])
                    src = u
                    sh = 1
                    dsts = [sA, sB]
                    lo = 0
                    for lev in range(g + 1):
                        dst = dsts[lev % 2]
                        lo += sh
                        P.tt("pool" if lev % 2 else "dve", dst[:, lo:TP + H], src[:, lo:TP + H], src[:, lo - sh:TP + H - sh],
                             ALU.add)
                        src = dst
                        sh *= 2
                    ssum = src
                    if t0 == 0:
                        P.tt("dve", ssum[:, H:H + 16], ssum[:, H:H + 16], corr[:, g * 16:(g + 1) * 16], ALU.mult)
                    P.stt(dbf, ssum[:, H:H + TP], 1.0 / win, u[:, H:H + TP], ALU.mult, ALU.subtract)
                    P.act(sgp, pg_, AF.Silu)
                    yb = ybr.next()
                    for hf in range(TP // 512):
                        py = pyr.next()
                        P.mm(py, lhsT=pwb[:, g * 128:(g + 1) * 128], rhs=dbf[:, hf * 512:(hf + 1) * 512])
                        P.stt(yb[:, hf * 512:(hf + 1) * 512], py, prm[:, PM_PS + g:PM_PS + g + 1],
                              sgp[:, hf * 512:(hf + 1) * 512], ALU.mult, ALU.mult)
                    P.dma("sp", ymix[RW + g * 128:RW + (g + 1) * 128, t0:t0 + TP], yb)

        def phase_D(l):
            sb.off = const_off
            ps.off = 0
            TPq = min(1024, S)
            QH = sb.alloc(S, BF16, "QH")
            KHt = sb.alloc(S, BF16, "KHt")
            V = sb.alloc(NB * 128, BF16, "V")
            V3 = V.re("p (n c) -> p n c", c=128)
            sgc = sb.alloc(S, BF16, "sgc")
            ycst = sb.alloc(S, BF16, "ycst")
            SGr = Rot([sb.alloc(S, F32, "SG%d" % i) for i in range(2)])
            EZr = Rot([sb.alloc(S, BF16, "EZ%d" % i) for i in range(2)])
            Pn = sb.alloc(S, F32, "Pn")
            Wtr = Rot([sb.alloc(S, BF16, "Wt%d" % i) for i in range(2)])
            WTr = Rot([sb.alloc(S, BF16, "WT%d" % i) for i in range(2)])
            qf = Rot([sb.alloc(TPq, F32, "qf%d" % i) for i in range(2)])
            sqq = sb.alloc(TPq, F32, "sqq")
            rt = sb.alloc(TPq, F32, "rt")
            zr = Rot([ps.bank("z%d" % i) for i in range(3)])
            ptwr = Rot([ps.bank("ptw%d" % i) for i in range(2)])
            por = Rot([ps.bank("po%d" % i) for i in range(2)])
            pgen = ps.bank("pgenD")
            for hp in range(6):
                P.dma("sp", V3, vtok[:, hp * 128:(hp + 1) * 128].rearrange("(n p) c -> p n c", p=128))
                for (dst, row0, gcol) in ((QH, C0 + hp * 128, qgs), (KHt, C0 + SW + hp * 128, prm[:, PM_KN:PM_KN + 1])):
                    for pc in range(S // TPq):
                        q = qf.next()
                        P.dma("sp", q, pj[row0:row0 + 128, pc * TPq:(pc + 1) * TPq])
                        P.act(sqq, q, AF.Square)
                        for hf in range(TPq // 512):
                            hs = slice(hf * 512, (hf + 1) * 512)
                            P.mm(pgen, lhsT=bones, rhs=sqq[:, hs])
                            P.act(rt[:, hs], pgen, AF.Sqrt, bias=epsr, scale=1.0 / 64)
                        P.recip(rt, rt)
                        P.stt(dst[:, pc * TPq:(pc + 1) * TPq], q, gcol, rt, ALU.mult, ALU.mult)
                for pc in range(S // TPq):
                    q = qf.next()
                    r0 = C0 + 3 * SW + hp * 128
                    P.dma("act", q, pj[r0:r0 + 128, pc * TPq:(pc + 1) * TPq])
                    P.act(sgc[:, pc * TPq:(pc + 1) * TPq], q, AF.Silu)
                items = [(T, h) for T in range(NB) for h in range(2)]
                state = {}

                def stage1(T, h):
                    sl = slice(64 * h, 64 * h + 64)
                    kend = (T + 1) * 128
                    SG = SGr.next()
                    EZ = EZr.next()
                    for ck in range((kend + 511) // 512):
                        w_ = min(512, kend - ck * 512)
                        pz = zr.next()
                        P.mm(pz[:, 0:w_], lhsT=QH[sl, T * 128:(T + 1) * 128], rhs=KHt[sl, ck * 512:ck * 512 + w_])
                        P.act(SG[:, ck * 512:ck * 512 + w_], pz[:, 0:w_], AF.Sigmoid, scale=-1.0)
                        P.act(EZ[:, ck * 512:ck * 512 + w_], pz[:, 0:w_], AF.Exp)
                    dsl = slice(T * 128, kend)
                    P.tt("pool", SG[:, dsl], SG[:, dsl], MLs, ALU.mult)
                    P.tt("pool", SG[:, dsl], SG[:, dsl], MGE, ALU.add)
                    P.tt("pool", EZ[:, dsl], EZ[:, dsl], MLs, ALU.mult)
                    P.scan(Pn[:, 0:kend][:, ::-1], SG[:, 0:kend][:, ::-1], zero1.bc([128, kend]), 1.0, ALU.mult, ALU.add)
                    Wt = Wtr.next()
                    P.tt("pool", Wt[:, 0:kend], EZ[:, 0:kend], Pn[:, 0:kend], ALU.mult)
                    state[(T, h)] = Wt

                def stage2(T, h):
                    sl = slice(64 * h, 64 * h + 64)
                    Wt = state.pop((T, h))
                    WT = WTr.next()
                    nsb = T + 1
                    for g0 in range(0, nsb, 4):
                        n = min(4, nsb - g0)
                        pt = ptwr.next()
                        ptb = Tl(pt.ap.bitcast(BF16)[:, 0:512], pt.b)
                        for j in range(n):
                            P.tr(ptb[:, j * 128:(j + 1) * 128], Wt[:, (g0 + j) * 128:(g0 + j + 1) * 128], ident)
                        P.copy("act" if (g0 // 4) % 2 == 0 else "dve", WT[:, g0 * 128:(g0 + n) * 128], ptb[:, 0:n * 128])
                    if h == 0:
                        state["po"] = por.next()
                    po = state["po"]
                    for sbk in range(nsb):
                        P.mm(po[sl, 0:128], lhsT=V3[:, sbk, 64 * h:64 * h + 64], rhs=WT[:, sbk * 128:(sbk + 1) * 128],
                             start=(sbk == 0), stop=(sbk == nsb - 1))
                    if h == 1:
                        P.tt("dve", ycst[:, T * 128:(T + 1) * 128], po[:, 0:128], sgc[:, T * 128:(T + 1) * 128], ALU.mult)

                for i in range(len(items) + 1):
                    if i < len(items):
                        stage1(*items[i])
                    if i >= 1:
                        stage2(*items[i - 1])
                P.dma("sp", ymix[RW + PW + hp * 128:RW + PW + (hp + 1) * 128, :], ycst)

        def phase_E(l, xsrc):
            sb.off = const_off
            ps.off = 0
            wo = sb.alloc(16 * D, BF16, "wo")
            wo3 = wo.re("p (k c) -> p k c", c=D)
            wst = Rot([sb.alloc(4 * D, F32, "wst%d" % i) for i in range(2)])
            for kq in range(4):
                w = wst.next()
                w3 = w.re("p (k c) -> p k c", c=D)
                P.dma("sp" if kq % 2 == 0 else "act", w3,
                      w_out[l, kq * 512:(kq + 1) * 512, :].rearrange("(k p) c -> p k c", p=128))
                P.copy("pool", wo3[:, kq * 4:kq * 4 + 2, :], w3[:, 0:2, :])
                P.copy("dve", wo3[:, kq * 4 + 2:kq * 4 + 4, :], w3[:, 2:4, :])
            TQ = 512
            yr = Rot([sb.alloc(16 * TQ, BF16, "ymx%d" % i) for i in range(2)])
            xr = Rot([sb.alloc(D, F32, "xe%d" % i) for i in range(3)])
            pr = Rot([ps.bank("pe%d" % i) for i in range(8)])
            for tq in range(S // TQ):
                ym = yr.next()
                ym3 = ym.re("p (k t) -> p k t", t=TQ)
                P.dma("sp", ym3, ymix[:, tq * TQ:(tq + 1) * TQ].rearrange("(k p) t -> p k t", p=128))
                for tb in range(TQ // 128):
                    r0 = tq * TQ + tb * 128
                    xe = xr.next()
                    P.dma("sp", xe, xsrc[r0:r0 + 128, :])
                    for cq in range(4):
                        po = pr.next()
                        for k in range(16):
                            P.mm(po, lhsT=ym3[:, k, tb * 128:(tb + 1) * 128], rhs=wo3[:, k, cq * 512:(cq + 1) * 512],
                                 start=(k == 0), stop=(k == 15))
                        P.tt("dve", xe[:, cq * 512:(cq + 1) * 512], po, xe[:, cq * 512:(cq + 1) * 512], ALU.add)
                    P.dma("act", out[r0:r0 + 128, :], xe)

        for l in range(L):
            xsrc = x_in if l == 0 else out
            load_params(l)
            P.barrier()
            phase_A(l, xsrc)
            P.barrier()
            phase_B(l)
            P.barrier()
            phase_C(l)
            P.barrier()
            phase_D(l)
            P.barrier()
            phase_E(l, xsrc)
            P.barrier()
        P.emit()
    import os
    if os.environ.get("KSTATS"):
        print("op counts", {e: len(P.ops[e]) for e in ENGS}, flush=True)
    return nc


_CACHE = {}


def run(inputs, S, L, n_cores, dbg=False, trace=False):
    key = (S, L, dbg)
    if key not in _CACHE:
        _CACHE[key] = build(S, L, dbg)
    nc = _CACHE[key]
    x = np.ascontiguousarray(np.asarray(inputs["x"], dtype=np.float32))
    B = x.shape[0]
    shared = {}
    for k, v in inputs.items():
        if k == "x":
            continue
        a = np.ascontiguousarray(np.asarray(v, dtype=np.float32))
        if k == "r_k":
            a = a.reshape(a.shape[0], -1)
        shared[k] = a
    in_maps = []
    for c in range(n_cores):
        m = dict(shared)
        m["x"] = x[c % B]
        in_maps.append(m)
    res = run_bass_kernel_spmd(nc, in_maps, core_ids=list(range(n_cores)))
    return res


def kernel(**inputs):
    x = np.asarray(inputs["x"])
    B, S, _ = x.shape
    L = np.asarray(inputs["norm_g"]).shape[0]
    res = run(inputs, S, L, 8)
    return np.stack([np.asarray(res.results[b]["out"], dtype=np.float32) for b in range(B)], axis=0)
```

```python
import contextlib
import math
import numpy as np
import concourse.bass as bass
import concourse.mybir as mybir
from concourse.bass_utils import run_bass_kernel_spmd

F32 = mybir.dt.float32
BF16 = mybir.dt.bfloat16
ALU = mybir.AluOpType
AF = mybir.ActivationFunctionType

ENGS = ("pe", "act", "dve", "pool", "sp")

D = 2048
RW = 768
PW = 512
SW = 768
A_COLS = 4 * RW + 128
B_COLS = 2 * PW
C_COLS = 4 * SW
IN_COLS = A_COLS + B_COLS + C_COLS
C0 = A_COLS + B_COLS
RMS_EPS = 1e-6
GN_EPS = 64e-5
CDEC = math.exp(-0.5)


class Buf:
    __slots__ = ("name", "w", "r", "rd", "excl")

    def __init__(self, name="", excl=False):
        self.name = name
        self.excl = excl
        self.w = None
        self.r = {}
        self.rd = []


class Op:
    __slots__ = ("eng", "fn", "dma", "deps", "seq", "needs_inc", "sem", "val", "prev_val", "xw")

    def __init__(self, eng, fn, dma):
        self.eng = eng
        self.fn = fn
        self.dma = dma
        self.deps = []
        self.needs_inc = False
        self.seq = None
        self.sem = None
        self.val = None
        self.prev_val = 0
        self.xw = None


class Tl:
    __slots__ = ("ap", "b")

    def __init__(self, ap, b):
        self.ap = ap
        self.b = b

    def __getitem__(self, k):
        return Tl(self.ap[k], self.b)

    def re(self, pat, **kw):
        return Tl(self.ap.rearrange(pat, **kw), self.b)

    def bc(self, shape):
        return Tl(self.ap.broadcast_to(shape), self.b)

    def wb(self, b):
        return Tl(self.ap, b)


def _ap(x):
    return x.ap if isinstance(x, Tl) else x


def _bufs(*xs):
    return [x.b for x in xs if isinstance(x, Tl) and x.b is not None]


class Prog:
    def __init__(self, nc, n_dma_sems=8):
        self.nc = nc
        self.ops = {e: [] for e in ENGS}
        self.n_dma_sems = n_dma_sems
        self.dma_rr = {e: 0 for e in ENGS}
        self.dma_cnt = {}

    def _dep(self, op, p, kind):
        if p is None or p is op:
            return
        if (not p.dma) and (not op.dma) and p.eng == op.eng:
            if kind != "raw" or op.eng == "pe":
                return
        op.deps.append(p)
        if not p.dma:
            p.needs_inc = True

    def op(self, eng, fn, reads=(), writes=(), dma=False):
        o = Op(eng, fn, dma)
        writes = list(writes) + [b for b in reads if b.excl and b not in writes]
        reads = [b for b in reads if not b.excl]
        for b in reads:
            self._dep(o, b.w, "raw")
        for b in writes:
            self._dep(o, b.w, "waw")
            for p in b.r.values():
                self._dep(o, p, "war")
            for p in b.rd:
                self._dep(o, p, "war")
        for b in reads:
            if dma:
                b.rd.append(o)
            else:
                b.r[eng] = o
        for b in writes:
            b.w = o
            b.r = {}
            b.rd = []
        if dma:
            k = (eng, self.dma_rr[eng] % self.n_dma_sems)
            self.dma_rr[eng] += 1
            o.sem = k
            o.prev_val = self.dma_cnt.get(k, 0)
            o.val = o.prev_val + 16
            self.dma_cnt[k] = o.val
        self.ops[eng].append(o)
        return o

    def barrier(self):
        lasts = []
        for e in ENGS:
            for o in reversed(self.ops[e]):
                if not o.dma and o.fn is not None:
                    lasts.append(o)
                    break
        snap = dict(self.dma_cnt)
        for e in ENGS:
            o = Op(e, None, False)
            for p in lasts:
                if p.eng != e:
                    o.deps.append(p)
                    p.needs_inc = True
            o.xw = snap
            self.ops[e].append(o)

    def dma(self, eng, out, in_, **kw):
        o_, i_ = _ap(out), _ap(in_)
        return self.op(eng, lambda e: e.dma_start(out=o_, in_=i_, **kw), _bufs(in_), _bufs(out), dma=True)

    def mm(self, out, lhsT, rhs, start=True, stop=True):
        o_, l_, r_ = _ap(out), _ap(lhsT), _ap(rhs)
        return self.op("pe", lambda e: e.matmul(o_, lhsT=l_, rhs=r_, start=start, stop=stop),
                       _bufs(lhsT, rhs), _bufs(out))

    def tr(self, out, in_, ident):
        o_, i_, d_ = _ap(out), _ap(in_), _ap(ident)
        return self.op("pe", lambda e: e.transpose(out=o_, in_=i_, identity=d_), _bufs(in_, ident), _bufs(out))

    def act(self, out, in_, func, bias=None, scale=1.0, accum=None):
        o_, i_ = _ap(out), _ap(in_)
        kw = {"scale": _ap(scale)}
        if bias is not None:
            kw["bias"] = _ap(bias)
        if accum is not None:
            kw["accum_out"] = _ap(accum)
        return self.op("act", lambda e: e.activation(out=o_, in_=i_, func=func, **kw),
                       _bufs(in_, bias, scale), _bufs(out, accum))

    def tt(self, eng, out, in0, in1, op):
        o_, a_, b_ = _ap(out), _ap(in0), _ap(in1)
        return self.op(eng, lambda e: e.tensor_tensor(out=o_, in0=a_, in1=b_, op=op), _bufs(in0, in1), _bufs(out))

    def ts(self, eng, out, in0, s1, s2, op0, op1=None):
        o_, a_, s1_, s2_ = _ap(out), _ap(in0), _ap(s1), _ap(s2)
        if op1 is None:
            fn = lambda e: e.tensor_scalar(out=o_, in0=a_, scalar1=s1_, scalar2=None, op0=op0)
        else:
            fn = lambda e: e.tensor_scalar(out=o_, in0=a_, scalar1=s1_, scalar2=s2_, op0=op0, op1=op1)
        return self.op(eng, fn, _bufs(in0, s1, s2), _bufs(out))

    def stt(self, out, in0, scalar, in1, op0, op1):
        o_, a_, s_, b_ = _ap(out), _ap(in0), _ap(scalar), _ap(in1)
        return self.op("dve", lambda e: e.scalar_tensor_tensor(out=o_, in0=a_, scalar=s_, in1=b_, op0=op0, op1=op1),
                       _bufs(in0, scalar, in1), _bufs(out))

    def copy(self, eng, out, in_):
        o_, i_ = _ap(out), _ap(in_)
        if eng == "act":
            fn = lambda e: e.copy(out=o_, in_=i_)
        else:
            fn = lambda e: e.tensor_copy(out=o_, in_=i_)
        return self.op(eng, fn, _bufs(in_), _bufs(out))

    def recip(self, out, in_):
        o_, i_ = _ap(out), _ap(in_)
        return self.op("dve", lambda e: e.reciprocal(out=o_, in_=i_), _bufs(in_), _bufs(out))

    def scan(self, out, d0, d1, init, op0, op1):
        o_, a_, b_, i_ = _ap(out), _ap(d0), _ap(d1), _ap(init)
        return self.op("dve", lambda e: e.tensor_tensor_scan(out=o_, data0=a_, data1=b_, initial=i_, op0=op0, op1=op1),
                       _bufs(d0, d1, init), _bufs(out))

    def memset(self, eng, out, val):
        o_ = _ap(out)
        return self.op(eng, lambda e: e.memset(o_, val), (), _bufs(out))

    def asel(self, out, in_, pattern, cmp, fill, base, cm):
        o_, i_ = _ap(out), _ap(in_)
        return self.op("pool", lambda e: e.affine_select(out=o_, in_=i_, pattern=pattern, compare_op=cmp, fill=fill,
                                                          base=base, channel_multiplier=cm), _bufs(in_), _bufs(out))

    def emit(self):
        nc = self.nc
        with contextlib.ExitStack() as st:
            esem = {e: st.enter_context(nc.semaphore("s_" + e)) for e in ENGS}
            dsem = {}
            for k in self.dma_cnt:
                dsem[k] = st.enter_context(nc.semaphore("d_%s_%d" % k))
            for e in ENGS:
                n = 0
                for o in self.ops[e]:
                    if o.needs_inc:
                        n += 1
                        o.seq = n
            final_waits = dict(self.dma_cnt)
            block = st.enter_context(nc.Block())

            def run(e, eng):
                waited = {}

                def wait(sem, key, val):
                    if waited.get(key, 0) >= val:
                        return
                    eng.wait_ge(sem, val)
                    waited[key] = val

                for o in self.ops[e]:
                    for p in o.deps:
                        if p.dma:
                            wait(dsem[p.sem], p.sem, p.val)
                        else:
                            wait(esem[p.eng], p.eng, p.seq)
                    if o.xw:
                        for k, v in o.xw.items():
                            wait(dsem[k], k, v)
                    if o.fn is None:
                        continue
                    if o.dma:
                        if o.prev_val:
                            wait(dsem[o.sem], o.sem, o.prev_val)
                        o.fn(eng).then_inc(dsem[o.sem], 16)
                    else:
                        ins = o.fn(eng)
                        if o.needs_inc:
                            ins.then_inc(esem[e], 1)
                if e == "sp":
                    for k, v in final_waits.items():
                        wait(dsem[k], k, v)

            @block.tensor
            def _(eng):
                run("pe", eng)

            @block.scalar
            def _(eng):
                run("act", eng)

            @block.vector
            def _(eng):
                run("dve", eng)

            @block.gpsimd
            def _(eng):
                run("pool", eng)

            @block.sync
            def _(eng):
                run("sp", eng)


class Arena:
    def __init__(self, t, nbytes):
        self.t = t
        self.nbytes = nbytes
        self.off = 0

    def alloc(self, cols, dtype=F32, name="", align=4):
        sz = cols * (2 if dtype == BF16 else 4)
        sz = (sz + 3) // 4 * 4
        self.off = (self.off + align - 1) // align * align
        a = self.off
        self.off += sz
        assert self.off <= self.nbytes, ("arena overflow", name, self.off, self.nbytes)
        ap = self.t[:, a // 4:(a + sz) // 4]
        if dtype == BF16:
            ap = ap.bitcast(BF16)[:, 0:cols]
        return Tl(ap, Buf(name))

    def bank(self, name=""):
        t = self.alloc(512, F32, name, align=2048)
        t.b.excl = True
        return t


class Rot:
    def __init__(self, items):
        self.items = items
        self.i = 0

    def next(self):
        x = self.items[self.i % len(self.items)]
        self.i += 1
        return x


def build(S, L, dbg=False):
    nc = bass.Bass("TRN2", target_bir_lowering=False)
    P = Prog(nc)
    NB = S // 128

    def din(name, shape):
        return nc.dram_tensor(name, shape, F32, kind="ExternalInput").ap()

    x_in = din("x", [S, D])
    norm_g = din("norm_g", [L, D])
    w_in = din("w_in", [L, D, IN_COLS])
    mu_a = din("mu_a", [L, A_COLS])
    w_up = din("w_up", [L, 64, RW])
    w0 = din("w0", [L, RW])
    a_up = din("a_up", [L, 64, RW])
    a0 = din("a0", [L, RW])
    k_k = din("k_k", [L, RW])
    k_a = din("k_a", [L, RW])
    r_k = din("r_k", [L, RW])
    gn_g = din("gn_g", [L, RW])
    gn_b = din("gn_b", [L, RW])
    pool_w = din("pool_w", [L, 4, 128, 128])
    pool_scale = din("pool_scale", [L, PW])
    qn_g = din("qn_g", [L, 64])
    kn_g = din("kn_g", [L, 64])
    w_out = din("w_out", [L, D, D])
    out = nc.dram_tensor("out", [S, D], F32, kind="ExternalOutput").ap()
    skind = "ExternalOutput" if dbg else "Internal"
    dbg_outs = {}

    def dump(name, tile, cols, dt=F32):
        if not dbg or name in dbg_outs:
            return
        dbg_outs[name] = nc.dram_tensor("dbg_" + name, [128, cols], dt, kind="ExternalOutput").ap()
        P.dma("sp", dbg_outs[name], tile)
    pj = nc.dram_tensor("pj", [IN_COLS, S], F32, kind=skind).ap()
    vtok = nc.dram_tensor("vtok", [S, SW], BF16, kind=skind).ap()
    ymix = nc.dram_tensor("ymix", [D, S], BF16, kind=skind).ap()

    with contextlib.ExitStack() as st:
        SB_BYTES = 206 * 1024
        sb_t = st.enter_context(nc.sbuf_tensor("arena", [128, SB_BYTES // 4], F32))
        ps_t = st.enter_context(nc.psum_tensor("psarena", [128, 4096], F32))
        sb = Arena(sb_t, SB_BYTES)
        ps = Arena(ps_t, 16384)

        identf = sb.alloc(128, F32, "identf")
        ident = sb.alloc(128, BF16, "ident")
        M2 = sb.alloc(256, F32, "M2")
        MLs = sb.alloc(128, F32, "MLs")
        MGE = sb.alloc(128, F32, "MGE")
        bones = sb.alloc(128, F32, "bones")
        zero1 = sb.alloc(1, F32, "zero1")
        epsr = sb.alloc(1, F32, "epsr")
        epsg = sb.alloc(1, F32, "epsg")
        TPB = min(512, S)
        reset = sb.alloc(TPB, F32, "reset")
        prm = sb.alloc(80, F32, "prm")
        qgs = sb.alloc(1, F32, "qgs")

        P.memset("pool", identf, 1.0)
        P.asel(identf, identf, [[-1, 128]], ALU.is_equal, 0.0, 0, 1)
        P.copy("dve", ident, identf)
        P.memset("pool", M2, 1.0)
        P.asel(M2[:, 0:128], M2[:, 0:128], [[1, 128]], ALU.is_gt, 0.0, 0, -1)
        P.asel(M2[:, 128:256], M2[:, 128:256], [[1, 128]], ALU.is_ge, 0.0, 0, -1)
        P.memset("pool", MLs, 1.0)
        P.asel(MLs, MLs, [[-1, 128]], ALU.is_gt, 0.0, 0, 1)
        P.ts("dve", MGE, MLs, -1.0, 1.0, ALU.mult, ALU.add)
        P.memset("dve", bones, 0.0)
        P.memset("dve", bones[0:64, 0:64], 1.0)
        P.memset("dve", bones[64:128, 64:128], 1.0)
        P.memset("dve", zero1, 0.0)
        P.memset("dve", epsr, RMS_EPS)
        P.memset("dve", epsg, GN_EPS)
        P.memset("dve", reset, 1.0)
        P.memset("dve", reset.re("p (c n) -> p c n", n=128)[:, :, 0:1], 0.0)
        const_off = sb.off

        PM_MU, PM_W0, PM_A0, PM_KK, PM_KA, PM_RK, PM_GG, PM_GB, PM_PS, PM_QN, PM_KN = 0, 25, 31, 37, 43, 49, 55, 61, 67, 71, 72

        def load_params(l):
            sb.off = const_off
            ps.off = 0
            stg = sb.alloc(128, F32, "prm_stage")
            P.memset("dve", stg, 0.0)
            stg_w = stg

            def ld(row0, src, n):
                P.dma("sp", stg_w[row0:row0 + n, :], src)
            ld(PM_MU, mu_a[l].rearrange("(t p) -> t p", p=128), 25)
            for r0, src in ((PM_W0, w0), (PM_A0, a0), (PM_KK, k_k), (PM_KA, k_a), (PM_RK, r_k), (PM_GG, gn_g),
                            (PM_GB, gn_b)):
                ld(r0, src[l].rearrange("(t p) -> t p", p=128), 6)
            ld(PM_PS, pool_scale[l].rearrange("(t p) -> t p", p=128), 4)
            P.dma("sp", stg_w[PM_QN:PM_QN + 1, 0:64], qn_g[l:l + 1, :])
            P.dma("sp", stg_w[PM_QN:PM_QN + 1, 64:128], qn_g[l:l + 1, :])
            P.dma("sp", stg_w[PM_KN:PM_KN + 1, 0:64], kn_g[l:l + 1, :])
            P.dma("sp", stg_w[PM_KN:PM_KN + 1, 64:128], kn_g[l:l + 1, :])
            pt = ps.bank("prm_ps")
            P.mm(pt[:, 0:73], lhsT=stg[0:73, :], rhs=identf[0:73, 0:73])
            P.copy("dve", prm[:, 0:73], pt[:, 0:73])
            P.ts("dve", qgs, prm[:, PM_QN:PM_QN + 1], 0.125, None, ALU.mult)

        def phase_A(l, xsrc):
            sb.off = const_off
            ps.off = 0
            TOKP = min(2048, S)
            npass = S // TOKP
            nb = TOKP // 128
            gbc = sb.alloc(D, F32, "gbc")
            P.dma("sp", gbc, norm_g[l].partition_broadcast(128))
            hT = sb.alloc(16 * TOKP, BF16, "hT")
            hT3 = hT.re("p (k t) -> p k t", t=TOKP)
            hbufs = [Buf("hT%d" % i) for i in range(TOKP // 512)]
            xrot = Rot([sb.alloc(D, F32, "x%d" % i) for i in range(2)])
            hrot = Rot([sb.alloc(D, BF16, "h%d" % i) for i in range(2)])
            junk = sb.alloc(D, BF16, "junk")
            ssr = Rot([sb.alloc(1, F32, "ss%d" % i) for i in range(4)])
            rsr = Rot([sb.alloc(1, F32, "rs%d" % i) for i in range(4)])
            CG = 384
            wfrot = Rot([sb.alloc(16 * CG, F32, "wf%d" % i) for i in range(2)])
            wbrot = Rot([sb.alloc(16 * CG, BF16, "wb%d" % i) for i in range(2)])
            ostrot = Rot([sb.alloc(512, F32, "ost%d" % i) for i in range(3)])
            vstrot = Rot([sb.alloc(CG, BF16, "vst%d" % i) for i in range(3)])
            trps = Rot([ps.bank("trp%d" % i) for i in range(2)])
            mops = Rot([ps.bank("mo%d" % i) for i in range(4)])
            for pa in range(npass):
                t0 = pa * TOKP
                for tb in range(nb):
                    xt = xrot.next()
                    P.dma("sp", xt, xsrc[t0 + tb * 128:t0 + (tb + 1) * 128, :])
                    ss = ssr.next()
                    rs = rsr.next()
                    P.memset("pool", ss, 0.0)
                    P.act(junk, xt, AF.Square, accum=ss)
                    P.act(rs, ss, AF.Sqrt, bias=epsr, scale=1.0 / D)
                    P.recip(rs, rs)
                    hb = hrot.next()
                    P.stt(hb, xt, rs, gbc, ALU.mult, ALU.mult)
                    hbuf = hbufs[tb // 4]
                    for q in range(4):
                        tp = trps.next()
                        tpb = Tl(tp.ap.bitcast(BF16)[:, 0:512], tp.b)
                        for j in range(4):
                            kc = 4 * q + j
                            P.tr(tpb[:, j * 128:(j + 1) * 128], hb[:, kc * 128:(kc + 1) * 128], ident)
                        dst = Tl(hT3.ap[:, 4 * q:4 * q + 4, tb * 128:(tb + 1) * 128], hbuf)
                        P.copy("act" if q % 2 == 0 else "dve", dst, tpb.re("p (j t) -> p j t", t=128))
                for cg in range(IN_COLS // CG):
                    wf = wfrot.next()
                    wsrc = w_in[l, :, cg * CG:(cg + 1) * CG].rearrange("(k p) c -> p k c", p=128)
                    wf3 = wf.re("p (k c) -> p k c", c=CG)
                    P.dma("sp", wf3[:, 0:8, :], wsrc[:, 0:8, :])
                    P.dma("pool", wf3[:, 8:16, :], wsrc[:, 8:16, :])
                    wb = wbrot.next()
                    wb3 = wb.re("p (k c) -> p k c", c=CG)
                    P.copy("pool", wb3[:, 0:6, :], wf3[:, 0:6, :])
                    P.copy("dve", wb3[:, 6:16, :], wf3[:, 6:16, :])
                    if cg not in (15, 16):
                        for ci in range(3):
                            ct = cg * 3 + ci
                            for tc in range(TOKP // 512):
                                po = mops.next()
                                for k in range(16):
                                    rhs = Tl(hT3.ap[:, k, tc * 512:(tc + 1) * 512], hbufs[tc])
                                    P.mm(po, lhsT=wb3[:, k, ci * 128:(ci + 1) * 128], rhs=rhs, start=(k == 0),
                                         stop=(k == 15))
                                ost = ostrot.next()
                                P.copy("act", ost, po)
                                P.dma("act", pj[ct * 128:(ct + 1) * 128, t0 + tc * 512:t0 + (tc + 1) * 512], ost)
                    else:
                        for tb in range(nb):
                            po = mops.next()
                            for k in range(16):
                                lhsT = Tl(hT3.ap[:, k, tb * 128:(tb + 1) * 128], hbufs[tb // 4])
                                P.mm(po[:, 0:CG], lhsT=lhsT, rhs=wb3[:, k, :], start=(k == 0), stop=(k == 15))
                            vst = vstrot.next()
                            P.copy("act", vst, po[:, 0:CG])
                            P.dma("act", vtok[t0 + tb * 128:t0 + (tb + 1) * 128, (cg - 15) * CG:(cg - 14) * CG], vst)

        def phase_B(l):
            sb.off = const_off
            ps.off = 0
            TP = TPB
            NCH = TP // 128
            npiece = S // TP
            wup = sb.alloc(RW, F32, "wup")
            P.dma("sp", wup[0:64, :], w_up[l])
            P.dma("sp", wup[64:128, :], a_up[l])
            Sf = [[sb.alloc(64, F32, "Sf%d_%d" % (hp, i)) for i in range(2)] for hp in range(6)]
            Sb = [[sb.alloc(64, BF16, "Sb%d_%d" % (hp, i)) for i in range(2)] for hp in range(6)]
            cur = [0] * 6
            for hp in range(6):
                P.memset("dve", Sf[hp][0], 0.0)
                P.memset("dve", Sb[hp][0], 0.0)

            def f32t(name, n=TP):
                return sb.alloc(n, F32, name)

            def bft(name, n=TP):
                return sb.alloc(n, BF16, name)
            lda, tmpA, wa, wa2 = f32t("lda", TP + 1), f32t("tmpA"), f32t("wa"), f32t("wa2")
            ldr = Rot([[f32t("ld%s%d" % (nm, i), TP + 1) for nm in "rkvg"] for i in range(2)])
            tmpd = f32t("tmpd")
            rs_, ks_, vs_, gs_ = f32t("rs"), f32t("ks"), f32t("vs"), f32t("gs")
            sig, a_, cs, W, Winv, csm, Wm, e2, E2 = (f32t(n) for n in
                                                     ("sig", "a", "cs", "W", "Winv", "csm", "Wm", "e2", "E2"))
            kk, kk2, nrm, kkn, tmpk, kp, b_ = (f32t(n) for n in ("kk", "kk2", "nrm", "kkn", "tmpk", "kp", "b"))
            rkr, bon, sg = f32t("rkr"), f32t("bon"), f32t("sg")
            ysb, yc, sq, rstd, yn = f32t("ysb"), f32t("yc"), f32t("sq"), f32t("rstd"), f32t("yn")
            AR = bft("AR", 2 * TP)
            AR4 = AR.re("p (c w n) -> p c w n", w=2, n=128)
            KH, BH, KB, BB, VB = bft("KH"), bft("BH"), bft("KB"), bft("BB"), bft("VB")
            yarot = Rot([bft("ya%d" % i) for i in range(2)])
            TOKr = Rot([bft("TOK%d" % i, 384) for i in range(2)])
            NIT = 2 * NCH
            SA1 = [bft("SA1_%d" % i, 256) for i in range(NIT)]
            SA2 = [bft("SA2_%d" % i, 256) for i in range(NIT)]
            Xr = [Rot([bft("X%d_%d" % (i, j), 128) for j in range(2)]) for i in range(NIT)]
            XTr = [Rot([bft("XT%d_%d" % (i, j), 128) for j in range(2)]) for i in range(NIT)]
            Pr = [Rot([bft("P%d_%d" % (i, j), 128) for j in range(2)]) for i in range(NIT)]
            Zbr = Rot([bft("Zb%d" % i, 64) for i in range(2)])
            Ubr = Rot([bft("Ub%d" % i, 64) for i in range(2)])
            pgen = Rot([ps.bank("pgen%d" % i) for i in range(1)])
            pxb = [ps.bank("pxb%d" % i) for i in range(4)]
            ppb = [ps.bank("ppb%d" % i) for i in range(2)]
            pmisc = ps.bank("pmisc")
            ptr = Tl(pmisc.ap.bitcast(BF16)[:, 0:384], pmisc.b)
            pzr = Rot([Tl(pmisc.ap[:, 192:256], pmisc.b), Tl(pmisc.ap[:, 256:320], pmisc.b)])
            pur = Rot([Tl(pmisc.ap[:, 320:384], pmisc.b), Tl(pmisc.ap[:, 384:448], pmisc.b)])
            psn = Tl(pmisc.ap[:, 448:512], pmisc.b)
            for pc in range(npiece):
                t0 = pc * TP

                def load_shift(dst, row0, eng="sp"):
                    if t0 > 0:
                        P.dma(eng, dst, pj[row0:row0 + 128, t0 - 1:t0 + TP])
                    else:
                        P.memset("pool", dst[:, 0:1], 0.0)
                        P.dma(eng, dst[:, 1:TP + 1], pj[row0:row0 + 128, 0:TP])

                def lerp(dst, ld, mucol):
                    P.tt("pool", tmpd, ld[:, 0:TP], ld[:, 1:TP + 1], ALU.subtract)
                    P.stt(dst, tmpd, prm[:, mucol:mucol + 1], ld[:, 1:TP + 1], ALU.mult, ALU.add)
                load_shift(lda, 3072)
                lerp(wa, lda, PM_MU + 24)
                P.act(wa2[0:64, :], wa[0:64, :], AF.Tanh)
                P.copy("pool", wa2[64:128, :], wa[64:128, :])
                for hp in range(6):
                    lds = ldr.next()
                    for i, (ld, r0) in enumerate(zip(lds, (0, 768, 1536, 2304))):
                        load_shift(ld, r0 + hp * 128, "sp" if i % 2 == 0 else "act")
                    lerp(rs_, lds[0], PM_MU + hp)
                    lerp(ks_, lds[1], PM_MU + 6 + hp)
                    lerp(vs_, lds[2], PM_MU + 12 + hp)
                    lerp(gs_, lds[3], PM_MU + 18 + hp)
                    pd = pgen.next()
                    P.mm(pd[:, 0:TP], lhsT=wup[0:64, hp * 128:(hp + 1) * 128], rhs=wa2[0:64, :])
                    P.act(sig, pd[:, 0:TP], AF.Sigmoid, bias=prm[:, PM_W0 + hp:PM_W0 + hp + 1])
                    pa = pgen.next()
                    P.mm(pa[:, 0:TP], lhsT=wup[64:128, hp * 128:(hp + 1) * 128], rhs=wa2[64:128, :])
                    P.act(a_, pa[:, 0:TP], AF.Sigmoid, bias=prm[:, PM_A0 + hp:PM_A0 + hp + 1])
                    P.act(sq, gs_, AF.Sigmoid)
                    P.tt("pool", sg, sq, gs_, ALU.mult)
                    P.scan(cs, reset, sig, 0.0, ALU.mult, ALU.add)
                    P.act(W, cs, AF.Exp, scale=-CDEC)
                    P.act(Winv, cs, AF.Exp, scale=CDEC)
                    P.tt("pool", csm, cs, sig, ALU.subtract)
                    P.act(Wm, csm, AF.Exp, scale=-CDEC)
                    cs3 = cs.re("p (c n) -> p c n", n=128)
                    P.tt("dve", e2.re("p (c n) -> p c n", n=128), cs3[:, :, 127:128].bc([128, NCH, 128]), cs3,
                         ALU.subtract)
                    P.act(E2, e2, AF.Exp, scale=-CDEC)
                    P.ts("dve", kk, ks_, prm[:, PM_KK + hp:PM_KK + hp + 1], None, ALU.mult)
                    P.act(kk2, kk, AF.Square)
                    pn = pgen.next()
                    P.mm(pn[:, 0:TP], lhsT=bones, rhs=kk2)
                    P.act(nrm, pn[:, 0:TP], AF.Sqrt)
                    P.ts("dve", nrm, nrm, 1e-12, None, ALU.max)
                    P.recip(nrm, nrm)
                    P.tt("dve", kkn, kk, nrm, ALU.mult)
                    P.ts("dve", tmpk, a_, -1.0, prm[:, PM_KA + hp:PM_KA + hp + 1], ALU.add, ALU.mult)
                    P.stt(kp, tmpk, 1.0, ks_, ALU.add, ALU.mult)
                    P.tt("dve", b_, kkn, a_, ALU.mult)
                    TPv = "p (c n) -> p c n"
                    P.stt(AR4[:, :, 0, :], kkn.re(TPv, n=128), -1.0, Wm.re(TPv, n=128), ALU.mult, ALU.mult)
                    P.tt("pool", AR4[:, :, 1, :], rs_.re(TPv, n=128), W.re(TPv, n=128), ALU.mult)
                    P.tt("dve", KH, kp, Winv, ALU.mult)
                    P.tt("dve", BH, b_, Winv, ALU.mult)
                    P.tt("dve", KB, kp, E2, ALU.mult)
                    P.tt("pool", BB, b_, E2, ALU.mult)
                    P.copy("act", VB, vs_)
                    P.stt(rkr, rs_, prm[:, PM_RK + hp:PM_RK + hp + 1], kp, ALU.mult, ALU.mult)
                    pb = pgen.next()
                    P.mm(pb[:, 0:TP], lhsT=bones, rhs=rkr)
                    P.tt("dve", bon, pb[:, 0:TP], vs_, ALU.mult)
                    if hp == 0 and pc == 0:
                        for nm, t_ in (("rs", rs_), ("ks", ks_), ("vs", vs_), ("sig", sig), ("a", a_), ("kkn", kkn),
                                       ("kp", kp), ("b", b_), ("bon", bon), ("sg", sg), ("W", W), ("Wm", Wm),
                                       ("Winv", Winv), ("E2", E2)):
                            dump(nm, t_, TP)
                        dump("AR", AR, 2 * TP, BF16)
                        dump("KH", KH, TP, BF16)
                        dump("BH", BH, TP, BF16)
                    items = [(c, h) for c in range(NCH) for h in range(2)]
                    Xs, XTs, Ps = {}, {}, {}
                    for it, (c, h) in enumerate(items):
                        sl = slice(64 * h, 64 * h + 64)
                        csl = slice(c * 128, (c + 1) * 128)
                        pscb = pxb[it % 4]
                        ps1 = pscb[:, 0:256]
                        ps2 = pscb[:, 256:512]
                        arc = Tl(AR4.ap[sl, c].rearrange("p w n -> p (w n)"), AR.b)
                        P.mm(ps1, lhsT=BH[sl, csl], rhs=arc)
                        P.mm(ps2, lhsT=KH[sl, csl], rhs=arc)
                        ps3 = ppb[it // 4][:, (it % 4) * 128:(it % 4 + 1) * 128]
                        P.mm(ps3, lhsT=AR4[sl, c, 0, :], rhs=BH[sl, csl])
                        P.tt("dve", SA1[it], ps1, M2, ALU.mult)
                        P.tt("dve", SA2[it], ps2, M2, ALU.mult)
                        xt0 = XTr[it].next()
                        P.tt("dve", xt0, ps3, MLs, ALU.mult)
                        p0 = Pr[it].next()
                        P.tt("pool", p0, SA1[it][:, 0:128], ident, ALU.add)
                        Xs[it], XTs[it], Ps[it] = SA1[it][:, 0:128], xt0, p0
                    for k in range(7):
                        pend = []
                        for it in range(len(items)):
                            pp = ppb[it // 4][:, (it % 4) * 128:(it % 4 + 1) * 128]
                            pxbk = pxb[it // 2]
                            if k >= 1:
                                P.mm(pp, lhsT=XTs[it], rhs=Ps[it])
                            if k < 6:
                                px = pxbk[:, (it % 2) * 256:(it % 2) * 256 + 128]
                                P.mm(px, lhsT=XTs[it], rhs=Xs[it])
                                pxt = pxbk[:, (it % 2) * 256 + 128:(it % 2) * 256 + 256]
                                P.mm(pxt, lhsT=Xs[it], rhs=XTs[it])
                            else:
                                px = pxt = None
                            pend.append((pp, px, pxt))
                        for it in range(len(items)):
                            pp, px, pxt = pend[it]
                            if k >= 1:
                                pn_ = Pr[it].next()
                                P.tt("dve", pn_, pp, Ps[it], ALU.add)
                                Ps[it] = pn_
                            if k < 6:
                                xn = Xr[it].next()
                                P.copy("act", xn, px)
                                xtn = XTr[it].next()
                                P.copy("act", xtn, pxt)
                                Xs[it], XTs[it] = xn, xtn
                    for c in range(NCH):
                        csl = slice(c * 128, (c + 1) * 128)
                        TOK = TOKr.next()
                        P.tr(ptr[:, 0:128], VB[:, csl], ident)
                        P.tr(ptr[:, 128:256], KB[:, csl], ident)
                        P.tr(ptr[:, 256:384], BB[:, csl], ident)
                        P.copy("act", TOK, ptr)
                        pyb = pgen.next()
                        sfo, sbo = Sf[hp][cur[hp]], Sb[hp][cur[hp]]
                        sfn, sbn = Sf[hp][1 - cur[hp]], Sb[hp][1 - cur[hp]]
                        hv = []
                        for h in range(2):
                            it = items.index((c, h))
                            sl = slice(64 * h, 64 * h + 64)
                            hv.append((it, sl, TOK[:, 64 * h:64 * h + 64], TOK[:, 128 + 64 * h:128 + 64 * h + 64],
                                       TOK[:, 256 + 64 * h:256 + 64 * h + 64]))
                        pzs, zbs, pus, ubs = [], [], [], []
                        for (it, sl, vt, kbt, bbt) in hv:
                            pz = pzr.next()
                            P.mm(pz, lhsT=AR4[sl, c, 0, :], rhs=sbo[sl, :], start=True, stop=False)
                            P.mm(pz, lhsT=SA2[it][:, 0:128], rhs=vt, start=False, stop=True)
                            pzs.append(pz)
                        for pz in pzs:
                            zb = Zbr.next()
                            P.copy("act", zb, pz)
                            zbs.append(zb)
                        for (it, sl, vt, kbt, bbt), zb in zip(hv, zbs):
                            pu = pur.next()
                            P.mm(pu, lhsT=Ps[it], rhs=zb)
                            pus.append(pu)
                        for pu in pus:
                            ub = Ubr.next()
                            P.copy("act", ub, pu)
                            ubs.append(ub)
                        for (it, sl, vt, kbt, bbt), ub in zip(hv, ubs):
                            P.mm(pyb[sl, 0:128], lhsT=sbo[sl, :], rhs=AR4[sl, c, 1, :], start=True, stop=False)
                            P.mm(pyb[sl, 0:128], lhsT=ub, rhs=SA1[it][:, 128:256], start=False, stop=False)
                            P.mm(pyb[sl, 0:128], lhsT=vt, rhs=SA2[it][:, 128:256], start=False, stop=True)
                        for (it, sl, vt, kbt, bbt), ub in zip(hv, ubs):
                            P.mm(psn[sl, :], lhsT=bbt, rhs=ub, start=True, stop=False)
                            P.mm(psn[sl, :], lhsT=kbt, rhs=vt, start=False, stop=True)
                        P.copy("act", ysb[:, csl], pyb[:, 0:128])
                        wc = W[:, c * 128 + 127:c * 128 + 128]
                        P.stt(sfn, sfo, wc, psn, ALU.mult, ALU.add)
                        P.stt(sbn, sfo, wc, psn, ALU.mult, ALU.add)
                        cur[hp] = 1 - cur[hp]
                    pm = pgen.next()
                    P.mm(pm[:, 0:TP], lhsT=bones, rhs=ysb)
                    P.stt(yc, pm[:, 0:TP], -1.0 / 64, ysb, ALU.mult, ALU.add)
                    P.act(sq, yc, AF.Square)
                    pv = pgen.next()
                    P.mm(pv[:, 0:TP], lhsT=bones, rhs=sq)
                    P.act(rstd, pv[:, 0:TP], AF.Sqrt, bias=epsg, scale=1.0 / 64)
                    P.recip(rstd, rstd)
                    P.tt("pool", yn, yc, rstd, ALU.mult)
                    P.ts("dve", yn, yn, prm[:, PM_GG + hp:PM_GG + hp + 1], prm[:, PM_GB + hp:PM_GB + hp + 1], ALU.mult,
                         ALU.add)
                    P.tt("pool", yn, yn, bon, ALU.add)
                    if hp == 0 and pc == 0:
                        dump("ysb", ysb, TP)
                        dump("yn", yn, TP)
                    ya = yarot.next()
                    P.tt("dve", ya, yn, sg, ALU.mult)
                    P.dma("sp", ymix[hp * 128:(hp + 1) * 128, t0:t0 + TP], ya)

        def phase_C(l):
            sb.off = const_off
            ps.off = 0
            TP = min(1024, S)
            npiece = S // TP
            H = 16
            pwf = sb.alloc(512, F32, "pwf")
            pwb = sb.alloc(512, BF16, "pwb")
            P.dma("sp", pwf.re("p (g d) -> p g d", d=128), pool_w[l].rearrange("g c d -> c g d"))
            P.copy("dve", pwb, pwf)
            corr = sb.alloc(4 * 16, F32, "corr")
            for g, win in enumerate((2, 4, 8, 16)):
                for t in range(16):
                    P.memset("pool", corr[:, g * 16 + t:g * 16 + t + 1], float(win) / min(t + 1, win))
            urot = Rot([sb.alloc(TP + H, F32, "pu%d" % i) for i in range(2)])
            grot = Rot([sb.alloc(TP, F32, "pg%d" % i) for i in range(2)])
            sA = sb.alloc(TP + H, F32, "sA")
            sB = sb.alloc(TP + H, F32, "sB")
            dbf = sb.alloc(TP, BF16, "dbf")
            sgp = sb.alloc(TP, F32, "sgp")
            ybr = Rot([sb.alloc(TP, BF16, "yb%d" % i) for i in range(2)])
            pyr = Rot([ps.bank("pc%d" % i) for i in range(2)])
            for pc in range(npiece):
                t0 = pc * TP
                for g, win in enumerate((2, 4, 8, 16)):
                    u = urot.next()
                    row = A_COLS + g * 128
                    if t0 > 0:
                        P.dma("sp", u, pj[row:row + 128, t0 - H:t0 + TP])
                    else:
                        P.memset("pool", u[:, 0:H], 0.0)
                        P.dma("sp", u[:, H:H + TP], pj[row:row + 128, 0:TP])
                    pg_ = grot.next()
                    P.dma("act", pg_, pj[A_COLS + PW + g * 128:A_COLS + PW + (g + 1) * 128, t0:t0 + TP])
                    src = u
                    sh = 1
                    dsts = [sA, sB]
                    lo = 0
                    for lev in range(g + 1):
                        dst = dsts[lev % 2]
                        lo += sh
                        P.tt("pool" if lev % 2 else "dve", dst[:, lo:TP + H], src[:, lo:TP + H], src[:, lo - sh:TP + H - sh],
                             ALU.add)
                        src = dst
                        sh *= 2
                    ssum = src
                    if t0 == 0:
                        P.tt("dve", ssum[:, H:H + 16], ssum[:, H:H + 16], corr[:, g * 16:(g + 1) * 16], ALU.mult)
                    P.stt(dbf, ssum[:, H:H + TP], 1.0 / win, u[:, H:H + TP], ALU.mult, ALU.subtract)
                    P.act(sgp, pg_, AF.Silu)
                    yb = ybr.next()
                    for hf in range(TP // 512):
                        py = pyr.next()
                        P.mm(py, lhsT=pwb[:, g * 128:(g + 1) * 128], rhs=dbf[:, hf * 512:(hf + 1) * 512])
                        P.stt(yb[:, hf * 512:(hf + 1) * 512], py, prm[:, PM_PS + g:PM_PS + g + 1],
                              sgp[:, hf * 512:(hf + 1) * 512], ALU.mult, ALU.mult)
                    P.dma("sp", ymix[RW + g * 128:RW + (g + 1) * 128, t0:t0 + TP], yb)

        def phase_D(l):
            sb.off = const_off
            ps.off = 0
            TPq = min(1024, S)
            QH = sb.alloc(S, BF16, "QH")
            KHt = sb.alloc(S, BF16, "KHt")
            V = sb.alloc(NB * 128, BF16, "V")
            V3 = V.re("p (n c) -> p n c", c=128)
            sgc = sb.alloc(S, BF16, "sgc")
            ycst = sb.alloc(S, BF16, "ycst")
            SGr = Rot([sb.alloc(S, F32, "SG%d" % i) for i in range(2)])
            EZr = Rot([sb.alloc(S, BF16, "EZ%d" % i) for i in range(2)])
            Pn = sb.alloc(S, F32, "Pn")
            Wtr = Rot([sb.alloc(S, BF16, "Wt%d" % i) for i in range(2)])
            WTr = Rot([sb.alloc(S, BF16, "WT%d" % i) for i in range(2)])
            qf = Rot([sb.alloc(TPq, F32, "qf%d" % i) for i in range(2)])
            sqq = sb.alloc(TPq, F32, "sqq")
            rt = sb.alloc(TPq, F32, "rt")
            zr = Rot([ps.bank("z%d" % i) for i in range(3)])
            ptwr = Rot([ps.bank("ptw%d" % i) for i in range(2)])
            por = Rot([ps.bank("po%d" % i) for i in range(2)])
            pgen = ps.bank("pgenD")
            for hp in range(6):
                P.dma("sp", V3, vtok[:, hp * 128:(hp + 1) * 128].rearrange("(n p) c -> p n c", p=128))
                for (dst, row0, gcol) in ((QH, C0 + hp * 128, qgs), (KHt, C0 + SW + hp * 128, prm[:, PM_KN:PM_KN + 1])):
                    for pc in range(S // TPq):
                        q = qf.next()
                        P.dma("sp", q, pj[row0:row0 + 128, pc * TPq:(pc + 1) * TPq])
                        P.act(sqq, q, AF.Square)
                        for hf in range(TPq // 512):
                            hs = slice(hf * 512, (hf + 1) * 512)
                            P.mm(pgen, lhsT=bones, rhs=sqq[:, hs])
                            P.act(rt[:, hs], pgen, AF.Sqrt, bias=epsr, scale=1.0 / 64)
                        P.recip(rt, rt)
                        P.stt(dst[:, pc * TPq:(pc + 1) * TPq], q, gcol, rt, ALU.mult, ALU.mult)
                for pc in range(S // TPq):
                    q = qf.next()
                    r0 = C0 + 3 * SW + hp * 128
                    P.dma("act", q, pj[r0:r0 + 128, pc * TPq:(pc + 1) * TPq])
                    P.act(sgc[:, pc * TPq:(pc + 1) * TPq], q, AF.Silu)
                items = [(T, h) for T in range(NB) for h in range(2)]
                state = {}

                def stage1(T, h):
                    sl = slice(64 * h, 64 * h + 64)
                    kend = (T + 1) * 128
                    SG = SGr.next()
                    EZ = EZr.next()
                    for ck in range((kend + 511) // 512):
                        w_ = min(512, kend - ck * 512)
                        pz = zr.next()
                        P.mm(pz[:, 0:w_], lhsT=QH[sl, T * 128:(T + 1) * 128], rhs=KHt[sl, ck * 512:ck * 512 + w_])
                        P.act(SG[:, ck * 512:ck * 512 + w_], pz[:, 0:w_], AF.Sigmoid, scale=-1.0)
                        P.act(EZ[:, ck * 512:ck * 512 + w_], pz[:, 0:w_], AF.Sigmoid)
                    dsl = slice(T * 128, kend)
                    P.tt("pool", SG[:, dsl], SG[:, dsl], MLs, ALU.mult)
                    P.tt("pool", SG[:, dsl], SG[:, dsl], MGE, ALU.add)
                    P.tt("pool", EZ[:, dsl], EZ[:, dsl], MLs, ALU.mult)
                    P.scan(Pn[:, 0:kend][:, ::-1], SG[:, 0:kend][:, ::-1], zero1.bc([128, kend]), 1.0, ALU.mult, ALU.add)
                    Wt = Wtr.next()
                    npl = ((kend - 1) * 3 // 8) // 64 * 64
                    if npl > 0:
                        P.tt("pool", Wt[:, 0:npl], EZ[:, 0:npl], Pn[:, 1:npl + 1], ALU.mult)
                    P.tt("dve", Wt[:, npl:kend - 1], EZ[:, npl:kend - 1], Pn[:, npl + 1:kend], ALU.mult)
                    P.memset("pool", Wt[:, kend - 1:kend], 0.0)
                    state[(T, h)] = Wt

                def stage2(T, h):
                    sl = slice(64 * h, 64 * h + 64)
                    Wt = state.pop((T, h))
                    WT = WTr.next()
                    nsb = T + 1
                    for g0 in range(0, nsb, 4):
                        n = min(4, nsb - g0)
                        pt = ptwr.next()
                        ptb = Tl(pt.ap.bitcast(BF16)[:, 0:512], pt.b)
                        for j in range(n):
                            P.tr(ptb[:, j * 128:(j + 1) * 128], Wt[:, (g0 + j) * 128:(g0 + j + 1) * 128], ident)
                        P.copy("act", WT[:, g0 * 128:(g0 + n) * 128], ptb[:, 0:n * 128])
                    if h == 0:
                        state["po"] = por.next()
                    po = state["po"]
                    for sbk in range(nsb):
                        P.mm(po[sl, 0:128], lhsT=V3[:, sbk, 64 * h:64 * h + 64], rhs=WT[:, sbk * 128:(sbk + 1) * 128],
                             start=(sbk == 0), stop=(sbk == nsb - 1))
                    if h == 1:
                        P.tt("dve", ycst[:, T * 128:(T + 1) * 128], po[:, 0:128], sgc[:, T * 128:(T + 1) * 128], ALU.mult)

                for i in range(len(items) + 1):
                    if i < len(items):
                        stage1(*items[i])
                    if i >= 1:
                        stage2(*items[i - 1])
                P.dma("sp", ymix[RW + PW + hp * 128:RW + PW + (hp + 1) * 128, :], ycst)

        def phase_E(l, xsrc):
            sb.off = const_off
            ps.off = 0
            wo = sb.alloc(16 * D, BF16, "wo")
            wo3 = wo.re("p (k c) -> p k c", c=D)
            wst = Rot([sb.alloc(4 * D, F32, "wst%d" % i) for i in range(2)])
            for kq in range(4):
                w = wst.next()
                w3 = w.re("p (k c) -> p k c", c=D)
                P.dma("sp" if kq % 2 == 0 else "act", w3,
                      w_out[l, kq * 512:(kq + 1) * 512, :].rearrange("(k p) c -> p k c", p=128))
                P.copy("pool", wo3[:, kq * 4:kq * 4 + 2, :], w3[:, 0:2, :])
                P.copy("dve", wo3[:, kq * 4 + 2:kq * 4 + 4, :], w3[:, 2:4, :])
            TQ = 512
            yr = Rot([sb.alloc(16 * TQ, BF16, "ymx%d" % i) for i in range(2)])
            xr = Rot([sb.alloc(D, F32, "xe%d" % i) for i in range(3)])
            pr = Rot([ps.bank("pe%d" % i) for i in range(8)])
            for tq in range(S // TQ):
                ym = yr.next()
                ym3 = ym.re("p (k t) -> p k t", t=TQ)
                P.dma("sp", ym3, ymix[:, tq * TQ:(tq + 1) * TQ].rearrange("(k p) t -> p k t", p=128))
                for tb in range(TQ // 128):
                    r0 = tq * TQ + tb * 128
                    xe = xr.next()
                    P.dma("sp", xe, xsrc[r0:r0 + 128, :])
                    for cq in range(4):
                        po = pr.next()
                        for k in range(16):
                            P.mm(po, lhsT=ym3[:, k, tb * 128:(tb + 1) * 128], rhs=wo3[:, k, cq * 512:(cq + 1) * 512],
                                 start=(k == 0), stop=(k == 15))
                        P.tt("dve", xe[:, cq * 512:(cq + 1) * 512], po, xe[:, cq * 512:(cq + 1) * 512], ALU.add)
                    P.dma("act", out[r0:r0 + 128, :], xe)

        for l in range(L):
            xsrc = x_in if l == 0 else out
            load_params(l)
            P.barrier()
            phase_A(l, xsrc)
            P.barrier()
            phase_B(l)
            P.barrier()
            phase_C(l)
            P.barrier()
            phase_D(l)
            P.barrier()
            phase_E(l, xsrc)
            P.barrier()
        P.emit()
    import os
    if os.environ.get("KSTATS"):
        print("op counts", {e: len(P.ops[e]) for e in ENGS}, flush=True)
    return nc


_CACHE = {}


def run(inputs, S, L, n_cores, dbg=False, trace=False):
    key = (S, L, dbg)
    if key not in _CACHE:
        _CACHE[key] = build(S, L, dbg)
    nc = _CACHE[key]
    x = np.ascontiguousarray(np.asarray(inputs["x"], dtype=np.float32))
    B = x.shape[0]
    shared = {}
    for k, v in inputs.items():
        if k == "x":
            continue
        a = np.ascontiguousarray(np.asarray(v, dtype=np.float32))
        if k == "r_k":
            a = a.reshape(a.shape[0], -1)
        shared[k] = a
    in_maps = []
    for c in range(n_cores):
        m = dict(shared)
        m["x"] = x[c % B]
        in_maps.append(m)
    res = run_bass_kernel_spmd(nc, in_maps, core_ids=list(range(n_cores)))
    return res


def kernel(**inputs):
    x = np.asarray(inputs["x"])
    B, S, _ = x.shape
    L = np.asarray(inputs["norm_g"]).shape[0]
    res = run(inputs, S, L, 8)
    return np.stack([np.asarray(res.results[b]["out"], dtype=np.float32) for b in range(B)], axis=0)
```

```python
import contextlib
import math
import numpy as np
import concourse.bass as bass
import concourse.mybir as mybir
from concourse.bass_utils import run_bass_kernel_spmd

F32 = mybir.dt.float32
BF16 = mybir.dt.bfloat16
ALU = mybir.AluOpType
AF = mybir.ActivationFunctionType

ENGS = ("pe", "act", "dve", "pool", "sp")

D = 2048
RW = 768
PW = 512
SW = 768
A_COLS = 4 * RW + 128
B_COLS = 2 * PW
C_COLS = 4 * SW
IN_COLS = A_COLS + B_COLS + C_COLS
C0 = A_COLS + B_COLS
RMS_EPS = 1e-6
GN_EPS = 64e-5
CDEC = math.exp(-0.5)


class Buf:
    __slots__ = ("name", "w", "r", "rd", "excl")

    def __init__(self, name="", excl=False):
        self.name = name
        self.excl = excl
        self.w = None
        self.r = {}
        self.rd = []


class Op:
    __slots__ = ("eng", "fn", "dma", "deps", "seq", "needs_inc", "sem", "val", "prev_val", "xw")

    def __init__(self, eng, fn, dma):
        self.eng = eng
        self.fn = fn
        self.dma = dma
        self.deps = []
        self.needs_inc = False
        self.seq = None
        self.sem = None
        self.val = None
        self.prev_val = 0
        self.xw = None


class Tl:
    __slots__ = ("ap", "b")

    def __init__(self, ap, b):
        self.ap = ap
        self.b = b

    def __getitem__(self, k):
        return Tl(self.ap[k], self.b)

    def re(self, pat, **kw):
        return Tl(self.ap.rearrange(pat, **kw), self.b)

    def bc(self, shape):
        return Tl(self.ap.broadcast_to(shape), self.b)

    def wb(self, b):
        return Tl(self.ap, b)


def _ap(x):
    return x.ap if isinstance(x, Tl) else x


def _bufs(*xs):
    return [x.b for x in xs if isinstance(x, Tl) and x.b is not None]


class Prog:
    def __init__(self, nc, n_dma_sems=8):
        self.nc = nc
        self.ops = {e: [] for e in ENGS}
        self.n_dma_sems = n_dma_sems
        self.dma_rr = {e: 0 for e in ENGS}
        self.dma_cnt = {}

    def _dep(self, op, p, kind):
        if p is None or p is op:
            return
        if (not p.dma) and (not op.dma) and p.eng == op.eng:
            if kind != "raw" or op.eng == "pe":
                return
        op.deps.append(p)
        if not p.dma:
            p.needs_inc = True

    def op(self, eng, fn, reads=(), writes=(), dma=False):
        o = Op(eng, fn, dma)
        writes = list(writes) + [b for b in reads if b.excl and b not in writes]
        reads = [b for b in reads if not b.excl]
        for b in reads:
            self._dep(o, b.w, "raw")
        for b in writes:
            self._dep(o, b.w, "waw")
            for p in b.r.values():
                self._dep(o, p, "war")
            for p in b.rd:
                self._dep(o, p, "war")
        for b in reads:
            if dma:
                b.rd.append(o)
            else:
                b.r[eng] = o
        for b in writes:
            b.w = o
            b.r = {}
            b.rd = []
        if dma:
            k = (eng, self.dma_rr[eng] % self.n_dma_sems)
            self.dma_rr[eng] += 1
            o.sem = k
            o.prev_val = self.dma_cnt.get(k, 0)
            o.val = o.prev_val + 16
            self.dma_cnt[k] = o.val
        self.ops[eng].append(o)
        return o

    def barrier(self):
        lasts = []
        for e in ENGS:
            for o in reversed(self.ops[e]):
                if not o.dma and o.fn is not None:
                    lasts.append(o)
                    break
        snap = dict(self.dma_cnt)
        for e in ENGS:
            o = Op(e, None, False)
            for p in lasts:
                if p.eng != e:
                    o.deps.append(p)
                    p.needs_inc = True
            o.xw = snap
            self.ops[e].append(o)

    def dma(self, eng, out, in_, **kw):
        o_, i_ = _ap(out), _ap(in_)
        return self.op(eng, lambda e: e.dma_start(out=o_, in_=i_, **kw), _bufs(in_), _bufs(out), dma=True)

    def mm(self, out, lhsT, rhs, start=True, stop=True):
        o_, l_, r_ = _ap(out), _ap(lhsT), _ap(rhs)
        return self.op("pe", lambda e: e.matmul(o_, lhsT=l_, rhs=r_, start=start, stop=stop),
                       _bufs(lhsT, rhs), _bufs(out))

    def tr(self, out, in_, ident):
        o_, i_, d_ = _ap(out), _ap(in_), _ap(ident)
        return self.op("pe", lambda e: e.transpose(out=o_, in_=i_, identity=d_), _bufs(in_, ident), _bufs(out))

    def act(self, out, in_, func, bias=None, scale=1.0, accum=None):
        o_, i_ = _ap(out), _ap(in_)
        kw = {"scale": _ap(scale)}
        if bias is not None:
            kw["bias"] = _ap(bias)
        if accum is not None:
            kw["accum_out"] = _ap(accum)
        return self.op("act", lambda e: e.activation(out=o_, in_=i_, func=func, **kw),
                       _bufs(in_, bias, scale), _bufs(out, accum))

    def tt(self, eng, out, in0, in1, op):
        o_, a_, b_ = _ap(out), _ap(in0), _ap(in1)
        return self.op(eng, lambda e: e.tensor_tensor(out=o_, in0=a_, in1=b_, op=op), _bufs(in0, in1), _bufs(out))

    def ts(self, eng, out, in0, s1, s2, op0, op1=None):
        o_, a_, s1_, s2_ = _ap(out), _ap(in0), _ap(s1), _ap(s2)
        if op1 is None:
            fn = lambda e: e.tensor_scalar(out=o_, in0=a_, scalar1=s1_, scalar2=None, op0=op0)
        else:
            fn = lambda e: e.tensor_scalar(out=o_, in0=a_, scalar1=s1_, scalar2=s2_, op0=op0, op1=op1)
        return self.op(eng, fn, _bufs(in0, s1, s2), _bufs(out))

    def stt(self, out, in0, scalar, in1, op0, op1):
        o_, a_, s_, b_ = _ap(out), _ap(in0), _ap(scalar), _ap(in1)
        return self.op("dve", lambda e: e.scalar_tensor_tensor(out=o_, in0=a_, scalar=s_, in1=b_, op0=op0, op1=op1),
                       _bufs(in0, scalar, in1), _bufs(out))

    def copy(self, eng, out, in_):
        o_, i_ = _ap(out), _ap(in_)
        if eng == "act":
            fn = lambda e: e.copy(out=o_, in_=i_)
        else:
            fn = lambda e: e.tensor_copy(out=o_, in_=i_)
        return self.op(eng, fn, _bufs(in_), _bufs(out))

    def recip(self, out, in_):
        o_, i_ = _ap(out), _ap(in_)
        return self.op("dve", lambda e: e.reciprocal(out=o_, in_=i_), _bufs(in_), _bufs(out))

    def scan(self, out, d0, d1, init, op0, op1):
        o_, a_, b_, i_ = _ap(out), _ap(d0), _ap(d1), _ap(init)
        return self.op("dve", lambda e: e.tensor_tensor_scan(out=o_, data0=a_, data1=b_, initial=i_, op0=op0, op1=op1),
                       _bufs(d0, d1, init), _bufs(out))

    def memset(self, eng, out, val):
        o_ = _ap(out)
        return self.op(eng, lambda e: e.memset(o_, val), (), _bufs(out))

    def asel(self, out, in_, pattern, cmp, fill, base, cm):
        o_, i_ = _ap(out), _ap(in_)
        return self.op("pool", lambda e: e.affine_select(out=o_, in_=i_, pattern=pattern, compare_op=cmp, fill=fill,
                                                          base=base, channel_multiplier=cm), _bufs(in_), _bufs(out))

    def emit(self):
        nc = self.nc
        with contextlib.ExitStack() as st:
            esem = {e: st.enter_context(nc.semaphore("s_" + e)) for e in ENGS}
            dsem = {}
            for k in self.dma_cnt:
                dsem[k] = st.enter_context(nc.semaphore("d_%s_%d" % k))
            for e in ENGS:
                n = 0
                for o in self.ops[e]:
                    if o.needs_inc:
                        n += 1
                        o.seq = n
            final_waits = dict(self.dma_cnt)
            block = st.enter_context(nc.Block())

            def run(e, eng):
                waited = {}

                def wait(sem, key, val):
                    if waited.get(key, 0) >= val:
                        return
                    eng.wait_ge(sem, val)
                    waited[key] = val

                for o in self.ops[e]:
                    for p in o.deps:
                        if p.dma:
                            wait(dsem[p.sem], p.sem, p.val)
                        else:
                            wait(esem[p.eng], p.eng, p.seq)
                    if o.xw:
                        for k, v in o.xw.items():
                            wait(dsem[k], k, v)
                    if o.fn is None:
                        continue
                    if o.dma:
                        if o.prev_val:
                            wait(dsem[o.sem], o.sem, o.prev_val)
                        o.fn(eng).then_inc(dsem[o.sem], 16)
                    else:
                        ins = o.fn(eng)
                        if o.needs_inc:
                            ins.then_inc(esem[e], 1)
                if e == "sp":
                    for k, v in final_waits.items():
                        wait(dsem[k], k, v)

            @block.tensor
            def _(eng):
                run("pe", eng)

            @block.scalar
            def _(eng):
                run("act", eng)

            @block.vector
            def _(eng):
                run("dve", eng)

            @block.gpsimd
            def _(eng):
                run("pool", eng)

            @block.sync
            def _(eng):
                run("sp", eng)


class Arena:
    def __init__(self, t, nbytes):
        self.t = t
        self.nbytes = nbytes
        self.off = 0

    def alloc(self, cols, dtype=F32, name="", align=4):
        sz = cols * (2 if dtype == BF16 else 4)
        sz = (sz + 3) // 4 * 4
        self.off = (self.off + align - 1) // align * align
        a = self.off
        self.off += sz
        assert self.off <= self.nbytes, ("arena overflow", name, self.off, self.nbytes)
        ap = self.t[:, a // 4:(a + sz) // 4]
        if dtype == BF16:
            ap = ap.bitcast(BF16)[:, 0:cols]
        return Tl(ap, Buf(name))

    def bank(self, name=""):
        t = self.alloc(512, F32, name, align=2048)
        t.b.excl = True
        return t


class Rot:
    def __init__(self, items):
        self.items = items
        self.i = 0

    def next(self):
        x = self.items[self.i % len(self.items)]
        self.i += 1
        return x


def build(S, L, dbg=False):
    nc = bass.Bass("TRN2", target_bir_lowering=False)
    P = Prog(nc)
    NB = S // 128

    def din(name, shape):
        return nc.dram_tensor(name, shape, F32, kind="ExternalInput").ap()

    x_in = din("x", [S, D])
    norm_g = din("norm_g", [L, D])
    w_in = din("w_in", [L, D, IN_COLS])
    mu_a = din("mu_a", [L, A_COLS])
    w_up = din("w_up", [L, 64, RW])
    w0 = din("w0", [L, RW])
    a_up = din("a_up", [L, 64, RW])
    a0 = din("a0", [L, RW])
    k_k = din("k_k", [L, RW])
    k_a = din("k_a", [L, RW])
    r_k = din("r_k", [L, RW])
    gn_g = din("gn_g", [L, RW])
    gn_b = din("gn_b", [L, RW])
    pool_w = din("pool_w", [L, 4, 128, 128])
    pool_scale = din("pool_scale", [L, PW])
    qn_g = din("qn_g", [L, 64])
    kn_g = din("kn_g", [L, 64])
    w_out = din("w_out", [L, D, D])
    out = nc.dram_tensor("out", [S, D], F32, kind="ExternalOutput").ap()
    skind = "ExternalOutput" if dbg else "Internal"
    dbg_outs = {}

    def dump(name, tile, cols, dt=F32):
        if not dbg or name in dbg_outs:
            return
        dbg_outs[name] = nc.dram_tensor("dbg_" + name, [128, cols], dt, kind="ExternalOutput").ap()
        P.dma("sp", dbg_outs[name], tile)
    pj = nc.dram_tensor("pj", [IN_COLS, S], F32, kind=skind).ap()
    vtok = nc.dram_tensor("vtok", [S, SW], BF16, kind=skind).ap()
    ymix = nc.dram_tensor("ymix", [D, S], BF16, kind=skind).ap()

    with contextlib.ExitStack() as st:
        SB_BYTES = 206 * 1024
        sb_t = st.enter_context(nc.sbuf_tensor("arena", [128, SB_BYTES // 4], F32))
        ps_t = st.enter_context(nc.psum_tensor("psarena", [128, 4096], F32))
        sb = Arena(sb_t, SB_BYTES)
        ps = Arena(ps_t, 16384)

        identf = sb.alloc(128, F32, "identf")
        ident = sb.alloc(128, BF16, "ident")
        M2 = sb.alloc(256, F32, "M2")
        MLs = sb.alloc(128, F32, "MLs")
        MGE = sb.alloc(128, F32, "MGE")
        bones = sb.alloc(128, F32, "bones")
        zero1 = sb.alloc(1, F32, "zero1")
        epsr = sb.alloc(1, F32, "epsr")
        epsg = sb.alloc(1, F32, "epsg")
        TPB = min(512, S)
        reset = sb.alloc(TPB, F32, "reset")
        prm = sb.alloc(80, F32, "prm")
        qgs = sb.alloc(1, F32, "qgs")

        P.memset("pool", identf, 1.0)
        P.asel(identf, identf, [[-1, 128]], ALU.is_equal, 0.0, 0, 1)
        P.copy("dve", ident, identf)
        P.memset("pool", M2, 1.0)
        P.asel(M2[:, 0:128], M2[:, 0:128], [[1, 128]], ALU.is_gt, 0.0, 0, -1)
        P.asel(M2[:, 128:256], M2[:, 128:256], [[1, 128]], ALU.is_ge, 0.0, 0, -1)
        P.memset("pool", MLs, 1.0)
        P.asel(MLs, MLs, [[-1, 128]], ALU.is_gt, 0.0, 0, 1)
        P.ts("dve", MGE, MLs, -1.0, 1.0, ALU.mult, ALU.add)
        P.memset("dve", bones, 0.0)
        P.memset("dve", bones[0:64, 0:64], 1.0)
        P.memset("dve", bones[64:128, 64:128], 1.0)
        P.memset("dve", zero1, 0.0)
        P.memset("dve", epsr, RMS_EPS)
        P.memset("dve", epsg, GN_EPS)
        P.memset("dve", reset, 1.0)
        P.memset("dve", reset.re("p (c n) -> p c n", n=128)[:, :, 0:1], 0.0)
        const_off = sb.off

        PM_MU, PM_W0, PM_A0, PM_KK, PM_KA, PM_RK, PM_GG, PM_GB, PM_PS, PM_QN, PM_KN = 0, 25, 31, 37, 43, 49, 55, 61, 67, 71, 72

        def load_params(l):
            sb.off = const_off
            ps.off = 0
            stg = sb.alloc(128, F32, "prm_stage")
            P.memset("dve", stg, 0.0)
            stg_w = stg

            def ld(row0, src, n):
                P.dma("sp", stg_w[row0:row0 + n, :], src)
            ld(PM_MU, mu_a[l].rearrange("(t p) -> t p", p=128), 25)
            for r0, src in ((PM_W0, w0), (PM_A0, a0), (PM_KK, k_k), (PM_KA, k_a), (PM_RK, r_k), (PM_GG, gn_g),
                            (PM_GB, gn_b)):
                ld(r0, src[l].rearrange("(t p) -> t p", p=128), 6)
            ld(PM_PS, pool_scale[l].rearrange("(t p) -> t p", p=128), 4)
            P.dma("sp", stg_w[PM_QN:PM_QN + 1, 0:64], qn_g[l:l + 1, :])
            P.dma("sp", stg_w[PM_QN:PM_QN + 1, 64:128], qn_g[l:l + 1, :])
            P.dma("sp", stg_w[PM_KN:PM_KN + 1, 0:64], kn_g[l:l + 1, :])
            P.dma("sp", stg_w[PM_KN:PM_KN + 1, 64:128], kn_g[l:l + 1, :])
            pt = ps.bank("prm_ps")
            P.mm(pt[:, 0:73], lhsT=stg[0:73, :], rhs=identf[0:73, 0:73])
            P.copy("dve", prm[:, 0:73], pt[:, 0:73])
            P.ts("dve", qgs, prm[:, PM_QN:PM_QN + 1], 0.125, None, ALU.mult)

        def phase_A(l, xsrc):
            sb.off = const_off
            ps.off = 0
            TOKP = min(2048, S)
            npass = S // TOKP
            nb = TOKP // 128
            gbc = sb.alloc(D, F32, "gbc")
            P.dma("sp", gbc, norm_g[l].partition_broadcast(128))
            hT = sb.alloc(16 * TOKP, BF16, "hT")
            hT3 = hT.re("p (k t) -> p k t", t=TOKP)
            hbufs = [Buf("hT%d" % i) for i in range(TOKP // 512)]
            xrot = Rot([sb.alloc(D, F32, "x%d" % i) for i in range(2)])
            hrot = Rot([sb.alloc(D, BF16, "h%d" % i) for i in range(2)])
            junk = sb.alloc(D, BF16, "junk")
            ssr = Rot([sb.alloc(1, F32, "ss%d" % i) for i in range(4)])
            rsr = Rot([sb.alloc(1, F32, "rs%d" % i) for i in range(4)])
            CG = 384
            wfrot = Rot([sb.alloc(16 * CG, F32, "wf%d" % i) for i in range(2)])
            wbrot = Rot([sb.alloc(16 * CG, BF16, "wb%d" % i) for i in range(2)])
            ostrot = Rot([sb.alloc(512, F32, "ost%d" % i) for i in range(3)])
            vstrot = Rot([sb.alloc(CG, BF16, "vst%d" % i) for i in range(3)])
            trps = Rot([ps.bank("trp%d" % i) for i in range(2)])
            mops = Rot([ps.bank("mo%d" % i) for i in range(4)])
            for pa in range(npass):
                t0 = pa * TOKP
                for tb in range(nb):
                    xt = xrot.next()
                    P.dma("sp", xt, xsrc[t0 + tb * 128:t0 + (tb + 1) * 128, :])
                    ss = ssr.next()
                    rs = rsr.next()
                    P.memset("pool", ss, 0.0)
                    P.act(junk, xt, AF.Square, accum=ss)
                    P.act(rs, ss, AF.Sqrt, bias=epsr, scale=1.0 / D)
                    P.recip(rs, rs)
                    hb = hrot.next()
                    P.stt(hb, xt, rs, gbc, ALU.mult, ALU.mult)
                    hbuf = hbufs[tb // 4]
                    for q in range(4):
                        tp = trps.next()
                        tpb = Tl(tp.ap.bitcast(BF16)[:, 0:512], tp.b)
                        for j in range(4):
                            kc = 4 * q + j
                            P.tr(tpb[:, j * 128:(j + 1) * 128], hb[:, kc * 128:(kc + 1) * 128], ident)
                        dst = Tl(hT3.ap[:, 4 * q:4 * q + 4, tb * 128:(tb + 1) * 128], hbuf)
                        P.copy("act" if q % 2 == 0 else "dve", dst, tpb.re("p (j t) -> p j t", t=128))
                for cg in range(IN_COLS // CG):
                    wf = wfrot.next()
                    wsrc = w_in[l, :, cg * CG:(cg + 1) * CG].rearrange("(k p) c -> p k c", p=128)
                    wf3 = wf.re("p (k c) -> p k c", c=CG)
                    P.dma("sp", wf3[:, 0:8, :], wsrc[:, 0:8, :])
                    P.dma("pool", wf3[:, 8:16, :], wsrc[:, 8:16, :])
                    wb = wbrot.next()
                    wb3 = wb.re("p (k c) -> p k c", c=CG)
                    P.copy("pool", wb3[:, 0:6, :], wf3[:, 0:6, :])
                    P.copy("dve", wb3[:, 6:16, :], wf3[:, 6:16, :])
                    if cg not in (15, 16):
                        for ci in range(3):
                            ct = cg * 3 + ci
                            for tc in range(TOKP // 512):
                                po = mops.next()
                                for k in range(16):
                                    rhs = Tl(hT3.ap[:, k, tc * 512:(tc + 1) * 512], hbufs[tc])
                                    P.mm(po, lhsT=wb3[:, k, ci * 128:(ci + 1) * 128], rhs=rhs, start=(k == 0),
                                         stop=(k == 15))
                                ost = ostrot.next()
                                P.copy("act", ost, po)
                                P.dma("act", pj[ct * 128:(ct + 1) * 128, t0 + tc * 512:t0 + (tc + 1) * 512], ost)
                    else:
                        for tb in range(nb):
                            po = mops.next()
                            for k in range(16):
                                lhsT = Tl(hT3.ap[:, k, tb * 128:(tb + 1) * 128], hbufs[tb // 4])
                                P.mm(po[:, 0:CG], lhsT=lhsT, rhs=wb3[:, k, :], start=(k == 0), stop=(k == 15))
                            vst = vstrot.next()
                            P.copy("act", vst, po[:, 0:CG])
                            P.dma("act", vtok[t0 + tb * 128:t0 + (tb + 1) * 128, (cg - 15) * CG:(cg - 14) * CG], vst)

        def phase_B(l):
            sb.off = const_off
            ps.off = 0
            TP = TPB
            NCH = TP // 128
            npiece = S // TP
            NIT = 2 * NCH
            wup = sb.alloc(RW, F32, "wup")
            P.dma("sp", wup[0:64, :], w_up[l])
            P.dma("sp", wup[64:128, :], a_up[l])
            Sf = [[sb.alloc(64, F32, "Sf%d_%d" % (hp, i)) for i in range(2)] for hp in range(6)]
            Sb = [[sb.alloc(64, BF16, "Sb%d_%d" % (hp, i)) for i in range(2)] for hp in range(6)]
            cur = [0] * 6
            for hp in range(6):
                P.memset("dve", Sf[hp][0], 0.0)
                P.memset("dve", Sb[hp][0], 0.0)

            def f32t(name, n=TP):
                return sb.alloc(n, F32, name)

            def bft(name, n=TP):
                return sb.alloc(n, BF16, name)
            lda, wa, wa2 = f32t("lda", TP + 1), f32t("wa"), f32t("wa2")
            ldr = Rot([[f32t("ld%s%d" % (nm, i), TP + 1) for nm in "rkvg"] for i in range(2)])
            tmpd = f32t("tmpd")
            rs_, ks_, vs_, gs_ = f32t("rs"), f32t("ks"), f32t("vs"), f32t("gs")
            sig, a_, cs, Winv, csm, Wm, e2, E2 = (f32t(n) for n in ("sig", "a", "cs", "Winv", "csm", "Wm", "e2", "E2"))
            kk, kk2, nrm, kkn, tmpk, kp, b_ = (f32t(n) for n in ("kk", "kk2", "nrm", "kkn", "tmpk", "kp", "b"))
            rkr, sgt = f32t("rkr"), f32t("sgt")
            yc, sq, rstd, yn = f32t("yc"), f32t("sq"), f32t("rstd"), f32t("yn")
            KH, BH = bft("KH"), bft("BH")
            sets = []
            for i in range(2):
                d = {"AR": bft("AR%d" % i, 2 * TP), "KB": bft("KB%d" % i), "BB": bft("BB%d" % i), "VB": bft("VB%d" % i),
                     "W": f32t("W%d" % i), "bon": f32t("bon%d" % i), "sg": f32t("sg%d" % i), "ysb": f32t("ysb%d" % i),
                     "SA1": [bft("SA1_%d_%d" % (i, j), 256) for j in range(NIT)],
                     "SA2": [bft("SA2_%d_%d" % (i, j), 256) for j in range(NIT)],
                     "TT": [bft("TT_%d_%d" % (i, j), 128) for j in range(NIT)]}
                d["AR4"] = d["AR"].re("p (c w n) -> p c w n", w=2, n=128)
                sets.append(d)
            yarot = Rot([bft("ya%d" % i) for i in range(2)])
            TOKr = Rot([bft("TOK%d" % i, 384) for i in range(2)])
            Xr = [Rot([bft("X%d_%d" % (i, j), 128) for j in range(2)]) for i in range(NIT)]
            XTr = [Rot([bft("XT%d_%d" % (i, j), 128) for j in range(2)]) for i in range(NIT)]
            Pr = [Rot([bft("P%d_%d" % (i, j), 128) for j in range(2)]) for i in range(NIT)]
            Zbr = Rot([bft("Zb%d" % i, 64) for i in range(2)])
            Ubr = Rot([bft("Ub%d" % i, 64) for i in range(2)])
            pgen = ps.bank("pgenP")
            pgc = ps.bank("pgenC")
            pxb = [ps.bank("pxb%d" % i) for i in range(3)]
            ppb = [ps.bank("ppb%d" % i) for i in range(2)]
            pmisc = ps.bank("pmisc")
            ptr = Tl(pmisc.ap.bitcast(BF16)[:, 0:384], pmisc.b)
            pzr = Rot([Tl(pmisc.ap[:, 192:256], pmisc.b), Tl(pmisc.ap[:, 256:320], pmisc.b)])
            pur = Rot([Tl(pmisc.ap[:, 320:384], pmisc.b), Tl(pmisc.ap[:, 384:448], pmisc.b)])
            psn = Tl(pmisc.ap[:, 448:512], pmisc.b)
            items = [(c, h) for c in range(NCH) for h in range(2)]
            TPv = "p (c n) -> p c n"

            def px_tiles(it):
                bk = pxb[(it // 2) % 3]
                o = (it % 2) * 256
                return bk[:, o:o + 128], bk[:, o + 128:o + 256]

            def prep_scores(pc, hp, st):
                t0 = pc * TP
                AR, AR4, KB, BB, VB, W, bon, sg = (st[k] for k in ("AR", "AR4", "KB", "BB", "VB", "W", "bon", "sg"))
                SA1, SA2 = st["SA1"], st["SA2"]

                def load_shift(dst, row0, eng="sp"):
                    if t0 > 0:
                        P.dma(eng, dst, pj[row0:row0 + 128, t0 - 1:t0 + TP])
                    else:
                        P.memset("pool", dst[:, 0:1], 0.0)
                        P.dma(eng, dst[:, 1:TP + 1], pj[row0:row0 + 128, 0:TP])

                def lerp(dst, ld, mucol):
                    P.tt("pool", tmpd, ld[:, 0:TP], ld[:, 1:TP + 1], ALU.subtract)
                    P.stt(dst, tmpd, prm[:, mucol:mucol + 1], ld[:, 1:TP + 1], ALU.mult, ALU.add)
                if hp == 0:
                    load_shift(lda, 3072)
                    lerp(wa, lda, PM_MU + 24)
                    P.act(wa2[0:64, :], wa[0:64, :], AF.Tanh)
                    P.copy("pool", wa2[64:128, :], wa[64:128, :])
                lds = ldr.next()
                for i, (ld, r0) in enumerate(zip(lds, (0, 768, 1536, 2304))):
                    load_shift(ld, r0 + hp * 128, "sp" if i % 2 == 0 else "act")
                lerp(rs_, lds[0], PM_MU + hp)
                lerp(ks_, lds[1], PM_MU + 6 + hp)
                lerp(vs_, lds[2], PM_MU + 12 + hp)
                lerp(gs_, lds[3], PM_MU + 18 + hp)
                P.mm(pgen[:, 0:TP], lhsT=wup[0:64, hp * 128:(hp + 1) * 128], rhs=wa2[0:64, :])
                P.act(sig, pgen[:, 0:TP], AF.Sigmoid, bias=prm[:, PM_W0 + hp:PM_W0 + hp + 1])
                P.mm(pgen[:, 0:TP], lhsT=wup[64:128, hp * 128:(hp + 1) * 128], rhs=wa2[64:128, :])
                P.act(a_, pgen[:, 0:TP], AF.Sigmoid, bias=prm[:, PM_A0 + hp:PM_A0 + hp + 1])
                P.act(sgt, gs_, AF.Sigmoid)
                P.tt("pool", sg, sgt, gs_, ALU.mult)
                P.scan(cs, reset, sig, 0.0, ALU.mult, ALU.add)
                P.act(W, cs, AF.Exp, scale=-CDEC)
                P.act(Winv, cs, AF.Exp, scale=CDEC)
                P.tt("pool", csm, cs, sig, ALU.subtract)
                P.act(Wm, csm, AF.Exp, scale=-CDEC)
                cs3 = cs.re(TPv, n=128)
                P.tt("dve", e2.re(TPv, n=128), cs3[:, :, 127:128].bc([128, NCH, 128]), cs3, ALU.subtract)
                P.act(E2, e2, AF.Exp, scale=-CDEC)
                P.ts("dve", kk, ks_, prm[:, PM_KK + hp:PM_KK + hp + 1], None, ALU.mult)
                P.act(kk2, kk, AF.Square)
                P.mm(pgen[:, 0:TP], lhsT=bones, rhs=kk2)
                P.act(nrm, pgen[:, 0:TP], AF.Sqrt)
                P.ts("dve", nrm, nrm, 1e-12, None, ALU.max)
                P.recip(nrm, nrm)
                P.tt("dve", kkn, kk, nrm, ALU.mult)
                P.ts("dve", tmpk, a_, -1.0, prm[:, PM_KA + hp:PM_KA + hp + 1], ALU.add, ALU.mult)
                P.stt(kp, tmpk, 1.0, ks_, ALU.add, ALU.mult)
                P.tt("dve", b_, kkn, a_, ALU.mult)
                P.stt(AR4[:, :, 0, :], kkn.re(TPv, n=128), -1.0, Wm.re(TPv, n=128), ALU.mult, ALU.mult)
                P.tt("pool", AR4[:, :, 1, :], rs_.re(TPv, n=128), W.re(TPv, n=128), ALU.mult)
                P.tt("dve", KH, kp, Winv, ALU.mult)
                P.tt("dve", BH, b_, Winv, ALU.mult)
                P.tt("dve", KB, kp, E2, ALU.mult)
                P.tt("pool", BB, b_, E2, ALU.mult)
                P.copy("act", VB, vs_)
                P.stt(rkr, rs_, prm[:, PM_RK + hp:PM_RK + hp + 1], kp, ALU.mult, ALU.mult)
                P.mm(pgen[:, 0:TP], lhsT=bones, rhs=rkr)
                P.tt("dve", bon, pgen[:, 0:TP], vs_, ALU.mult)
                ctx = {"st": st, "hp": hp, "t0": t0, "X": {}, "XT": {}, "P": {}}
                for it, (c, h) in enumerate(items):
                    sl = slice(64 * h, 64 * h + 64)
                    csl = slice(c * 128, (c + 1) * 128)
                    pscb = pxb[it % 3]
                    ps1 = pscb[:, 0:256]
                    ps2 = pscb[:, 256:512]
                    arc = Tl(AR4.ap[sl, c].rearrange("p w n -> p (w n)"), AR.b)
                    P.mm(ps1, lhsT=BH[sl, csl], rhs=arc)
                    P.mm(ps2, lhsT=KH[sl, csl], rhs=arc)
                    ps3 = ppb[it // 4][:, (it % 4) * 128:(it % 4 + 1) * 128]
                    P.mm(ps3, lhsT=AR4[sl, c, 0, :], rhs=BH[sl, csl])
                    P.tt("dve", SA1[it], ps1, M2, ALU.mult)
                    P.tt("dve", SA2[it], ps2, M2, ALU.mult)
                    xt0 = XTr[it].next()
                    P.tt("dve", xt0, ps3, MLs, ALU.mult)
                    p0 = Pr[it].next()
                    P.tt("pool", p0, SA1[it][:, 0:128], ident, ALU.add)
                    ctx["X"][it], ctx["XT"][it], ctx["P"][it] = SA1[it][:, 0:128], xt0, p0
                return ctx

            def level(ctx, k):
                Xs, XTs, Ps, st = ctx["X"], ctx["XT"], ctx["P"], ctx["st"]
                for half in range(0, NIT, 4):
                    its = list(range(half, min(half + 4, NIT)))
                    pend = {}
                    for it in its:
                        pp = ppb[it // 4][:, (it % 4) * 128:(it % 4 + 1) * 128]
                        px, pxt = px_tiles(it)
                        if k >= 1:
                            P.mm(pp, lhsT=XTs[it], rhs=Ps[it])
                        if k < 6:
                            P.mm(px, lhsT=XTs[it], rhs=Xs[it])
                            P.mm(pxt, lhsT=Xs[it], rhs=XTs[it])
                        pend[it] = (pp, px, pxt)
                    for it in its:
                        pp, px, pxt = pend[it]
                        if k >= 1:
                            pn_ = st["TT"][it] if k == 6 else Pr[it].next()
                            P.tt("dve", pn_, pp, Ps[it], ALU.add)
                            Ps[it] = pn_
                        if k < 6:
                            xn = Xr[it].next()
                            P.copy("act", xn, px)
                            xtn = XTr[it].next()
                            P.copy("act", xtn, pxt)
                            Xs[it], XTs[it] = xn, xtn

            def chain_step(ctx, c):
                st, hp = ctx["st"], ctx["hp"]
                AR4, KB, BB, VB, W, ysb = (st[k] for k in ("AR4", "KB", "BB", "VB", "W", "ysb"))
                SA1, SA2, Ps = st["SA1"], st["SA2"], ctx["P"]
                csl = slice(c * 128, (c + 1) * 128)
                TOK = TOKr.next()
                P.tr(ptr[:, 0:128], VB[:, csl], ident)
                P.tr(ptr[:, 128:256], KB[:, csl], ident)
                P.tr(ptr[:, 256:384], BB[:, csl], ident)
                P.copy("act", TOK, ptr)
                sfo, sbo = Sf[hp][cur[hp]], Sb[hp][cur[hp]]
                sfn, sbn = Sf[hp][1 - cur[hp]], Sb[hp][1 - cur[hp]]
                hv = []
                for h in range(2):
                    it = items.index((c, h))
                    sl = slice(64 * h, 64 * h + 64)
                    hv.append((it, sl, TOK[:, 64 * h:64 * h + 64], TOK[:, 128 + 64 * h:128 + 64 * h + 64],
                               TOK[:, 256 + 64 * h:256 + 64 * h + 64]))
                pzs, zbs, pus, ubs = [], [], [], []
                for (it, sl, vt, kbt, bbt) in hv:
                    pz = pzr.next()
                    P.mm(pz, lhsT=AR4[sl, c, 0, :], rhs=sbo[sl, :], start=True, stop=False)
                    P.mm(pz, lhsT=SA2[it][:, 0:128], rhs=vt, start=False, stop=True)
                    pzs.append(pz)
                for pz in pzs:
                    zb = Zbr.next()
                    P.copy("act", zb, pz)
                    zbs.append(zb)
                for (it, sl, vt, kbt, bbt), zb in zip(hv, zbs):
                    pu = pur.next()
                    P.mm(pu, lhsT=Ps[it], rhs=zb)
                    pus.append(pu)
                for pu in pus:
                    ub = Ubr.next()
                    P.copy("act", ub, pu)
                    ubs.append(ub)
                for (it, sl, vt, kbt, bbt), ub in zip(hv, ubs):
                    P.mm(pgc[sl, 0:128], lhsT=sbo[sl, :], rhs=AR4[sl, c, 1, :], start=True, stop=False)
                    P.mm(pgc[sl, 0:128], lhsT=ub, rhs=SA1[it][:, 128:256], start=False, stop=False)
                    P.mm(pgc[sl, 0:128], lhsT=vt, rhs=SA2[it][:, 128:256], start=False, stop=True)
                for (it, sl, vt, kbt, bbt), ub in zip(hv, ubs):
                    P.mm(psn[sl, :], lhsT=bbt, rhs=ub, start=True, stop=False)
                    P.mm(psn[sl, :], lhsT=kbt, rhs=vt, start=False, stop=True)
                P.copy("act", ysb[:, csl], pgc[:, 0:128])
                wc = W[:, c * 128 + 127:c * 128 + 128]
                P.stt(sfn, sfo, wc, psn, ALU.mult, ALU.add)
                P.stt(sbn, sfo, wc, psn, ALU.mult, ALU.add)
                cur[hp] = 1 - cur[hp]

            def post(ctx):
                st, hp, t0 = ctx["st"], ctx["hp"], ctx["t0"]
                ysb, bon, sg = st["ysb"], st["bon"], st["sg"]
                P.mm(pgc[:, 0:TP], lhsT=bones, rhs=ysb)
                P.stt(yc, pgc[:, 0:TP], -1.0 / 64, ysb, ALU.mult, ALU.add)
                P.act(sq, yc, AF.Square)
                P.mm(pgc[:, 0:TP], lhsT=bones, rhs=sq)
                P.act(rstd, pgc[:, 0:TP], AF.Sqrt, bias=epsg, scale=1.0 / 64)
                P.recip(rstd, rstd)
                P.tt("pool", yn, yc, rstd, ALU.mult)
                P.ts("dve", yn, yn, prm[:, PM_GG + hp:PM_GG + hp + 1], prm[:, PM_GB + hp:PM_GB + hp + 1], ALU.mult,
                     ALU.add)
                P.tt("pool", yn, yn, bon, ALU.add)
                ya = yarot.next()
                P.tt("dve", ya, yn, sg, ALU.mult)
                P.dma("sp", ymix[hp * 128:(hp + 1) * 128, t0:t0 + TP], ya)

            units = [(pc, hp) for pc in range(npiece) for hp in range(6)]
            prev = None
            for ui, (pc, hp) in enumerate(units):
                ctx = prep_scores(pc, hp, sets[ui % 2])
                for k in range(7):
                    level(ctx, k)
                    if prev is not None and k >= 1 and (k - 1) < NCH:
                        chain_step(prev, k - 1)
                if prev is not None:
                    for c in range(6, NCH):
                        chain_step(prev, c)
                    post(prev)
                prev = ctx
            for c in range(NCH):
                chain_step(prev, c)
            post(prev)

        def phase_C(l):
            sb.off = const_off
            ps.off = 0
            TP = min(1024, S)
            npiece = S // TP
            H = 16
            pwf = sb.alloc(512, F32, "pwf")
            pwb = sb.alloc(512, BF16, "pwb")
            P.dma("sp", pwf.re("p (g d) -> p g d", d=128), pool_w[l].rearrange("g c d -> c g d"))
            P.copy("dve", pwb, pwf)
            corr = sb.alloc(4 * 16, F32, "corr")
            for g, win in enumerate((2, 4, 8, 16)):
                for t in range(16):
                    P.memset("pool", corr[:, g * 16 + t:g * 16 + t + 1], float(win) / min(t + 1, win))
            urot = Rot([sb.alloc(TP + H, F32, "pu%d" % i) for i in range(2)])
            grot = Rot([sb.alloc(TP, F32, "pg%d" % i) for i in range(2)])
            sA = sb.alloc(TP + H, F32, "sA")
            sB = sb.alloc(TP + H, F32, "sB")
            dbf = sb.alloc(TP, BF16, "dbf")
            sgp = sb.alloc(TP, F32, "sgp")
            ybr = Rot([sb.alloc(TP, BF16, "yb%d" % i) for i in range(2)])
            pyr = Rot([ps.bank("pc%d" % i) for i in range(2)])
            for pc in range(npiece):
                t0 = pc * TP
                for g, win in enumerate((2, 4, 8, 16)):
                    u = urot.next()
                    row = A_COLS + g * 128
                    if t0 > 0:
                        P.dma("sp", u, pj[row:row + 128, t0 - H:t0 + TP])
                    else:
                        P.memset("pool", u[:, 0:H], 0.0)
                        P.dma("sp", u[:, H:H + TP], pj[row:row + 128, 0:TP])
                    pg_ = grot.next()
                    P.dma("act", pg_, pj[A_COLS + PW + g * 128:A_COLS + PW + (g + 1) * 128, t0:t0 + TP])
                    src = u
                    sh = 1
                    dsts = [sA, sB]
                    lo = 0
                    for lev in range(g + 1):
                        dst = dsts[lev % 2]
                        lo += sh
                        P.tt("pool" if lev % 2 else "dve", dst[:, lo:TP + H], src[:, lo:TP + H], src[:, lo - sh:TP + H - sh],
                             ALU.add)
                        src = dst
                        sh *= 2
                    ssum = src
                    if t0 == 0:
                        P.tt("dve", ssum[:, H:H + 16], ssum[:, H:H + 16], corr[:, g * 16:(g + 1) * 16], ALU.mult)
                    P.stt(dbf, ssum[:, H:H + TP], 1.0 / win, u[:, H:H + TP], ALU.mult, ALU.subtract)
                    P.act(sgp, pg_, AF.Silu)
                    yb = ybr.next()
                    for hf in range(TP // 512):
                        py = pyr.next()
                        P.mm(py, lhsT=pwb[:, g * 128:(g + 1) * 128], rhs=dbf[:, hf * 512:(hf + 1) * 512])
                        P.stt(yb[:, hf * 512:(hf + 1) * 512], py, prm[:, PM_PS + g:PM_PS + g + 1],
                              sgp[:, hf * 512:(hf + 1) * 512], ALU.mult, ALU.mult)
                    P.dma("sp", ymix[RW + g * 128:RW + (g + 1) * 128, t0:t0 + TP], yb)

        def phase_D(l):
            sb.off = const_off
            ps.off = 0
            TPq = min(1024, S)
            QH = sb.alloc(S, BF16, "QH")
            KHt = sb.alloc(S, BF16, "KHt")
            V = sb.alloc(NB * 128, BF16, "V")
            V3 = V.re("p (n c) -> p n c", c=128)
            sgc = sb.alloc(S, BF16, "sgc")
            ycst = sb.alloc(S, BF16, "ycst")
            SGr = Rot([sb.alloc(S, F32, "SG%d" % i) for i in range(2)])
            EZr = Rot([sb.alloc(S, BF16, "EZ%d" % i) for i in range(2)])
            Pn = sb.alloc(S, F32, "Pn")
            Wtr = Rot([sb.alloc(S, BF16, "Wt%d" % i) for i in range(2)])
            WTr = Rot([sb.alloc(S, BF16, "WT%d" % i) for i in range(2)])
            qf = Rot([sb.alloc(TPq, F32, "qf%d" % i) for i in range(2)])
            sqq = sb.alloc(TPq, F32, "sqq")
            rt = sb.alloc(TPq, F32, "rt")
            zr = Rot([ps.bank("z%d" % i) for i in range(3)])
            ptwr = Rot([ps.bank("ptw%d" % i) for i in range(2)])
            por = Rot([ps.bank("po%d" % i) for i in range(2)])
            pgen = ps.bank("pgenD")
            for hp in range(6):
                P.dma("sp", V3, vtok[:, hp * 128:(hp + 1) * 128].rearrange("(n p) c -> p n c", p=128))
                for (dst, row0, gcol) in ((QH, C0 + hp * 128, qgs), (KHt, C0 + SW + hp * 128, prm[:, PM_KN:PM_KN + 1])):
                    for pc in range(S // TPq):
                        q = qf.next()
                        P.dma("sp", q, pj[row0:row0 + 128, pc * TPq:(pc + 1) * TPq])
                        P.act(sqq, q, AF.Square)
                        for hf in range(TPq // 512):
                            hs = slice(hf * 512, (hf + 1) * 512)
                            P.mm(pgen, lhsT=bones, rhs=sqq[:, hs])
                            P.act(rt[:, hs], pgen, AF.Sqrt, bias=epsr, scale=1.0 / 64)
                        P.recip(rt, rt)
                        P.stt(dst[:, pc * TPq:(pc + 1) * TPq], q, gcol, rt, ALU.mult, ALU.mult)
                for pc in range(S // TPq):
                    q = qf.next()
                    r0 = C0 + 3 * SW + hp * 128
                    P.dma("act", q, pj[r0:r0 + 128, pc * TPq:(pc + 1) * TPq])
                    P.act(sgc[:, pc * TPq:(pc + 1) * TPq], q, AF.Silu)
                items = [(T, h) for T in range(NB) for h in range(2)]
                state = {}

                def stage1(T, h):
                    sl = slice(64 * h, 64 * h + 64)
                    kend = (T + 1) * 128
                    SG = SGr.next()
                    EZ = EZr.next()
                    for ck in range((kend + 511) // 512):
                        w_ = min(512, kend - ck * 512)
                        pz = zr.next()
                        P.mm(pz[:, 0:w_], lhsT=QH[sl, T * 128:(T + 1) * 128], rhs=KHt[sl, ck * 512:ck * 512 + w_])
                        P.act(SG[:, ck * 512:ck * 512 + w_], pz[:, 0:w_], AF.Sigmoid, scale=-1.0)
                        P.act(EZ[:, ck * 512:ck * 512 + w_], pz[:, 0:w_], AF.Sigmoid)
                    dsl = slice(T * 128, kend)
                    P.tt("pool", SG[:, dsl], SG[:, dsl], MLs, ALU.mult)
                    P.tt("pool", SG[:, dsl], SG[:, dsl], MGE, ALU.add)
                    P.tt("pool", EZ[:, dsl], EZ[:, dsl], MLs, ALU.mult)
                    P.scan(Pn[:, 0:kend][:, ::-1], SG[:, 0:kend][:, ::-1], zero1.bc([128, kend]), 1.0, ALU.mult, ALU.add)
                    Wt = Wtr.next()
                    npl = ((kend - 1) * 3 // 8) // 64 * 64
                    if npl > 0:
                        P.tt("pool", Wt[:, 0:npl], EZ[:, 0:npl], Pn[:, 1:npl + 1], ALU.mult)
                    P.tt("dve", Wt[:, npl:kend - 1], EZ[:, npl:kend - 1], Pn[:, npl + 1:kend], ALU.mult)
                    P.memset("pool", Wt[:, kend - 1:kend], 0.0)
                    state[(T, h)] = Wt

                def stage2(T, h):
                    sl = slice(64 * h, 64 * h + 64)
                    Wt = state.pop((T, h))
                    WT = WTr.next()
                    nsb = T + 1
                    for g0 in range(0, nsb, 4):
                        n = min(4, nsb - g0)
                        pt = ptwr.next()
                        ptb = Tl(pt.ap.bitcast(BF16)[:, 0:512], pt.b)
                        for j in range(n):
                            P.tr(ptb[:, j * 128:(j + 1) * 128], Wt[:, (g0 + j) * 128:(g0 + j + 1) * 128], ident)
                        P.copy("act", WT[:, g0 * 128:(g0 + n) * 128], ptb[:, 0:n * 128])
                    if h == 0:
                        state["po"] = por.next()
                    po = state["po"]
                    for sbk in range(nsb):
                        P.mm(po[sl, 0:128], lhsT=V3[:, sbk, 64 * h:64 * h + 64], rhs=WT[:, sbk * 128:(sbk + 1) * 128],
                             start=(sbk == 0), stop=(sbk == nsb - 1))
                    if h == 1:
                        P.tt("dve", ycst[:, T * 128:(T + 1) * 128], po[:, 0:128], sgc[:, T * 128:(T + 1) * 128], ALU.mult)

                for i in range(len(items) + 1):
                    if i < len(items):
                        stage1(*items[i])
                    if i >= 1:
                        stage2(*items[i - 1])
                P.dma("sp", ymix[RW + PW + hp * 128:RW + PW + (hp + 1) * 128, :], ycst)

        def phase_E(l, xsrc):
            sb.off = const_off
            ps.off = 0
            wo = sb.alloc(16 * D, BF16, "wo")
            wo3 = wo.re("p (k c) -> p k c", c=D)
            wst = Rot([sb.alloc(4 * D, F32, "wst%d" % i) for i in range(2)])
            for kq in range(4):
                w = wst.next()
                w3 = w.re("p (k c) -> p k c", c=D)
                P.dma("sp" if kq % 2 == 0 else "act", w3,
                      w_out[l, kq * 512:(kq + 1) * 512, :].rearrange("(k p) c -> p k c", p=128))
                P.copy("pool", wo3[:, kq * 4:kq * 4 + 2, :], w3[:, 0:2, :])
                P.copy("dve", wo3[:, kq * 4 + 2:kq * 4 + 4, :], w3[:, 2:4, :])
            TQ = 512
            yr = Rot([sb.alloc(16 * TQ, BF16, "ymx%d" % i) for i in range(2)])
            xr = Rot([sb.alloc(D, F32, "xe%d" % i) for i in range(3)])
            pr = Rot([ps.bank("pe%d" % i) for i in range(8)])
            for tq in range(S // TQ):
                ym = yr.next()
                ym3 = ym.re("p (k t) -> p k t", t=TQ)
                P.dma("sp", ym3, ymix[:, tq * TQ:(tq + 1) * TQ].rearrange("(k p) t -> p k t", p=128))
                for tb in range(TQ // 128):
                    r0 = tq * TQ + tb * 128
                    xe = xr.next()
                    P.dma("sp", xe, xsrc[r0:r0 + 128, :])
                    for cq in range(4):
                        po = pr.next()
                        for k in range(16):
                            P.mm(po, lhsT=ym3[:, k, tb * 128:(tb + 1) * 128], rhs=wo3[:, k, cq * 512:(cq + 1) * 512],
                                 start=(k == 0), stop=(k == 15))
                        P.tt("dve", xe[:, cq * 512:(cq + 1) * 512], po, xe[:, cq * 512:(cq + 1) * 512], ALU.add)
                    P.dma("act", out[r0:r0 + 128, :], xe)

        for l in range(L):
            xsrc = x_in if l == 0 else out
            load_params(l)
            P.barrier()
            phase_A(l, xsrc)
            P.barrier()
            phase_B(l)
            P.barrier()
            phase_C(l)
            P.barrier()
            phase_D(l)
            P.barrier()
            phase_E(l, xsrc)
            P.barrier()
        P.emit()
    import os
    if os.environ.get("KSTATS"):
        print("op counts", {e: len(P.ops[e]) for e in ENGS}, flush=True)
    return nc


_CACHE = {}


def run(inputs, S, L, n_cores, dbg=False, trace=False):
    key = (S, L, dbg)
    if key not in _CACHE:
        _CACHE[key] = build(S, L, dbg)
    nc = _CACHE[key]
    x = np.ascontiguousarray(np.asarray(inputs["x"], dtype=np.float32))
    B = x.shape[0]
    shared = {}
    for k, v in inputs.items():
        if k == "x":
            continue
        a = np.ascontiguousarray(np.asarray(v, dtype=np.float32))
        if k == "r_k":
            a = a.reshape(a.shape[0], -1)
        shared[k] = a
    in_maps = []
    for c in range(n_cores):
        m = dict(shared)
        m["x"] = x[c % B]
        in_maps.append(m)
    res = run_bass_kernel_spmd(nc, in_maps, core_ids=list(range(n_cores)))
    return res


def kernel(**inputs):
    x = np.asarray(inputs["x"])
    B, S, _ = x.shape
    L = np.asarray(inputs["norm_g"]).shape[0]
    res = run(inputs, S, L, 8)
    return np.stack([np.asarray(res.results[b]["out"], dtype=np.float32) for b in range(B)], axis=0)
```

```python
import contextlib
import math
import numpy as np
import concourse.bass as bass
import concourse.mybir as mybir
from concourse.bass_utils import run_bass_kernel_spmd

F32 = mybir.dt.float32
BF16 = mybir.dt.bfloat16
ALU = mybir.AluOpType
AF = mybir.ActivationFunctionType

ENGS = ("pe", "act", "dve", "pool", "sp")

D = 2048
RW = 768
PW = 512
SW = 768
A_COLS = 4 * RW + 128
B_COLS = 2 * PW
C_COLS = 4 * SW
IN_COLS = A_COLS + B_COLS + C_COLS
C0 = A_COLS + B_COLS
RMS_EPS = 1e-6
GN_EPS = 64e-5
CDEC = math.exp(-0.5)


class Buf:
    __slots__ = ("name", "w", "r", "rd", "excl")

    def __init__(self, name="", excl=False):
        self.name = name
        self.excl = excl
        self.w = None
        self.r = {}
        self.rd = []


class Op:
    __slots__ = ("eng", "fn", "dma", "deps", "seq", "needs_inc", "sem", "val", "prev_val", "xw")

    def __init__(self, eng, fn, dma):
        self.eng = eng
        self.fn = fn
        self.dma = dma
        self.deps = []
        self.needs_inc = False
        self.seq = None
        self.sem = None
        self.val = None
        self.prev_val = 0
        self.xw = None


class Tl:
    __slots__ = ("ap", "b")

    def __init__(self, ap, b):
        self.ap = ap
        self.b = b

    def __getitem__(self, k):
        return Tl(self.ap[k], self.b)

    def re(self, pat, **kw):
        return Tl(self.ap.rearrange(pat, **kw), self.b)

    def bc(self, shape):
        return Tl(self.ap.broadcast_to(shape), self.b)

    def wb(self, b):
        return Tl(self.ap, b)


def _ap(x):
    return x.ap if isinstance(x, Tl) else x


def _bufs(*xs):
    return [x.b for x in xs if isinstance(x, Tl) and x.b is not None]


class Prog:
    def __init__(self, nc, n_dma_sems=8):
        self.nc = nc
        self.ops = {e: [] for e in ENGS}
        self.n_dma_sems = n_dma_sems
        self.dma_rr = {e: 0 for e in ENGS}
        self.dma_cnt = {}

    def _dep(self, op, p, kind):
        if p is None or p is op:
            return
        if (not p.dma) and (not op.dma) and p.eng == op.eng:
            if kind != "raw" or op.eng == "pe":
                return
        op.deps.append(p)
        if not p.dma:
            p.needs_inc = True

    def op(self, eng, fn, reads=(), writes=(), dma=False):
        o = Op(eng, fn, dma)
        writes = list(writes) + [b for b in reads if b.excl and b not in writes]
        reads = [b for b in reads if not b.excl]
        for b in reads:
            self._dep(o, b.w, "raw")
        for b in writes:
            self._dep(o, b.w, "waw")
            for p in b.r.values():
                self._dep(o, p, "war")
            for p in b.rd:
                self._dep(o, p, "war")
        for b in reads:
            if dma:
                b.rd.append(o)
            else:
                b.r[eng] = o
        for b in writes:
            b.w = o
            b.r = {}
            b.rd = []
        if dma:
            k = (eng, self.dma_rr[eng] % self.n_dma_sems)
            self.dma_rr[eng] += 1
            o.sem = k
            o.prev_val = self.dma_cnt.get(k, 0)
            o.val = o.prev_val + 16
            self.dma_cnt[k] = o.val
        self.ops[eng].append(o)
        return o

    def barrier(self):
        lasts = []
        for e in ENGS:
            for o in reversed(self.ops[e]):
                if not o.dma and o.fn is not None:
                    lasts.append(o)
                    break
        snap = dict(self.dma_cnt)
        for e in ENGS:
            o = Op(e, None, False)
            for p in lasts:
                if p.eng != e:
                    o.deps.append(p)
                    p.needs_inc = True
            o.xw = snap
            self.ops[e].append(o)

    def dma(self, eng, out, in_, **kw):
        o_, i_ = _ap(out), _ap(in_)
        return self.op(eng, lambda e: e.dma_start(out=o_, in_=i_, **kw), _bufs(in_), _bufs(out), dma=True)

    def mm(self, out, lhsT, rhs, start=True, stop=True):
        o_, l_, r_ = _ap(out), _ap(lhsT), _ap(rhs)
        return self.op("pe", lambda e: e.matmul(o_, lhsT=l_, rhs=r_, start=start, stop=stop),
                       _bufs(lhsT, rhs), _bufs(out))

    def tr(self, out, in_, ident):
        o_, i_, d_ = _ap(out), _ap(in_), _ap(ident)
        return self.op("pe", lambda e: e.transpose(out=o_, in_=i_, identity=d_), _bufs(in_, ident), _bufs(out))

    def act(self, out, in_, func, bias=None, scale=1.0, accum=None):
        o_, i_ = _ap(out), _ap(in_)
        kw = {"scale": _ap(scale)}
        if bias is not None:
            kw["bias"] = _ap(bias)
        if accum is not None:
            kw["accum_out"] = _ap(accum)
        return self.op("act", lambda e: e.activation(out=o_, in_=i_, func=func, **kw),
                       _bufs(in_, bias, scale), _bufs(out, accum))

    def tt(self, eng, out, in0, in1, op):
        o_, a_, b_ = _ap(out), _ap(in0), _ap(in1)
        return self.op(eng, lambda e: e.tensor_tensor(out=o_, in0=a_, in1=b_, op=op), _bufs(in0, in1), _bufs(out))

    def ts(self, eng, out, in0, s1, s2, op0, op1=None):
        o_, a_, s1_, s2_ = _ap(out), _ap(in0), _ap(s1), _ap(s2)
        if op1 is None:
            fn = lambda e: e.tensor_scalar(out=o_, in0=a_, scalar1=s1_, scalar2=None, op0=op0)
        else:
            fn = lambda e: e.tensor_scalar(out=o_, in0=a_, scalar1=s1_, scalar2=s2_, op0=op0, op1=op1)
        return self.op(eng, fn, _bufs(in0, s1, s2), _bufs(out))

    def stt(self, out, in0, scalar, in1, op0, op1):
        o_, a_, s_, b_ = _ap(out), _ap(in0), _ap(scalar), _ap(in1)
        return self.op("dve", lambda e: e.scalar_tensor_tensor(out=o_, in0=a_, scalar=s_, in1=b_, op0=op0, op1=op1),
                       _bufs(in0, scalar, in1), _bufs(out))

    def copy(self, eng, out, in_):
        o_, i_ = _ap(out), _ap(in_)
        if eng == "act":
            fn = lambda e: e.copy(out=o_, in_=i_)
        else:
            fn = lambda e: e.tensor_copy(out=o_, in_=i_)
        return self.op(eng, fn, _bufs(in_), _bufs(out))

    def recip(self, out, in_):
        o_, i_ = _ap(out), _ap(in_)
        return self.op("dve", lambda e: e.reciprocal(out=o_, in_=i_), _bufs(in_), _bufs(out))

    def scan(self, out, d0, d1, init, op0, op1):
        o_, a_, b_, i_ = _ap(out), _ap(d0), _ap(d1), _ap(init)
        return self.op("dve", lambda e: e.tensor_tensor_scan(out=o_, data0=a_, data1=b_, initial=i_, op0=op0, op1=op1),
                       _bufs(d0, d1, init), _bufs(out))

    def memset(self, eng, out, val):
        o_ = _ap(out)
        return self.op(eng, lambda e: e.memset(o_, val), (), _bufs(out))

    def asel(self, out, in_, pattern, cmp, fill, base, cm):
        o_, i_ = _ap(out), _ap(in_)
        return self.op("pool", lambda e: e.affine_select(out=o_, in_=i_, pattern=pattern, compare_op=cmp, fill=fill,
                                                          base=base, channel_multiplier=cm), _bufs(in_), _bufs(out))

    def emit(self):
        nc = self.nc
        with contextlib.ExitStack() as st:
            esem = {e: st.enter_context(nc.semaphore("s_" + e)) for e in ENGS}
            dsem = {}
            for k in self.dma_cnt:
                dsem[k] = st.enter_context(nc.semaphore("d_%s_%d" % k))
            for e in ENGS:
                n = 0
                for o in self.ops[e]:
                    if o.needs_inc:
                        n += 1
                        o.seq = n
            final_waits = dict(self.dma_cnt)
            block = st.enter_context(nc.Block())

            def run(e, eng):
                waited = {}

                def wait(sem, key, val):
                    if waited.get(key, 0) >= val:
                        return
                    eng.wait_ge(sem, val)
                    waited[key] = val

                for o in self.ops[e]:
                    for p in o.deps:
                        if p.dma:
                            wait(dsem[p.sem], p.sem, p.val)
                        else:
                            wait(esem[p.eng], p.eng, p.seq)
                    if o.xw:
                        for k, v in o.xw.items():
                            wait(dsem[k], k, v)
                    if o.fn is None:
                        continue
                    if o.dma:
                        if o.prev_val:
                            wait(dsem[o.sem], o.sem, o.prev_val)
                        o.fn(eng).then_inc(dsem[o.sem], 16)
                    else:
                        ins = o.fn(eng)
                        if o.needs_inc:
                            ins.then_inc(esem[e], 1)
                if e == "sp":
                    for k, v in final_waits.items():
                        wait(dsem[k], k, v)

            @block.tensor
            def _(eng):
                run("pe", eng)

            @block.scalar
            def _(eng):
                run("act", eng)

            @block.vector
            def _(eng):
                run("dve", eng)

            @block.gpsimd
            def _(eng):
                run("pool", eng)

            @block.sync
            def _(eng):
                run("sp", eng)


class Arena:
    def __init__(self, t, nbytes):
        self.t = t
        self.nbytes = nbytes
        self.off = 0

    def alloc(self, cols, dtype=F32, name="", align=4):
        sz = cols * (2 if dtype == BF16 else 4)
        sz = (sz + 3) // 4 * 4
        self.off = (self.off + align - 1) // align * align
        a = self.off
        self.off += sz
        assert self.off <= self.nbytes, ("arena overflow", name, self.off, self.nbytes)
        ap = self.t[:, a // 4:(a + sz) // 4]
        if dtype == BF16:
            ap = ap.bitcast(BF16)[:, 0:cols]
        return Tl(ap, Buf(name))

    def bank(self, name=""):
        t = self.alloc(512, F32, name, align=2048)
        t.b.excl = True
        return t


class Rot:
    def __init__(self, items):
        self.items = items
        self.i = 0

    def next(self):
        x = self.items[self.i % len(self.items)]
        self.i += 1
        return x


def build(S, L, dbg=False):
    nc = bass.Bass("TRN2", target_bir_lowering=False)
    P = Prog(nc)
    NB = S // 128

    def din(name, shape):
        return nc.dram_tensor(name, shape, F32, kind="ExternalInput").ap()

    x_in = din("x", [S, D])
    norm_g = din("norm_g", [L, D])
    w_in = din("w_in", [L, D, IN_COLS])
    mu_a = din("mu_a", [L, A_COLS])
    w_up = din("w_up", [L, 64, RW])
    w0 = din("w0", [L, RW])
    a_up = din("a_up", [L, 64, RW])
    a0 = din("a0", [L, RW])
    k_k = din("k_k", [L, RW])
    k_a = din("k_a", [L, RW])
    r_k = din("r_k", [L, RW])
    gn_g = din("gn_g", [L, RW])
    gn_b = din("gn_b", [L, RW])
    pool_w = din("pool_w", [L, 4, 128, 128])
    pool_scale = din("pool_scale", [L, PW])
    qn_g = din("qn_g", [L, 64])
    kn_g = din("kn_g", [L, 64])
    w_out = din("w_out", [L, D, D])
    out = nc.dram_tensor("out", [S, D], F32, kind="ExternalOutput").ap()
    skind = "ExternalOutput" if dbg else "Internal"
    dbg_outs = {}

    def dump(name, tile, cols, dt=F32):
        if not dbg or name in dbg_outs:
            return
        dbg_outs[name] = nc.dram_tensor("dbg_" + name, [128, cols], dt, kind="ExternalOutput").ap()
        P.dma("sp", dbg_outs[name], tile)
    pj = nc.dram_tensor("pj", [IN_COLS, S], F32, kind=skind).ap()
    vtok = nc.dram_tensor("vtok", [S, SW], BF16, kind=skind).ap()
    ymix = nc.dram_tensor("ymix", [D, S], BF16, kind=skind).ap()

    with contextlib.ExitStack() as st:
        SB_BYTES = 206 * 1024
        sb_t = st.enter_context(nc.sbuf_tensor("arena", [128, SB_BYTES // 4], F32))
        ps_t = st.enter_context(nc.psum_tensor("psarena", [128, 4096], F32))
        sb = Arena(sb_t, SB_BYTES)
        ps = Arena(ps_t, 16384)

        identf = sb.alloc(128, F32, "identf")
        ident = sb.alloc(128, BF16, "ident")
        M2 = sb.alloc(256, F32, "M2")
        MLs = sb.alloc(128, F32, "MLs")
        MGE = sb.alloc(128, F32, "MGE")
        bones = sb.alloc(128, F32, "bones")
        zero1 = sb.alloc(1, F32, "zero1")
        epsr = sb.alloc(1, F32, "epsr")
        epsg = sb.alloc(1, F32, "epsg")
        TPB = min(512, S)
        reset = sb.alloc(TPB, F32, "reset")
        prm = sb.alloc(80, F32, "prm")
        qgs = sb.alloc(1, F32, "qgs")

        P.memset("pool", identf, 1.0)
        P.asel(identf, identf, [[-1, 128]], ALU.is_equal, 0.0, 0, 1)
        P.copy("dve", ident, identf)
        P.memset("pool", M2, 1.0)
        P.asel(M2[:, 0:128], M2[:, 0:128], [[1, 128]], ALU.is_gt, 0.0, 0, -1)
        P.asel(M2[:, 128:256], M2[:, 128:256], [[1, 128]], ALU.is_ge, 0.0, 0, -1)
        P.memset("pool", MLs, 1.0)
        P.asel(MLs, MLs, [[-1, 128]], ALU.is_gt, 0.0, 0, 1)
        P.ts("dve", MGE, MLs, -1.0, 1.0, ALU.mult, ALU.add)
        P.memset("dve", bones, 0.0)
        P.memset("dve", bones[0:64, 0:64], 1.0)
        P.memset("dve", bones[64:128, 64:128], 1.0)
        P.memset("dve", zero1, 0.0)
        P.memset("dve", epsr, RMS_EPS)
        P.memset("dve", epsg, GN_EPS)
        P.memset("dve", reset, 1.0)
        P.memset("dve", reset.re("p (c n) -> p c n", n=128)[:, :, 0:1], 0.0)
        const_off = sb.off

        PM_MU, PM_W0, PM_A0, PM_KK, PM_KA, PM_RK, PM_GG, PM_GB, PM_PS, PM_QN, PM_KN = 0, 25, 31, 37, 43, 49, 55, 61, 67, 71, 72

        def load_params(l):
            sb.off = const_off
            ps.off = 0
            stg = sb.alloc(128, F32, "prm_stage")
            P.memset("dve", stg, 0.0)
            stg_w = stg

            def ld(row0, src, n):
                P.dma("sp", stg_w[row0:row0 + n, :], src)
            ld(PM_MU, mu_a[l].rearrange("(t p) -> t p", p=128), 25)
            for r0, src in ((PM_W0, w0), (PM_A0, a0), (PM_KK, k_k), (PM_KA, k_a), (PM_RK, r_k), (PM_GG, gn_g),
                            (PM_GB, gn_b)):
                ld(r0, src[l].rearrange("(t p) -> t p", p=128), 6)
            ld(PM_PS, pool_scale[l].rearrange("(t p) -> t p", p=128), 4)
            P.dma("sp", stg_w[PM_QN:PM_QN + 1, 0:64], qn_g[l:l + 1, :])
            P.dma("sp", stg_w[PM_QN:PM_QN + 1, 64:128], qn_g[l:l + 1, :])
            P.dma("sp", stg_w[PM_KN:PM_KN + 1, 0:64], kn_g[l:l + 1, :])
            P.dma("sp", stg_w[PM_KN:PM_KN + 1, 64:128], kn_g[l:l + 1, :])
            pt = ps.bank("prm_ps")
            P.mm(pt[:, 0:73], lhsT=stg[0:73, :], rhs=identf[0:73, 0:73])
            P.copy("dve", prm[:, 0:73], pt[:, 0:73])
            P.ts("dve", qgs, prm[:, PM_QN:PM_QN + 1], 0.125, None, ALU.mult)

        def phase_A(l, xsrc):
            sb.off = const_off
            ps.off = 0
            TOKP = min(2048, S)
            npass = S // TOKP
            nb = TOKP // 128
            gbc = sb.alloc(D, F32, "gbc")
            P.dma("sp", gbc, norm_g[l].partition_broadcast(128))
            hT = sb.alloc(16 * TOKP, BF16, "hT")
            hT3 = hT.re("p (k t) -> p k t", t=TOKP)
            hbufs = [Buf("hT%d" % i) for i in range(TOKP // 512)]
            xrot = Rot([sb.alloc(D, F32, "x%d" % i) for i in range(2)])
            hrot = Rot([sb.alloc(D, BF16, "h%d" % i) for i in range(2)])
            junk = sb.alloc(D, BF16, "junk")
            ssr = Rot([sb.alloc(1, F32, "ss%d" % i) for i in range(4)])
            rsr = Rot([sb.alloc(1, F32, "rs%d" % i) for i in range(4)])
            CG = 384
            wfrot = Rot([sb.alloc(16 * CG, F32, "wf%d" % i) for i in range(2)])
            wbrot = Rot([sb.alloc(16 * CG, BF16, "wb%d" % i) for i in range(2)])
            ostrot = Rot([sb.alloc(512, F32, "ost%d" % i) for i in range(3)])
            vstrot = Rot([sb.alloc(CG, BF16, "vst%d" % i) for i in range(3)])
            trps = Rot([ps.bank("trp%d" % i) for i in range(2)])
            mops = Rot([ps.bank("mo%d" % i) for i in range(4)])
            for pa in range(npass):
                t0 = pa * TOKP
                for tb in range(nb):
                    xt = xrot.next()
                    P.dma("sp", xt, xsrc[t0 + tb * 128:t0 + (tb + 1) * 128, :])
                    ss = ssr.next()
                    rs = rsr.next()
                    P.memset("pool", ss, 0.0)
                    P.act(junk, xt, AF.Square, accum=ss)
                    P.act(rs, ss, AF.Sqrt, bias=epsr, scale=1.0 / D)
                    P.recip(rs, rs)
                    hb = hrot.next()
                    P.stt(hb, xt, rs, gbc, ALU.mult, ALU.mult)
                    hbuf = hbufs[tb // 4]
                    for q in range(4):
                        tp = trps.next()
                        tpb = Tl(tp.ap.bitcast(BF16)[:, 0:512], tp.b)
                        for j in range(4):
                            kc = 4 * q + j
                            P.tr(tpb[:, j * 128:(j + 1) * 128], hb[:, kc * 128:(kc + 1) * 128], ident)
                        dst = Tl(hT3.ap[:, 4 * q:4 * q + 4, tb * 128:(tb + 1) * 128], hbuf)
                        P.copy("act" if q % 2 == 0 else "dve", dst, tpb.re("p (j t) -> p j t", t=128))
                for cg in range(IN_COLS // CG):
                    wf = wfrot.next()
                    wsrc = w_in[l, :, cg * CG:(cg + 1) * CG].rearrange("(k p) c -> p k c", p=128)
                    wf3 = wf.re("p (k c) -> p k c", c=CG)
                    P.dma("sp", wf3[:, 0:8, :], wsrc[:, 0:8, :])
                    P.dma("pool", wf3[:, 8:16, :], wsrc[:, 8:16, :])
                    wb = wbrot.next()
                    wb3 = wb.re("p (k c) -> p k c", c=CG)
                    P.copy("pool", wb3[:, 0:6, :], wf3[:, 0:6, :])
                    P.copy("dve", wb3[:, 6:16, :], wf3[:, 6:16, :])
                    if cg not in (15, 16):
                        for ci in range(3):
                            ct = cg * 3 + ci
                            for tc in range(TOKP // 512):
                                po = mops.next()
                                for k in range(16):
                                    rhs = Tl(hT3.ap[:, k, tc * 512:(tc + 1) * 512], hbufs[tc])
                                    P.mm(po, lhsT=wb3[:, k, ci * 128:(ci + 1) * 128], rhs=rhs, start=(k == 0),
                                         stop=(k == 15))
                                ost = ostrot.next()
                                P.copy("act", ost, po)
                                P.dma("act", pj[ct * 128:(ct + 1) * 128, t0 + tc * 512:t0 + (tc + 1) * 512], ost)
                    else:
                        for tb in range(nb):
                            po = mops.next()
                            for k in range(16):
                                lhsT = Tl(hT3.ap[:, k, tb * 128:(tb + 1) * 128], hbufs[tb // 4])
                                P.mm(po[:, 0:CG], lhsT=lhsT, rhs=wb3[:, k, :], start=(k == 0), stop=(k == 15))
                            vst = vstrot.next()
                            P.copy("act", vst, po[:, 0:CG])
                            P.dma("act", vtok[t0 + tb * 128:t0 + (tb + 1) * 128, (cg - 15) * CG:(cg - 14) * CG], vst)

        def phase_B(l):
            sb.off = const_off
            ps.off = 0
            TP = TPB
            NCH = TP // 128
            npiece = S // TP
            NIT = 2 * NCH
            wup = sb.alloc(RW, F32, "wup")
            P.dma("sp", wup[0:64, :], w_up[l])
            P.dma("sp", wup[64:128, :], a_up[l])
            Sf = [[sb.alloc(64, F32, "Sf%d_%d" % (hp, i)) for i in range(2)] for hp in range(6)]
            Sb = [[sb.alloc(64, BF16, "Sb%d_%d" % (hp, i)) for i in range(2)] for hp in range(6)]
            cur = [0] * 6
            for hp in range(6):
                P.memset("dve", Sf[hp][0], 0.0)
                P.memset("dve", Sb[hp][0], 0.0)

            def f32t(name, n=TP):
                return sb.alloc(n, F32, name)

            def bft(name, n=TP):
                return sb.alloc(n, BF16, name)
            lda, wa, wa2 = f32t("lda", TP + 1), f32t("wa"), f32t("wa2")
            ldr = Rot([[f32t("ld%s%d" % (nm, i), TP + 1) for nm in "rkvg"] for i in range(2)])
            tmpd = f32t("tmpd")
            rs_, ks_, vs_, gs_ = f32t("rs"), f32t("ks"), f32t("vs"), f32t("gs")
            sig, a_, cs, Winv, csm, Wm, e2, E2 = (f32t(n) for n in ("sig", "a", "cs", "Winv", "csm", "Wm", "e2", "E2"))
            kk, kk2, nrm, kkn, tmpk, kp, b_ = (f32t(n) for n in ("kk", "kk2", "nrm", "kkn", "tmpk", "kp", "b"))
            rkr, sgt = f32t("rkr"), f32t("sgt")
            yc, sq, rstd, yn = f32t("yc"), f32t("sq"), f32t("rstd"), f32t("yn")
            KH, BH = bft("KH"), bft("BH")
            sets = []
            for i in range(3):
                d = {"AR": bft("AR%d" % i, 2 * TP), "KB": bft("KB%d" % i), "BB": bft("BB%d" % i), "VB": bft("VB%d" % i),
                     "W": f32t("W%d" % i), "bon": f32t("bon%d" % i), "sg": f32t("sg%d" % i), "ysb": f32t("ysb%d" % i),
                     "SA1": [bft("SA1_%d_%d" % (i, j), 256) for j in range(NIT)],
                     "SA2": [bft("SA2_%d_%d" % (i, j), 256) for j in range(NIT)],
                     "TT": [bft("TT_%d_%d" % (i, j), 128) for j in range(NIT)]}
                d["AR4"] = d["AR"].re("p (c w n) -> p c w n", w=2, n=128)
                sets.append(d)
            yarot = Rot([bft("ya%d" % i) for i in range(2)])
            TOKr = Rot([bft("TOK%d" % i, 384) for i in range(2)])
            Xr = [Rot([bft("X%d_%d" % (i, j), 128) for j in range(2)]) for i in range(NIT)]
            XTr = [Rot([bft("XT%d_%d" % (i, j), 128) for j in range(2)]) for i in range(NIT)]
            Pr = [Rot([bft("P%d_%d" % (i, j), 128) for j in range(2)]) for i in range(NIT)]
            Zbr = Rot([bft("Zb%d" % i, 64) for i in range(2)])
            Ubr = Rot([bft("Ub%d" % i, 64) for i in range(2)])
            pgen = ps.bank("pgenP")
            pgc = ps.bank("pgenC")
            pxb = [ps.bank("pxb%d" % i) for i in range(3)]
            ppb = [ps.bank("ppb%d" % i) for i in range(2)]
            pmisc = ps.bank("pmisc")
            ptr = Tl(pmisc.ap.bitcast(BF16)[:, 0:384], pmisc.b)
            pzr = Rot([Tl(pmisc.ap[:, 192:256], pmisc.b), Tl(pmisc.ap[:, 256:320], pmisc.b)])
            pur = Rot([Tl(pmisc.ap[:, 320:384], pmisc.b), Tl(pmisc.ap[:, 384:448], pmisc.b)])
            psn = Tl(pmisc.ap[:, 448:512], pmisc.b)
            items = [(c, h) for c in range(NCH) for h in range(2)]
            TPv = "p (c n) -> p c n"

            def px_tiles(it):
                bk = pxb[(it // 2) % 3]
                o = (it % 2) * 256
                return bk[:, o:o + 128], bk[:, o + 128:o + 256]

            def prepA(pc, hp, st):
                t0 = pc * TP
                AR, AR4, KB, BB, VB, W, bon, sg = (st[k] for k in ("AR", "AR4", "KB", "BB", "VB", "W", "bon", "sg"))

                def load_shift(dst, row0, eng="sp"):
                    if t0 > 0:
                        P.dma(eng, dst, pj[row0:row0 + 128, t0 - 1:t0 + TP])
                    else:
                        P.memset("pool", dst[:, 0:1], 0.0)
                        P.dma(eng, dst[:, 1:TP + 1], pj[row0:row0 + 128, 0:TP])

                def lerp(dst, ld, mucol):
                    P.tt("pool", tmpd, ld[:, 0:TP], ld[:, 1:TP + 1], ALU.subtract)
                    P.stt(dst, tmpd, prm[:, mucol:mucol + 1], ld[:, 1:TP + 1], ALU.mult, ALU.add)
                if hp == 0:
                    load_shift(lda, 3072)
                    lerp(wa, lda, PM_MU + 24)
                    P.act(wa2[0:64, :], wa[0:64, :], AF.Tanh)
                    P.copy("pool", wa2[64:128, :], wa[64:128, :])
                lds = ldr.next()
                for i, (ld, r0) in enumerate(zip(lds, (0, 768, 1536, 2304))):
                    load_shift(ld, r0 + hp * 128, "sp" if i % 2 == 0 else "act")
                lerp(rs_, lds[0], PM_MU + hp)
                lerp(ks_, lds[1], PM_MU + 6 + hp)
                lerp(vs_, lds[2], PM_MU + 12 + hp)
                lerp(gs_, lds[3], PM_MU + 18 + hp)
                P.mm(pgen[:, 0:TP], lhsT=wup[0:64, hp * 128:(hp + 1) * 128], rhs=wa2[0:64, :])
                P.act(sig, pgen[:, 0:TP], AF.Sigmoid, bias=prm[:, PM_W0 + hp:PM_W0 + hp + 1])
                P.mm(pgen[:, 0:TP], lhsT=wup[64:128, hp * 128:(hp + 1) * 128], rhs=wa2[64:128, :])
                P.act(a_, pgen[:, 0:TP], AF.Sigmoid, bias=prm[:, PM_A0 + hp:PM_A0 + hp + 1])
                P.act(sgt, gs_, AF.Sigmoid)
                P.tt("pool", sg, sgt, gs_, ALU.mult)
                P.scan(cs, reset, sig, 0.0, ALU.mult, ALU.add)
                P.act(W, cs, AF.Exp, scale=-CDEC)
                P.act(Winv, cs, AF.Exp, scale=CDEC)
                P.tt("pool", csm, cs, sig, ALU.subtract)
                P.act(Wm, csm, AF.Exp, scale=-CDEC)
                cs3 = cs.re(TPv, n=128)
                P.tt("dve", e2.re(TPv, n=128), cs3[:, :, 127:128].bc([128, NCH, 128]), cs3, ALU.subtract)
                P.act(E2, e2, AF.Exp, scale=-CDEC)
                P.ts("dve", kk, ks_, prm[:, PM_KK + hp:PM_KK + hp + 1], None, ALU.mult)
                P.act(kk2, kk, AF.Square)
                return {"st": st, "hp": hp, "t0": t0, "X": {}, "XT": {}, "P": {}}

            def prepB(ctx):
                st, hp = ctx["st"], ctx["hp"]
                AR, AR4, KB, BB, VB, W, bon, sg = (st[k] for k in ("AR", "AR4", "KB", "BB", "VB", "W", "bon", "sg"))
                P.mm(pgen[:, 0:TP], lhsT=bones, rhs=kk2)
                P.act(nrm, pgen[:, 0:TP], AF.Sqrt)
                P.ts("dve", nrm, nrm, 1e-12, None, ALU.max)
                P.recip(nrm, nrm)
                P.tt("dve", kkn, kk, nrm, ALU.mult)
                P.ts("dve", tmpk, a_, -1.0, prm[:, PM_KA + hp:PM_KA + hp + 1], ALU.add, ALU.mult)
                P.stt(kp, tmpk, 1.0, ks_, ALU.add, ALU.mult)
                P.tt("dve", b_, kkn, a_, ALU.mult)
                P.stt(AR4[:, :, 0, :], kkn.re(TPv, n=128), -1.0, Wm.re(TPv, n=128), ALU.mult, ALU.mult)
                P.tt("pool", AR4[:, :, 1, :], rs_.re(TPv, n=128), W.re(TPv, n=128), ALU.mult)
                P.tt("dve", KH, kp, Winv, ALU.mult)
                P.tt("dve", BH, b_, Winv, ALU.mult)
                P.tt("dve", KB, kp, E2, ALU.mult)
                P.tt("pool", BB, b_, E2, ALU.mult)
                P.copy("act", VB, vs_)
                P.stt(rkr, rs_, prm[:, PM_RK + hp:PM_RK + hp + 1], kp, ALU.mult, ALU.mult)

            def prepC(ctx):
                st, hp = ctx["st"], ctx["hp"]
                AR, AR4, KB, BB, VB, W, bon, sg = (st[k] for k in ("AR", "AR4", "KB", "BB", "VB", "W", "bon", "sg"))
                SA1, SA2 = st["SA1"], st["SA2"]
                P.mm(pgen[:, 0:TP], lhsT=bones, rhs=rkr)
                P.tt("dve", bon, pgen[:, 0:TP], vs_, ALU.mult)
                for it, (c, h) in enumerate(items):
                    sl = slice(64 * h, 64 * h + 64)
                    csl = slice(c * 128, (c + 1) * 128)
                    pscb = pxb[it % 3]
                    ps1 = pscb[:, 0:256]
                    ps2 = pscb[:, 256:512]
                    arc = Tl(AR4.ap[sl, c].rearrange("p w n -> p (w n)"), AR.b)
                    P.mm(ps1, lhsT=BH[sl, csl], rhs=arc)
                    P.mm(ps2, lhsT=KH[sl, csl], rhs=arc)
                    ps3 = ppb[it // 4][:, (it % 4) * 128:(it % 4 + 1) * 128]
                    P.mm(ps3, lhsT=AR4[sl, c, 0, :], rhs=BH[sl, csl])
                    P.tt("dve", SA1[it], ps1, M2, ALU.mult)
                    P.tt("dve", SA2[it], ps2, M2, ALU.mult)
                    xt0 = XTr[it].next()
                    P.tt("dve", xt0, ps3, MLs, ALU.mult)
                    p0 = Pr[it].next()
                    P.tt("pool", p0, SA1[it][:, 0:128], ident, ALU.add)
                    ctx["X"][it], ctx["XT"][it], ctx["P"][it] = SA1[it][:, 0:128], xt0, p0

            def level(ctx, k):
                Xs, XTs, Ps, st = ctx["X"], ctx["XT"], ctx["P"], ctx["st"]
                for half in range(0, NIT, 4):
                    its = list(range(half, min(half + 4, NIT)))
                    pend = {}
                    for it in its:
                        pp = ppb[it // 4][:, (it % 4) * 128:(it % 4 + 1) * 128]
                        px, pxt = px_tiles(it)
                        if k >= 1:
                            P.mm(pp, lhsT=XTs[it], rhs=Ps[it])
                        if k < 6:
                            P.mm(px, lhsT=XTs[it], rhs=Xs[it])
                            P.mm(pxt, lhsT=Xs[it], rhs=XTs[it])
                        pend[it] = (pp, px, pxt)
                    for it in its:
                        pp, px, pxt = pend[it]
                        if k >= 1:
                            pn_ = st["TT"][it] if k == 6 else Pr[it].next()
                            P.tt("dve", pn_, pp, Ps[it], ALU.add)
                            Ps[it] = pn_
                        if k < 6:
                            xn = Xr[it].next()
                            P.copy("act", xn, px)
                            xtn = XTr[it].next()
                            P.copy("act", xtn, pxt)
                            Xs[it], XTs[it] = xn, xtn

            def chain_step(ctx, c):
                st, hp = ctx["st"], ctx["hp"]
                AR4, KB, BB, VB, W, ysb = (st[k] for k in ("AR4", "KB", "BB", "VB", "W", "ysb"))
                SA1, SA2, Ps = st["SA1"], st["SA2"], ctx["P"]
                csl = slice(c * 128, (c + 1) * 128)
                TOK = TOKr.next()
                P.tr(ptr[:, 0:128], VB[:, csl], ident)
                P.tr(ptr[:, 128:256], KB[:, csl], ident)
                P.tr(ptr[:, 256:384], BB[:, csl], ident)
                P.copy("act", TOK, ptr)
                sfo, sbo = Sf[hp][cur[hp]], Sb[hp][cur[hp]]
                sfn, sbn = Sf[hp][1 - cur[hp]], Sb[hp][1 - cur[hp]]
                hv = []
                for h in range(2):
                    it = items.index((c, h))
                    sl = slice(64 * h, 64 * h + 64)
                    hv.append((it, sl, TOK[:, 64 * h:64 * h + 64], TOK[:, 128 + 64 * h:128 + 64 * h + 64],
                               TOK[:, 256 + 64 * h:256 + 64 * h + 64]))
                pzs, zbs, pus, ubs = [], [], [], []
                for (it, sl, vt, kbt, bbt) in hv:
                    pz = pzr.next()
                    P.mm(pz, lhsT=AR4[sl, c, 0, :], rhs=sbo[sl, :], start=True, stop=False)
                    P.mm(pz, lhsT=SA2[it][:, 0:128], rhs=vt, start=False, stop=True)
                    pzs.append(pz)
                for pz in pzs:
                    zb = Zbr.next()
                    P.copy("act", zb, pz)
                    zbs.append(zb)
                for (it, sl, vt, kbt, bbt), zb in zip(hv, zbs):
                    pu = pur.next()
                    P.mm(pu, lhsT=Ps[it], rhs=zb)
                    pus.append(pu)
                for pu in pus:
                    ub = Ubr.next()
                    P.copy("act", ub, pu)
                    ubs.append(ub)
                for (it, sl, vt, kbt, bbt), ub in zip(hv, ubs):
                    P.mm(pgc[sl, 0:128], lhsT=sbo[sl, :], rhs=AR4[sl, c, 1, :], start=True, stop=False)
                    P.mm(pgc[sl, 0:128], lhsT=ub, rhs=SA1[it][:, 128:256], start=False, stop=False)
                    P.mm(pgc[sl, 0:128], lhsT=vt, rhs=SA2[it][:, 128:256], start=False, stop=True)
                for (it, sl, vt, kbt, bbt), ub in zip(hv, ubs):
                    P.mm(psn[sl, :], lhsT=bbt, rhs=ub, start=True, stop=False)
                    P.mm(psn[sl, :], lhsT=kbt, rhs=vt, start=False, stop=True)
                P.copy("act", ysb[:, csl], pgc[:, 0:128])
                wc = W[:, c * 128 + 127:c * 128 + 128]
                P.stt(sfn, sfo, wc, psn, ALU.mult, ALU.add)
                P.stt(sbn, sfo, wc, psn, ALU.mult, ALU.add)
                cur[hp] = 1 - cur[hp]

            def post(ctx):
                st, hp, t0 = ctx["st"], ctx["hp"], ctx["t0"]
                ysb, bon, sg = st["ysb"], st["bon"], st["sg"]
                P.mm(pgc[:, 0:TP], lhsT=bones, rhs=ysb)
                P.stt(yc, pgc[:, 0:TP], -1.0 / 64, ysb, ALU.mult, ALU.add)
                P.act(sq, yc, AF.Square)
                P.mm(pgc[:, 0:TP], lhsT=bones, rhs=sq)
                P.act(rstd, pgc[:, 0:TP], AF.Sqrt, bias=epsg, scale=1.0 / 64)
                P.recip(rstd, rstd)
                P.tt("pool", yn, yc, rstd, ALU.mult)
                P.ts("dve", yn, yn, prm[:, PM_GG + hp:PM_GG + hp + 1], prm[:, PM_GB + hp:PM_GB + hp + 1], ALU.mult,
                     ALU.add)
                P.tt("pool", yn, yn, bon, ALU.add)
                ya = yarot.next()
                P.tt("dve", ya, yn, sg, ALU.mult)
                P.dma("sp", ymix[hp * 128:(hp + 1) * 128, t0:t0 + TP], ya)

            units = [(pc, hp) for pc in range(npiece) for hp in range(6)]

            def start_unit(ui):
                pc, hp = units[ui]
                return prepA(pc, hp, sets[ui % 3])
            ctxs = {0: start_unit(0)}
            prepB(ctxs[0])
            prepC(ctxs[0])
            for ui in range(len(units)):
                ctx = ctxs[ui]
                prev = ctxs.get(ui - 1)
                nxt = None
                for k in range(7):
                    level(ctx, k)
                    if k == 0 and ui + 1 < len(units):
                        nxt = ctxs[ui + 1] = start_unit(ui + 1)
                    if prev is not None and 1 <= k <= 4 and (k - 1) < NCH:
                        chain_step(prev, k - 1)
                    if k == 2 and nxt is not None:
                        prepB(nxt)
                    if k == 5 and prev is not None:
                        for c in range(4, NCH):
                            chain_step(prev, c)
                        post(prev)
                    if k == 6 and nxt is not None:
                        prepC(nxt)
                ctxs.pop(ui - 1, None)
            last = ctxs[len(units) - 1]
            for c in range(NCH):
                chain_step(last, c)
            post(last)

        def phase_C(l):
            sb.off = const_off
            ps.off = 0
            TP = min(1024, S)
            npiece = S // TP
            H = 16
            pwf = sb.alloc(512, F32, "pwf")
            pwb = sb.alloc(512, BF16, "pwb")
            P.dma("sp", pwf.re("p (g d) -> p g d", d=128), pool_w[l].rearrange("g c d -> c g d"))
            P.copy("dve", pwb, pwf)
            corr = sb.alloc(4 * 16, F32, "corr")
            for g, win in enumerate((2, 4, 8, 16)):
                for t in range(16):
                    P.memset("pool", corr[:, g * 16 + t:g * 16 + t + 1], float(win) / min(t + 1, win))
            urot = Rot([sb.alloc(TP + H, F32, "pu%d" % i) for i in range(2)])
            grot = Rot([sb.alloc(TP, F32, "pg%d" % i) for i in range(2)])
            sA = sb.alloc(TP + H, F32, "sA")
            sB = sb.alloc(TP + H, F32, "sB")
            dbf = sb.alloc(TP, BF16, "dbf")
            sgp = sb.alloc(TP, F32, "sgp")
            ybr = Rot([sb.alloc(TP, BF16, "yb%d" % i) for i in range(2)])
            pyr = Rot([ps.bank("pc%d" % i) for i in range(2)])
            for pc in range(npiece):
                t0 = pc * TP
                for g, win in enumerate((2, 4, 8, 16)):
                    u = urot.next()
                    row = A_COLS + g * 128
                    if t0 > 0:
                        P.dma("sp", u, pj[row:row + 128, t0 - H:t0 + TP])
                    else:
                        P.memset("pool", u[:, 0:H], 0.0)
                        P.dma("sp", u[:, H:H + TP], pj[row:row + 128, 0:TP])
                    pg_ = grot.next()
                    P.dma("act", pg_, pj[A_COLS + PW + g * 128:A_COLS + PW + (g + 1) * 128, t0:t0 + TP])
                    src = u
                    sh = 1
                    dsts = [sA, sB]
                    lo = 0
                    for lev in range(g + 1):
                        dst = dsts[lev % 2]
                        lo += sh
                        P.tt("pool" if lev % 2 else "dve", dst[:, lo:TP + H], src[:, lo:TP + H], src[:, lo - sh:TP + H - sh],
                             ALU.add)
                        src = dst
                        sh *= 2
                    ssum = src
                    if t0 == 0:
                        P.tt("dve", ssum[:, H:H + 16], ssum[:, H:H + 16], corr[:, g * 16:(g + 1) * 16], ALU.mult)
                    P.stt(dbf, ssum[:, H:H + TP], 1.0 / win, u[:, H:H + TP], ALU.mult, ALU.subtract)
                    P.act(sgp, pg_, AF.Silu)
                    yb = ybr.next()
                    for hf in range(TP // 512):
                        py = pyr.next()
                        P.mm(py, lhsT=pwb[:, g * 128:(g + 1) * 128], rhs=dbf[:, hf * 512:(hf + 1) * 512])
                        P.stt(yb[:, hf * 512:(hf + 1) * 512], py, prm[:, PM_PS + g:PM_PS + g + 1],
                              sgp[:, hf * 512:(hf + 1) * 512], ALU.mult, ALU.mult)
                    P.dma("sp", ymix[RW + g * 128:RW + (g + 1) * 128, t0:t0 + TP], yb)

        def phase_D(l):
            sb.off = const_off
            ps.off = 0
            TPq = min(1024, S)
            QH = sb.alloc(S, BF16, "QH")
            KHt = sb.alloc(S, BF16, "KHt")
            V = sb.alloc(NB * 128, BF16, "V")
            V3 = V.re("p (n c) -> p n c", c=128)
            sgc = sb.alloc(S, BF16, "sgc")
            ycst = sb.alloc(S, BF16, "ycst")
            SGr = Rot([sb.alloc(S, F32, "SG%d" % i) for i in range(2)])
            EZr = Rot([sb.alloc(S, BF16, "EZ%d" % i) for i in range(2)])
            Pn = sb.alloc(S, F32, "Pn")
            Wtr = Rot([sb.alloc(S, BF16, "Wt%d" % i) for i in range(2)])
            WTr = Rot([sb.alloc(S, BF16, "WT%d" % i) for i in range(2)])
            qf = Rot([sb.alloc(TPq, F32, "qf%d" % i) for i in range(2)])
            sqq = sb.alloc(TPq, F32, "sqq")
            rt = sb.alloc(TPq, F32, "rt")
            zr = Rot([ps.bank("z%d" % i) for i in range(3)])
            ptwr = Rot([ps.bank("ptw%d" % i) for i in range(2)])
            por = Rot([ps.bank("po%d" % i) for i in range(2)])
            pgen = ps.bank("pgenD")
            for hp in range(6):
                P.dma("sp", V3, vtok[:, hp * 128:(hp + 1) * 128].rearrange("(n p) c -> p n c", p=128))
                for (dst, row0, gcol) in ((QH, C0 + hp * 128, qgs), (KHt, C0 + SW + hp * 128, prm[:, PM_KN:PM_KN + 1])):
                    for pc in range(S // TPq):
                        q = qf.next()
                        P.dma("sp", q, pj[row0:row0 + 128, pc * TPq:(pc + 1) * TPq])
                        P.act(sqq, q, AF.Square)
                        for hf in range(TPq // 512):
                            hs = slice(hf * 512, (hf + 1) * 512)
                            P.mm(pgen, lhsT=bones, rhs=sqq[:, hs])
                            P.act(rt[:, hs], pgen, AF.Sqrt, bias=epsr, scale=1.0 / 64)
                        P.recip(rt, rt)
                        P.stt(dst[:, pc * TPq:(pc + 1) * TPq], q, gcol, rt, ALU.mult, ALU.mult)
                for pc in range(S // TPq):
                    q = qf.next()
                    r0 = C0 + 3 * SW + hp * 128
                    P.dma("act", q, pj[r0:r0 + 128, pc * TPq:(pc + 1) * TPq])
                    P.act(sgc[:, pc * TPq:(pc + 1) * TPq], q, AF.Silu)
                items = [(T, h) for T in range(NB) for h in range(2)]
                state = {}

                def stage1(T, h):
                    sl = slice(64 * h, 64 * h + 64)
                    kend = (T + 1) * 128
                    SG = SGr.next()
                    EZ = EZr.next()
                    for ck in range((kend + 511) // 512):
                        w_ = min(512, kend - ck * 512)
                        pz = zr.next()
                        P.mm(pz[:, 0:w_], lhsT=QH[sl, T * 128:(T + 1) * 128], rhs=KHt[sl, ck * 512:ck * 512 + w_])
                        P.act(SG[:, ck * 512:ck * 512 + w_], pz[:, 0:w_], AF.Sigmoid, scale=-1.0)
                        P.act(EZ[:, ck * 512:ck * 512 + w_], pz[:, 0:w_], AF.Sigmoid)
                    dsl = slice(T * 128, kend)
                    P.tt("pool", SG[:, dsl], SG[:, dsl], MLs, ALU.mult)
                    P.tt("pool", SG[:, dsl], SG[:, dsl], MGE, ALU.add)
                    P.tt("pool", EZ[:, dsl], EZ[:, dsl], MLs, ALU.mult)
                    P.scan(Pn[:, 0:kend][:, ::-1], SG[:, 0:kend][:, ::-1], zero1.bc([128, kend]), 1.0, ALU.mult, ALU.add)
                    Wt = Wtr.next()
                    npl = ((kend - 1) * 3 // 8) // 64 * 64
                    if npl > 0:
                        P.tt("pool", Wt[:, 0:npl], EZ[:, 0:npl], Pn[:, 1:npl + 1], ALU.mult)
                    P.tt("dve", Wt[:, npl:kend - 1], EZ[:, npl:kend - 1], Pn[:, npl + 1:kend], ALU.mult)
                    P.memset("pool", Wt[:, kend - 1:kend], 0.0)
                    state[(T, h)] = Wt

                def stage2(T, h):
                    sl = slice(64 * h, 64 * h + 64)
                    Wt = state.pop((T, h))
                    WT = WTr.next()
                    nsb = T + 1
                    for g0 in range(0, nsb, 4):
                        n = min(4, nsb - g0)
                        pt = ptwr.next()
                        ptb = Tl(pt.ap.bitcast(BF16)[:, 0:512], pt.b)
                        for j in range(n):
                            P.tr(ptb[:, j * 128:(j + 1) * 128], Wt[:, (g0 + j) * 128:(g0 + j + 1) * 128], ident)
                        P.copy("act", WT[:, g0 * 128:(g0 + n) * 128], ptb[:, 0:n * 128])
                    if h == 0:
                        state["po"] = por.next()
                    po = state["po"]
                    for sbk in range(nsb):
                        P.mm(po[sl, 0:128], lhsT=V3[:, sbk, 64 * h:64 * h + 64], rhs=WT[:, sbk * 128:(sbk + 1) * 128],
                             start=(sbk == 0), stop=(sbk == nsb - 1))
                    if h == 1:
                        P.tt("dve", ycst[:, T * 128:(T + 1) * 128], po[:, 0:128], sgc[:, T * 128:(T + 1) * 128], ALU.mult)

                for i in range(len(items) + 1):
                    if i < len(items):
                        stage1(*items[i])
                    if i >= 1:
                        stage2(*items[i - 1])
                P.dma("sp", ymix[RW + PW + hp * 128:RW + PW + (hp + 1) * 128, :], ycst)

        def phase_E(l, xsrc):
            sb.off = const_off
            ps.off = 0
            wo = sb.alloc(16 * D, BF16, "wo")
            wo3 = wo.re("p (k c) -> p k c", c=D)
            wst = Rot([sb.alloc(4 * D, F32, "wst%d" % i) for i in range(2)])
            for kq in range(4):
                w = wst.next()
                w3 = w.re("p (k c) -> p k c", c=D)
                P.dma("sp" if kq % 2 == 0 else "act", w3,
                      w_out[l, kq * 512:(kq + 1) * 512, :].rearrange("(k p) c -> p k c", p=128))
                P.copy("pool", wo3[:, kq * 4:kq * 4 + 2, :], w3[:, 0:2, :])
                P.copy("dve", wo3[:, kq * 4 + 2:kq * 4 + 4, :], w3[:, 2:4, :])
            TQ = 512
            yr = Rot([sb.alloc(16 * TQ, BF16, "ymx%d" % i) for i in range(2)])
            xr = Rot([sb.alloc(D, F32, "xe%d" % i) for i in range(3)])
            pr = Rot([ps.bank("pe%d" % i) for i in range(8)])
            for tq in range(S // TQ):
                ym = yr.next()
                ym3 = ym.re("p (k t) -> p k t", t=TQ)
                P.dma("sp", ym3, ymix[:, tq * TQ:(tq + 1) * TQ].rearrange("(k p) t -> p k t", p=128))
                for tb in range(TQ // 128):
                    r0 = tq * TQ + tb * 128
                    xe = xr.next()
                    P.dma("sp", xe, xsrc[r0:r0 + 128, :])
                    for cq in range(4):
                        po = pr.next()
                        for k in range(16):
                            P.mm(po, lhsT=ym3[:, k, tb * 128:(tb + 1) * 128], rhs=wo3[:, k, cq * 512:(cq + 1) * 512],
                                 start=(k == 0), stop=(k == 15))
                        P.tt("dve", xe[:, cq * 512:(cq + 1) * 512], po, xe[:, cq * 512:(cq + 1) * 512], ALU.add)
                    P.dma("act", out[r0:r0 + 128, :], xe)

        for l in range(L):
            xsrc = x_in if l == 0 else out
            load_params(l)
            P.barrier()
            phase_A(l, xsrc)
            P.barrier()
            phase_B(l)
            P.barrier()
            phase_C(l)
            P.barrier()
            phase_D(l)
            P.barrier()
            phase_E(l, xsrc)
            P.barrier()
        P.emit()
    import os
    if os.environ.get("KSTATS"):
        print("op counts", {e: len(P.ops[e]) for e in ENGS}, flush=True)
    return nc


_CACHE = {}


def run(inputs, S, L, n_cores, dbg=False, trace=False):
    key = (S, L, dbg)
    if key not in _CACHE:
        _CACHE[key] = build(S, L, dbg)
    nc = _CACHE[key]
    x = np.ascontiguousarray(np.asarray(inputs["x"], dtype=np.float32))
    B = x.shape[0]
    shared = {}
    for k, v in inputs.items():
        if k == "x":
            continue
        a = np.ascontiguousarray(np.asarray(v, dtype=np.float32))
        if k == "r_k":
            a = a.reshape(a.shape[0], -1)
        shared[k] = a
    in_maps = []
    for c in range(n_cores):
        m = dict(shared)
        m["x"] = x[c % B]
        in_maps.append(m)
    res = run_bass_kernel_spmd(nc, in_maps, core_ids=list(range(n_cores)))
    return res


def kernel(**inputs):
    x = np.asarray(inputs["x"])
    B, S, _ = x.shape
    L = np.asarray(inputs["norm_g"]).shape[0]
    res = run(inputs, S, L, 8)
    return np.stack([np.asarray(res.results[b]["out"], dtype=np.float32) for b in range(B)], axis=0)
```

```python
import contextlib
import math
import numpy as np
import concourse.bass as bass
import concourse.mybir as mybir
from concourse.bass_utils import run_bass_kernel_spmd

F32 = mybir.dt.float32
BF16 = mybir.dt.bfloat16
ALU = mybir.AluOpType
AF = mybir.ActivationFunctionType

ENGS = ("pe", "act", "dve", "pool", "sp")

D = 2048
RW = 768
PW = 512
SW = 768
A_COLS = 4 * RW + 128
B_COLS = 2 * PW
C_COLS = 4 * SW
IN_COLS = A_COLS + B_COLS + C_COLS
C0 = A_COLS + B_COLS
RMS_EPS = 1e-6
GN_EPS = 64e-5
CDEC = math.exp(-0.5)


class Buf:
    __slots__ = ("name", "w", "r", "rd", "excl")

    def __init__(self, name="", excl=False):
        self.name = name
        self.excl = excl
        self.w = None
        self.r = {}
        self.rd = []


class Op:
    __slots__ = ("eng", "fn", "dma", "deps", "seq", "needs_inc", "sem", "val", "prev_val", "xw")

    def __init__(self, eng, fn, dma):
        self.eng = eng
        self.fn = fn
        self.dma = dma
        self.deps = []
        self.needs_inc = False
        self.seq = None
        self.sem = None
        self.val = None
        self.prev_val = 0
        self.xw = None


class Tl:
    __slots__ = ("ap", "b")

    def __init__(self, ap, b):
        self.ap = ap
        self.b = b

    def __getitem__(self, k):
        return Tl(self.ap[k], self.b)

    def re(self, pat, **kw):
        return Tl(self.ap.rearrange(pat, **kw), self.b)

    def bc(self, shape):
        return Tl(self.ap.broadcast_to(shape), self.b)

    def wb(self, b):
        return Tl(self.ap, b)


def _ap(x):
    return x.ap if isinstance(x, Tl) else x


def _bufs(*xs):
    return [x.b for x in xs if isinstance(x, Tl) and x.b is not None]


class Prog:
    def __init__(self, nc, n_dma_sems=8):
        self.nc = nc
        self.ops = {e: [] for e in ENGS}
        self.n_dma_sems = n_dma_sems
        self.dma_rr = {e: 0 for e in ENGS}
        self.dma_cnt = {}

    def _dep(self, op, p, kind):
        if p is None or p is op:
            return
        if (not p.dma) and (not op.dma) and p.eng == op.eng:
            if kind != "raw" or op.eng == "pe":
                return
        op.deps.append(p)
        if not p.dma:
            p.needs_inc = True

    def op(self, eng, fn, reads=(), writes=(), dma=False):
        o = Op(eng, fn, dma)
        writes = list(writes) + [b for b in reads if b.excl and b not in writes]
        reads = [b for b in reads if not b.excl]
        for b in reads:
            self._dep(o, b.w, "raw")
        for b in writes:
            self._dep(o, b.w, "waw")
            for p in b.r.values():
                self._dep(o, p, "war")
            for p in b.rd:
                self._dep(o, p, "war")
        for b in reads:
            if dma:
                b.rd.append(o)
            else:
                b.r[eng] = o
        for b in writes:
            b.w = o
            b.r = {}
            b.rd = []
        if dma:
            k = (eng, self.dma_rr[eng] % self.n_dma_sems)
            self.dma_rr[eng] += 1
            o.sem = k
            o.prev_val = self.dma_cnt.get(k, 0)
            o.val = o.prev_val + 16
            self.dma_cnt[k] = o.val
        self.ops[eng].append(o)
        return o

    def barrier(self):
        lasts = []
        for e in ENGS:
            for o in reversed(self.ops[e]):
                if not o.dma and o.fn is not None:
                    lasts.append(o)
                    break
        snap = dict(self.dma_cnt)
        for e in ENGS:
            o = Op(e, None, False)
            for p in lasts:
                if p.eng != e:
                    o.deps.append(p)
                    p.needs_inc = True
            o.xw = snap
            self.ops[e].append(o)

    def dma(self, eng, out, in_, **kw):
        o_, i_ = _ap(out), _ap(in_)
        return self.op(eng, lambda e: e.dma_start(out=o_, in_=i_, **kw), _bufs(in_), _bufs(out), dma=True)

    def mm(self, out, lhsT, rhs, start=True, stop=True):
        o_, l_, r_ = _ap(out), _ap(lhsT), _ap(rhs)
        return self.op("pe", lambda e: e.matmul(o_, lhsT=l_, rhs=r_, start=start, stop=stop),
                       _bufs(lhsT, rhs), _bufs(out))

    def tr(self, out, in_, ident):
        o_, i_, d_ = _ap(out), _ap(in_), _ap(ident)
        return self.op("pe", lambda e: e.transpose(out=o_, in_=i_, identity=d_), _bufs(in_, ident), _bufs(out))

    def act(self, out, in_, func, bias=None, scale=1.0, accum=None):
        o_, i_ = _ap(out), _ap(in_)
        kw = {"scale": _ap(scale)}
        if bias is not None:
            kw["bias"] = _ap(bias)
        if accum is not None:
            kw["accum_out"] = _ap(accum)
        return self.op("act", lambda e: e.activation(out=o_, in_=i_, func=func, **kw),
                       _bufs(in_, bias, scale), _bufs(out, accum))

    def tt(self, eng, out, in0, in1, op):
        o_, a_, b_ = _ap(out), _ap(in0), _ap(in1)
        return self.op(eng, lambda e: e.tensor_tensor(out=o_, in0=a_, in1=b_, op=op), _bufs(in0, in1), _bufs(out))

    def ts(self, eng, out, in0, s1, s2, op0, op1=None):
        o_, a_, s1_, s2_ = _ap(out), _ap(in0), _ap(s1), _ap(s2)
        if op1 is None:
            fn = lambda e: e.tensor_scalar(out=o_, in0=a_, scalar1=s1_, scalar2=None, op0=op0)
        else:
            fn = lambda e: e.tensor_scalar(out=o_, in0=a_, scalar1=s1_, scalar2=s2_, op0=op0, op1=op1)
        return self.op(eng, fn, _bufs(in0, s1, s2), _bufs(out))

    def stt(self, out, in0, scalar, in1, op0, op1):
        o_, a_, s_, b_ = _ap(out), _ap(in0), _ap(scalar), _ap(in1)
        return self.op("dve", lambda e: e.scalar_tensor_tensor(out=o_, in0=a_, scalar=s_, in1=b_, op0=op0, op1=op1),
                       _bufs(in0, scalar, in1), _bufs(out))

    def copy(self, eng, out, in_):
        o_, i_ = _ap(out), _ap(in_)
        if eng == "act":
            fn = lambda e: e.copy(out=o_, in_=i_)
        else:
            fn = lambda e: e.tensor_copy(out=o_, in_=i_)
        return self.op(eng, fn, _bufs(in_), _bufs(out))

    def recip(self, out, in_):
        o_, i_ = _ap(out), _ap(in_)
        return self.op("dve", lambda e: e.reciprocal(out=o_, in_=i_), _bufs(in_), _bufs(out))

    def scan(self, out, d0, d1, init, op0, op1):
        o_, a_, b_, i_ = _ap(out), _ap(d0), _ap(d1), _ap(init)
        return self.op("dve", lambda e: e.tensor_tensor_scan(out=o_, data0=a_, data1=b_, initial=i_, op0=op0, op1=op1),
                       _bufs(d0, d1, init), _bufs(out))

    def memset(self, eng, out, val):
        o_ = _ap(out)
        return self.op(eng, lambda e: e.memset(o_, val), (), _bufs(out))

    def asel(self, out, in_, pattern, cmp, fill, base, cm):
        o_, i_ = _ap(out), _ap(in_)
        return self.op("pool", lambda e: e.affine_select(out=o_, in_=i_, pattern=pattern, compare_op=cmp, fill=fill,
                                                          base=base, channel_multiplier=cm), _bufs(in_), _bufs(out))

    def emit(self):
        nc = self.nc
        with contextlib.ExitStack() as st:
            esem = {e: st.enter_context(nc.semaphore("s_" + e)) for e in ENGS}
            dsem = {}
            for k in self.dma_cnt:
                dsem[k] = st.enter_context(nc.semaphore("d_%s_%d" % k))
            for e in ENGS:
                n = 0
                for o in self.ops[e]:
                    if o.needs_inc:
                        n += 1
                        o.seq = n
            final_waits = dict(self.dma_cnt)
            block = st.enter_context(nc.Block())

            def run(e, eng):
                waited = {}

                def wait(sem, key, val):
                    if waited.get(key, 0) >= val:
                        return
                    eng.wait_ge(sem, val)
                    waited[key] = val

                for o in self.ops[e]:
                    for p in o.deps:
                        if p.dma:
                            wait(dsem[p.sem], p.sem, p.val)
                        else:
                            wait(esem[p.eng], p.eng, p.seq)
                    if o.xw:
                        for k, v in o.xw.items():
                            wait(dsem[k], k, v)
                    if o.fn is None:
                        continue
                    if o.dma:
                        if o.prev_val:
                            wait(dsem[o.sem], o.sem, o.prev_val)
                        o.fn(eng).then_inc(dsem[o.sem], 16)
                    else:
                        ins = o.fn(eng)
                        if o.needs_inc:
                            ins.then_inc(esem[e], 1)
                if e == "sp":
                    for k, v in final_waits.items():
                        wait(dsem[k], k, v)

            @block.tensor
            def _(eng):
                run("pe", eng)

            @block.scalar
            def _(eng):
                run("act", eng)

            @block.vector
            def _(eng):
                run("dve", eng)

            @block.gpsimd
            def _(eng):
                run("pool", eng)

            @block.sync
            def _(eng):
                run("sp", eng)


class Arena:
    def __init__(self, t, nbytes):
        self.t = t
        self.nbytes = nbytes
        self.off = 0

    def alloc(self, cols, dtype=F32, name="", align=4):
        sz = cols * (2 if dtype == BF16 else 4)
        sz = (sz + 3) // 4 * 4
        self.off = (self.off + align - 1) // align * align
        a = self.off
        self.off += sz
        assert self.off <= self.nbytes, ("arena overflow", name, self.off, self.nbytes)
        ap = self.t[:, a // 4:(a + sz) // 4]
        if dtype == BF16:
            ap = ap.bitcast(BF16)[:, 0:cols]
        return Tl(ap, Buf(name))

    def bank(self, name=""):
        t = self.alloc(512, F32, name, align=2048)
        t.b.excl = True
        return t


class Rot:
    def __init__(self, items):
        self.items = items
        self.i = 0

    def next(self):
        x = self.items[self.i % len(self.items)]
        self.i += 1
        return x


def build(S, L, dbg=False):
    nc = bass.Bass("TRN2", target_bir_lowering=False)
    P = Prog(nc)
    NB = S // 128

    def din(name, shape):
        return nc.dram_tensor(name, shape, F32, kind="ExternalInput").ap()

    x_in = din("x", [S, D])
    norm_g = din("norm_g", [L, D])
    w_in = din("w_in", [L, D, IN_COLS])
    mu_a = din("mu_a", [L, A_COLS])
    w_up = din("w_up", [L, 64, RW])
    w0 = din("w0", [L, RW])
    a_up = din("a_up", [L, 64, RW])
    a0 = din("a0", [L, RW])
    k_k = din("k_k", [L, RW])
    k_a = din("k_a", [L, RW])
    r_k = din("r_k", [L, RW])
    gn_g = din("gn_g", [L, RW])
    gn_b = din("gn_b", [L, RW])
    pool_w = din("pool_w", [L, 4, 128, 128])
    pool_scale = din("pool_scale", [L, PW])
    qn_g = din("qn_g", [L, 64])
    kn_g = din("kn_g", [L, 64])
    w_out = din("w_out", [L, D, D])
    out = nc.dram_tensor("out", [S, D], F32, kind="ExternalOutput").ap()
    skind = "ExternalOutput" if dbg else "Internal"
    dbg_outs = {}

    def dump(name, tile, cols, dt=F32):
        if not dbg or name in dbg_outs:
            return
        dbg_outs[name] = nc.dram_tensor("dbg_" + name, [128, cols], dt, kind="ExternalOutput").ap()
        P.dma("sp", dbg_outs[name], tile)
    pj = nc.dram_tensor("pj", [IN_COLS, S], F32, kind=skind).ap()
    vtok = nc.dram_tensor("vtok", [S, SW], BF16, kind=skind).ap()
    ymix = nc.dram_tensor("ymix", [D, S], BF16, kind=skind).ap()

    with contextlib.ExitStack() as st:
        SB_BYTES = 206 * 1024
        sb_t = st.enter_context(nc.sbuf_tensor("arena", [128, SB_BYTES // 4], F32))
        ps_t = st.enter_context(nc.psum_tensor("psarena", [128, 4096], F32))
        sb = Arena(sb_t, SB_BYTES)
        ps = Arena(ps_t, 16384)

        identf = sb.alloc(128, F32, "identf")
        ident = sb.alloc(128, BF16, "ident")
        M2 = sb.alloc(256, F32, "M2")
        MLs = sb.alloc(128, F32, "MLs")
        MGE = sb.alloc(128, F32, "MGE")
        bones = sb.alloc(128, F32, "bones")
        zero1 = sb.alloc(1, F32, "zero1")
        epsr = sb.alloc(1, F32, "epsr")
        epsg = sb.alloc(1, F32, "epsg")
        TPB = min(512, S)
        reset = sb.alloc(TPB, F32, "reset")
        prm = sb.alloc(80, F32, "prm")
        qgs = sb.alloc(1, F32, "qgs")

        P.memset("pool", identf, 1.0)
        P.asel(identf, identf, [[-1, 128]], ALU.is_equal, 0.0, 0, 1)
        P.copy("dve", ident, identf)
        P.memset("pool", M2, 1.0)
        P.asel(M2[:, 0:128], M2[:, 0:128], [[1, 128]], ALU.is_gt, 0.0, 0, -1)
        P.asel(M2[:, 128:256], M2[:, 128:256], [[1, 128]], ALU.is_ge, 0.0, 0, -1)
        P.memset("pool", MLs, 1.0)
        P.asel(MLs, MLs, [[-1, 128]], ALU.is_gt, 0.0, 0, 1)
        P.ts("dve", MGE, MLs, -1.0, 1.0, ALU.mult, ALU.add)
        P.memset("dve", bones, 0.0)
        P.memset("dve", bones[0:64, 0:64], 1.0)
        P.memset("dve", bones[64:128, 64:128], 1.0)
        P.memset("dve", zero1, 0.0)
        P.memset("dve", epsr, RMS_EPS)
        P.memset("dve", epsg, GN_EPS)
        P.memset("dve", reset, 1.0)
        P.memset("dve", reset.re("p (c n) -> p c n", n=128)[:, :, 0:1], 0.0)
        const_off = sb.off

        PM_MU, PM_W0, PM_A0, PM_KK, PM_KA, PM_RK, PM_GG, PM_GB, PM_PS, PM_QN, PM_KN = 0, 25, 31, 37, 43, 49, 55, 61, 67, 71, 72

        def load_params(l):
            sb.off = const_off
            ps.off = 0
            stg = sb.alloc(128, F32, "prm_stage")
            P.memset("dve", stg, 0.0)
            stg_w = stg

            def ld(row0, src, n):
                P.dma("sp", stg_w[row0:row0 + n, :], src)
            ld(PM_MU, mu_a[l].rearrange("(t p) -> t p", p=128), 25)
            for r0, src in ((PM_W0, w0), (PM_A0, a0), (PM_KK, k_k), (PM_KA, k_a), (PM_RK, r_k), (PM_GG, gn_g),
                            (PM_GB, gn_b)):
                ld(r0, src[l].rearrange("(t p) -> t p", p=128), 6)
            ld(PM_PS, pool_scale[l].rearrange("(t p) -> t p", p=128), 4)
            P.dma("sp", stg_w[PM_QN:PM_QN + 1, 0:64], qn_g[l:l + 1, :])
            P.dma("sp", stg_w[PM_QN:PM_QN + 1, 64:128], qn_g[l:l + 1, :])
            P.dma("sp", stg_w[PM_KN:PM_KN + 1, 0:64], kn_g[l:l + 1, :])
            P.dma("sp", stg_w[PM_KN:PM_KN + 1, 64:128], kn_g[l:l + 1, :])
            pt = ps.bank("prm_ps")
            P.mm(pt[:, 0:73], lhsT=stg[0:73, :], rhs=identf[0:73, 0:73])
            P.copy("dve", prm[:, 0:73], pt[:, 0:73])
            P.ts("dve", qgs, prm[:, PM_QN:PM_QN + 1], 0.125, None, ALU.mult)

        def phase_A(l, xsrc):
            sb.off = const_off
            ps.off = 0
            TOKP = min(2048, S)
            npass = S // TOKP
            nb = TOKP // 128
            gbc = sb.alloc(D, F32, "gbc")
            P.dma("sp", gbc, norm_g[l].partition_broadcast(128))
            hT = sb.alloc(16 * TOKP, BF16, "hT")
            hT3 = hT.re("p (k t) -> p k t", t=TOKP)
            hbufs = [Buf("hT%d" % i) for i in range(TOKP // 512)]
            xrot = Rot([sb.alloc(D, F32, "x%d" % i) for i in range(2)])
            hrot = Rot([sb.alloc(D, BF16, "h%d" % i) for i in range(2)])
            junk = sb.alloc(D, BF16, "junk")
            ssr = Rot([sb.alloc(1, F32, "ss%d" % i) for i in range(4)])
            rsr = Rot([sb.alloc(1, F32, "rs%d" % i) for i in range(4)])
            CG = 384
            wfrot = Rot([sb.alloc(16 * CG, F32, "wf%d" % i) for i in range(2)])
            wbrot = Rot([sb.alloc(16 * CG, BF16, "wb%d" % i) for i in range(2)])
            ostrot = Rot([sb.alloc(512, F32, "ost%d" % i) for i in range(3)])
            vstrot = Rot([sb.alloc(CG, BF16, "vst%d" % i) for i in range(3)])
            trps = Rot([ps.bank("trp%d" % i) for i in range(2)])
            mops = Rot([ps.bank("mo%d" % i) for i in range(4)])
            for pa in range(npass):
                t0 = pa * TOKP
                for tb in range(nb):
                    xt = xrot.next()
                    P.dma("sp", xt, xsrc[t0 + tb * 128:t0 + (tb + 1) * 128, :])
                    ss = ssr.next()
                    rs = rsr.next()
                    P.memset("pool", ss, 0.0)
                    P.act(junk, xt, AF.Square, accum=ss)
                    P.act(rs, ss, AF.Sqrt, bias=epsr, scale=1.0 / D)
                    P.recip(rs, rs)
                    hb = hrot.next()
                    P.stt(hb, xt, rs, gbc, ALU.mult, ALU.mult)
                    hbuf = hbufs[tb // 4]
                    for q in range(4):
                        tp = trps.next()
                        tpb = Tl(tp.ap.bitcast(BF16)[:, 0:512], tp.b)
                        for j in range(4):
                            kc = 4 * q + j
                            P.tr(tpb[:, j * 128:(j + 1) * 128], hb[:, kc * 128:(kc + 1) * 128], ident)
                        dst = Tl(hT3.ap[:, 4 * q:4 * q + 4, tb * 128:(tb + 1) * 128], hbuf)
                        P.copy("act" if q % 2 == 0 else "dve", dst, tpb.re("p (j t) -> p j t", t=128))
                for cg in range(IN_COLS // CG):
                    wf = wfrot.next()
                    wsrc = w_in[l, :, cg * CG:(cg + 1) * CG].rearrange("(k p) c -> p k c", p=128)
                    wf3 = wf.re("p (k c) -> p k c", c=CG)
                    P.dma("sp", wf3[:, 0:8, :], wsrc[:, 0:8, :])
                    P.dma("pool", wf3[:, 8:16, :], wsrc[:, 8:16, :])
                    wb = wbrot.next()
                    wb3 = wb.re("p (k c) -> p k c", c=CG)
                    P.copy("pool", wb3[:, 0:6, :], wf3[:, 0:6, :])
                    P.copy("dve", wb3[:, 6:16, :], wf3[:, 6:16, :])
                    if cg not in (15, 16):
                        for ci in range(3):
                            ct = cg * 3 + ci
                            for tc in range(TOKP // 512):
                                po = mops.next()
                                for k in range(16):
                                    rhs = Tl(hT3.ap[:, k, tc * 512:(tc + 1) * 512], hbufs[tc])
                                    P.mm(po, lhsT=wb3[:, k, ci * 128:(ci + 1) * 128], rhs=rhs, start=(k == 0),
                                         stop=(k == 15))
                                ost = ostrot.next()
                                P.copy("act", ost, po)
                                P.dma("act", pj[ct * 128:(ct + 1) * 128, t0 + tc * 512:t0 + (tc + 1) * 512], ost)
                    else:
                        for tb in range(nb):
                            po = mops.next()
                            for k in range(16):
                                lhsT = Tl(hT3.ap[:, k, tb * 128:(tb + 1) * 128], hbufs[tb // 4])
                                P.mm(po[:, 0:CG], lhsT=lhsT, rhs=wb3[:, k, :], start=(k == 0), stop=(k == 15))
                            vst = vstrot.next()
                            P.copy("act", vst, po[:, 0:CG])
                            P.dma("act", vtok[t0 + tb * 128:t0 + (tb + 1) * 128, (cg - 15) * CG:(cg - 14) * CG], vst)

        def phase_B(l):
            sb.off = const_off
            ps.off = 0
            TP = TPB
            NCH = TP // 128
            npiece = S // TP
            NIT = 2 * NCH
            wup = sb.alloc(RW, F32, "wup")
            P.dma("sp", wup[0:64, :], w_up[l])
            P.dma("sp", wup[64:128, :], a_up[l])
            Sf = [[sb.alloc(64, F32, "Sf%d_%d" % (hp, i)) for i in range(2)] for hp in range(6)]
            Sb = [[sb.alloc(64, BF16, "Sb%d_%d" % (hp, i)) for i in range(2)] for hp in range(6)]
            cur = [0] * 6
            for hp in range(6):
                P.memset("dve", Sf[hp][0], 0.0)
                P.memset("dve", Sb[hp][0], 0.0)

            def f32t(name, n=TP):
                return sb.alloc(n, F32, name)

            def bft(name, n=TP):
                return sb.alloc(n, BF16, name)
            lda, wa, wa2 = f32t("lda", TP + 1), f32t("wa"), f32t("wa2")
            ldr = Rot([[f32t("ld%s%d" % (nm, i), TP + 1) for nm in "rkvg"] for i in range(2)])
            tmpd = f32t("tmpd")
            rs_, ks_, vs_, gs_ = f32t("rs"), f32t("ks"), f32t("vs"), f32t("gs")
            sig, a_, cs, Winv, csm, Wm, e2, E2 = (f32t(n) for n in ("sig", "a", "cs", "Winv", "csm", "Wm", "e2", "E2"))
            kk, kk2, nrm, kkn, tmpk, kp, b_ = (f32t(n) for n in ("kk", "kk2", "nrm", "kkn", "tmpk", "kp", "b"))
            rkr, sgt = f32t("rkr"), f32t("sgt")
            yc, sq, rstd, yn = f32t("yc"), f32t("sq"), f32t("rstd"), f32t("yn")
            KH, BH = bft("KH"), bft("BH")
            sets = []
            for i in range(3):
                d = {"AR": bft("AR%d" % i, 2 * TP), "KB": bft("KB%d" % i), "BB": bft("BB%d" % i), "VB": bft("VB%d" % i),
                     "W": f32t("W%d" % i), "bon": f32t("bon%d" % i), "sg": f32t("sg%d" % i), "ysb": f32t("ysb%d" % i),
                     "SA1": [bft("SA1_%d_%d" % (i, j), 256) for j in range(NIT)],
                     "SA2": [bft("SA2_%d_%d" % (i, j), 256) for j in range(NIT)],
                     "TT": [bft("TT_%d_%d" % (i, j), 128) for j in range(NIT)]}
                d["AR4"] = d["AR"].re("p (c w n) -> p c w n", w=2, n=128)
                sets.append(d)
            yarot = Rot([bft("ya%d" % i) for i in range(2)])
            TOKr = Rot([bft("TOK%d" % i, 384) for i in range(2)])
            Xr = [Rot([bft("X%d_%d" % (i, j), 128) for j in range(2)]) for i in range(NIT)]
            XTr = [Rot([bft("XT%d_%d" % (i, j), 128) for j in range(2)]) for i in range(NIT)]
            Pr = [Rot([bft("P%d_%d" % (i, j), 128) for j in range(2)]) for i in range(NIT)]
            Zbr = Rot([bft("Zb%d" % i, 64) for i in range(2)])
            Ubr = Rot([bft("Ub%d" % i, 64) for i in range(2)])
            pgen = ps.bank("pgenP")
            pgc = ps.bank("pgenC")
            pxb = [ps.bank("pxb%d" % i) for i in range(3)]
            ppb = [ps.bank("ppb%d" % i) for i in range(2)]
            pmisc = ps.bank("pmisc")
            ptr = Tl(pmisc.ap.bitcast(BF16)[:, 0:384], pmisc.b)
            pzr = Rot([Tl(pmisc.ap[:, 192:256], pmisc.b), Tl(pmisc.ap[:, 256:320], pmisc.b)])
            pur = Rot([Tl(pmisc.ap[:, 320:384], pmisc.b), Tl(pmisc.ap[:, 384:448], pmisc.b)])
            psn = Tl(pmisc.ap[:, 448:512], pmisc.b)
            items = [(c, h) for c in range(NCH) for h in range(2)]
            TPv = "p (c n) -> p c n"

            def px_tiles(it):
                bk = pxb[(it // 2) % 3]
                o = (it % 2) * 256
                return bk[:, o:o + 128], bk[:, o + 128:o + 256]

            def prepA(pc, hp, st):
                t0 = pc * TP
                AR, AR4, KB, BB, VB, W, bon, sg = (st[k] for k in ("AR", "AR4", "KB", "BB", "VB", "W", "bon", "sg"))

                def load_shift(dst, row0, eng="sp"):
                    if t0 > 0:
                        P.dma(eng, dst, pj[row0:row0 + 128, t0 - 1:t0 + TP])
                    else:
                        P.memset("pool", dst[:, 0:1], 0.0)
                        P.dma(eng, dst[:, 1:TP + 1], pj[row0:row0 + 128, 0:TP])

                def lerp(dst, ld, mucol):
                    P.tt("pool", tmpd, ld[:, 0:TP], ld[:, 1:TP + 1], ALU.subtract)
                    P.stt(dst, tmpd, prm[:, mucol:mucol + 1], ld[:, 1:TP + 1], ALU.mult, ALU.add)
                if hp == 0:
                    load_shift(lda, 3072)
                    lerp(wa, lda, PM_MU + 24)
                    P.act(wa2[0:64, :], wa[0:64, :], AF.Tanh)
                    P.copy("pool", wa2[64:128, :], wa[64:128, :])
                lds = ldr.next()
                for i, (ld, r0) in enumerate(zip(lds, (0, 768, 1536, 2304))):
                    load_shift(ld, r0 + hp * 128, "sp" if i % 2 == 0 else "act")
                lerp(rs_, lds[0], PM_MU + hp)
                lerp(ks_, lds[1], PM_MU + 6 + hp)
                lerp(vs_, lds[2], PM_MU + 12 + hp)
                lerp(gs_, lds[3], PM_MU + 18 + hp)
                P.mm(pgen[:, 0:TP], lhsT=wup[0:64, hp * 128:(hp + 1) * 128], rhs=wa2[0:64, :])
                P.act(sig, pgen[:, 0:TP], AF.Sigmoid, bias=prm[:, PM_W0 + hp:PM_W0 + hp + 1])
                P.mm(pgen[:, 0:TP], lhsT=wup[64:128, hp * 128:(hp + 1) * 128], rhs=wa2[64:128, :])
                P.act(a_, pgen[:, 0:TP], AF.Sigmoid, bias=prm[:, PM_A0 + hp:PM_A0 + hp + 1])
                P.act(sgt, gs_, AF.Sigmoid)
                P.tt("pool", sg, sgt, gs_, ALU.mult)
                P.scan(cs, reset, sig, 0.0, ALU.mult, ALU.add)
                P.act(W, cs, AF.Exp, scale=-CDEC)
                P.act(Winv, cs, AF.Exp, scale=CDEC)
                P.tt("pool", csm, cs, sig, ALU.subtract)
                P.act(Wm, csm, AF.Exp, scale=-CDEC)
                cs3 = cs.re(TPv, n=128)
                P.tt("dve", e2.re(TPv, n=128), cs3[:, :, 127:128].bc([128, NCH, 128]), cs3, ALU.subtract)
                P.act(E2, e2, AF.Exp, scale=-CDEC)
                P.ts("dve", kk, ks_, prm[:, PM_KK + hp:PM_KK + hp + 1], None, ALU.mult)
                P.act(kk2, kk, AF.Square)
                return {"st": st, "hp": hp, "t0": t0, "X": {}, "XT": {}, "P": {}}

            def prepB(ctx):
                st, hp = ctx["st"], ctx["hp"]
                AR, AR4, KB, BB, VB, W, bon, sg = (st[k] for k in ("AR", "AR4", "KB", "BB", "VB", "W", "bon", "sg"))
                P.mm(pgen[:, 0:TP], lhsT=bones, rhs=kk2)
                P.act(nrm, pgen[:, 0:TP], AF.Sqrt)
                P.ts("dve", nrm, nrm, 1e-12, None, ALU.max)
                P.recip(nrm, nrm)
                P.tt("dve", kkn, kk, nrm, ALU.mult)
                P.ts("dve", tmpk, a_, -1.0, prm[:, PM_KA + hp:PM_KA + hp + 1], ALU.add, ALU.mult)
                P.stt(kp, tmpk, 1.0, ks_, ALU.add, ALU.mult)
                P.tt("dve", b_, kkn, a_, ALU.mult)
                P.stt(AR4[:, :, 0, :], kkn.re(TPv, n=128), -1.0, Wm.re(TPv, n=128), ALU.mult, ALU.mult)
                P.tt("pool", AR4[:, :, 1, :], rs_.re(TPv, n=128), W.re(TPv, n=128), ALU.mult)
                P.tt("dve", KH, kp, Winv, ALU.mult)
                P.tt("dve", BH, b_, Winv, ALU.mult)
                P.tt("dve", KB, kp, E2, ALU.mult)
                P.tt("pool", BB, b_, E2, ALU.mult)
                P.copy("act", VB, vs_)
                P.stt(rkr, rs_, prm[:, PM_RK + hp:PM_RK + hp + 1], kp, ALU.mult, ALU.mult)

            def prepC(ctx):
                st, hp = ctx["st"], ctx["hp"]
                AR, AR4, KB, BB, VB, W, bon, sg = (st[k] for k in ("AR", "AR4", "KB", "BB", "VB", "W", "bon", "sg"))
                SA1, SA2 = st["SA1"], st["SA2"]
                P.mm(pgen[:, 0:TP], lhsT=bones, rhs=rkr)
                P.tt("dve", bon, pgen[:, 0:TP], vs_, ALU.mult)
                for it, (c, h) in enumerate(items):
                    sl = slice(64 * h, 64 * h + 64)
                    csl = slice(c * 128, (c + 1) * 128)
                    pscb = pxb[it % 3]
                    ps1 = pscb[:, 0:256]
                    ps2 = pscb[:, 256:512]
                    arc = Tl(AR4.ap[sl, c].rearrange("p w n -> p (w n)"), AR.b)
                    P.mm(ps1, lhsT=BH[sl, csl], rhs=arc)
                    P.mm(ps2, lhsT=KH[sl, csl], rhs=arc)
                    ps3 = ppb[it // 4][:, (it % 4) * 128:(it % 4 + 1) * 128]
                    P.mm(ps3, lhsT=AR4[sl, c, 0, :], rhs=BH[sl, csl])
                    P.tt("dve", SA1[it], ps1, M2, ALU.mult)
                    P.tt("dve", SA2[it], ps2, M2, ALU.mult)
                    xt0 = XTr[it].next()
                    P.tt("dve", xt0, ps3, MLs, ALU.mult)
                    p0 = Pr[it].next()
                    P.tt("pool", p0, SA1[it][:, 0:128], ident, ALU.add)
                    ctx["X"][it], ctx["XT"][it], ctx["P"][it] = SA1[it][:, 0:128], xt0, p0

            def level(ctx, k):
                Xs, XTs, Ps, st = ctx["X"], ctx["XT"], ctx["P"], ctx["st"]
                for half in range(0, NIT, 4):
                    its = list(range(half, min(half + 4, NIT)))
                    pend = {}
                    for it in its:
                        pp = ppb[it // 4][:, (it % 4) * 128:(it % 4 + 1) * 128]
                        px, pxt = px_tiles(it)
                        if k >= 1:
                            P.mm(pp, lhsT=XTs[it], rhs=Ps[it])
                        if k < 6:
                            P.mm(px, lhsT=XTs[it], rhs=Xs[it])
                            P.mm(pxt, lhsT=Xs[it], rhs=XTs[it])
                        pend[it] = (pp, px, pxt)
                    for it in its:
                        pp, px, pxt = pend[it]
                        if k >= 1:
                            pn_ = st["TT"][it] if k == 6 else Pr[it].next()
                            P.tt("dve", pn_, pp, Ps[it], ALU.add)
                            Ps[it] = pn_
                        if k < 6:
                            xn = Xr[it].next()
                            P.copy("act", xn, px)
                            xtn = XTr[it].next()
                            P.copy("act", xtn, pxt)
                            Xs[it], XTs[it] = xn, xtn

            def chain_step(ctx, c):
                st, hp = ctx["st"], ctx["hp"]
                AR4, KB, BB, VB, W, ysb = (st[k] for k in ("AR4", "KB", "BB", "VB", "W", "ysb"))
                SA1, SA2, Ps = st["SA1"], st["SA2"], ctx["P"]
                csl = slice(c * 128, (c + 1) * 128)
                TOK = TOKr.next()
                P.tr(ptr[:, 0:128], VB[:, csl], ident)
                P.tr(ptr[:, 128:256], KB[:, csl], ident)
                P.tr(ptr[:, 256:384], BB[:, csl], ident)
                P.copy("act", TOK, ptr)
                sfo, sbo = Sf[hp][cur[hp]], Sb[hp][cur[hp]]
                sfn, sbn = Sf[hp][1 - cur[hp]], Sb[hp][1 - cur[hp]]
                hv = []
                for h in range(2):
                    it = items.index((c, h))
                    sl = slice(64 * h, 64 * h + 64)
                    hv.append((it, sl, TOK[:, 64 * h:64 * h + 64], TOK[:, 128 + 64 * h:128 + 64 * h + 64],
                               TOK[:, 256 + 64 * h:256 + 64 * h + 64]))
                pzs, zbs, pus, ubs = [], [], [], []
                for (it, sl, vt, kbt, bbt) in hv:
                    pz = pzr.next()
                    P.mm(pz, lhsT=AR4[sl, c, 0, :], rhs=sbo[sl, :], start=True, stop=False)
                    P.mm(pz, lhsT=SA2[it][:, 0:128], rhs=vt, start=False, stop=True)
                    pzs.append(pz)
                for pz in pzs:
                    zb = Zbr.next()
                    P.copy("act", zb, pz)
                    zbs.append(zb)
                for (it, sl, vt, kbt, bbt), zb in zip(hv, zbs):
                    pu = pur.next()
                    P.mm(pu, lhsT=Ps[it], rhs=zb)
                    pus.append(pu)
                for pu in pus:
                    ub = Ubr.next()
                    P.copy("act", ub, pu)
                    ubs.append(ub)
                for (it, sl, vt, kbt, bbt), ub in zip(hv, ubs):
                    P.mm(pgc[sl, 0:128], lhsT=sbo[sl, :], rhs=AR4[sl, c, 1, :], start=True, stop=False)
                    P.mm(pgc[sl, 0:128], lhsT=ub, rhs=SA1[it][:, 128:256], start=False, stop=False)
                    P.mm(pgc[sl, 0:128], lhsT=vt, rhs=SA2[it][:, 128:256], start=False, stop=True)
                for (it, sl, vt, kbt, bbt), ub in zip(hv, ubs):
                    P.mm(psn[sl, :], lhsT=bbt, rhs=ub, start=True, stop=False)
                    P.mm(psn[sl, :], lhsT=kbt, rhs=vt, start=False, stop=True)
                P.copy("act", ysb[:, csl], pgc[:, 0:128])
                wc = W[:, c * 128 + 127:c * 128 + 128]
                P.stt(sfn, sfo, wc, psn, ALU.mult, ALU.add)
                P.stt(sbn, sfo, wc, psn, ALU.mult, ALU.add)
                cur[hp] = 1 - cur[hp]

            def post(ctx):
                st, hp, t0 = ctx["st"], ctx["hp"], ctx["t0"]
                ysb, bon, sg = st["ysb"], st["bon"], st["sg"]
                P.mm(pgc[:, 0:TP], lhsT=bones, rhs=ysb)
                P.stt(yc, pgc[:, 0:TP], -1.0 / 64, ysb, ALU.mult, ALU.add)
                P.act(sq, yc, AF.Square)
                P.mm(pgc[:, 0:TP], lhsT=bones, rhs=sq)
                P.act(rstd, pgc[:, 0:TP], AF.Sqrt, bias=epsg, scale=1.0 / 64)
                P.recip(rstd, rstd)
                P.tt("pool", yn, yc, rstd, ALU.mult)
                P.ts("dve", yn, yn, prm[:, PM_GG + hp:PM_GG + hp + 1], prm[:, PM_GB + hp:PM_GB + hp + 1], ALU.mult,
                     ALU.add)
                P.tt("pool", yn, yn, bon, ALU.add)
                ya = yarot.next()
                P.tt("dve", ya, yn, sg, ALU.mult)
                P.dma("sp", ymix[hp * 128:(hp + 1) * 128, t0:t0 + TP], ya)

            units = [(pc, hp) for pc in range(npiece) for hp in range(6)]

            def start_unit(ui):
                pc, hp = units[ui]
                return prepA(pc, hp, sets[ui % 3])
            ctxs = {0: start_unit(0)}
            prepB(ctxs[0])
            prepC(ctxs[0])
            for ui in range(len(units)):
                ctx = ctxs[ui]
                prev = ctxs.get(ui - 1)
                nxt = None
                for k in range(7):
                    level(ctx, k)
                    if k == 0 and ui + 1 < len(units):
                        nxt = ctxs[ui + 1] = start_unit(ui + 1)
                    if prev is not None and 1 <= k <= 4 and (k - 1) < NCH:
                        chain_step(prev, k - 1)
                    if k == 2 and nxt is not None:
                        prepB(nxt)
                    if k == 5 and prev is not None:
                        for c in range(4, NCH):
                            chain_step(prev, c)
                        post(prev)
                    if k == 6 and nxt is not None:
                        prepC(nxt)
                ctxs.pop(ui - 1, None)
            last = ctxs[len(units) - 1]
            for c in range(NCH):
                chain_step(last, c)
            post(last)

        def phase_C(l):
            sb.off = const_off
            ps.off = 0
            TP = min(1024, S)
            npiece = S // TP
            H = 16
            pwf = sb.alloc(512, F32, "pwf")
            pwb = sb.alloc(512, BF16, "pwb")
            P.dma("sp", pwf.re("p (g d) -> p g d", d=128), pool_w[l].rearrange("g c d -> c g d"))
            P.copy("dve", pwb, pwf)
            corr = sb.alloc(4 * 16, F32, "corr")
            for g, win in enumerate((2, 4, 8, 16)):
                for t in range(16):
                    P.memset("pool", corr[:, g * 16 + t:g * 16 + t + 1], float(win) / min(t + 1, win))
            urot = Rot([sb.alloc(TP + H, F32, "pu%d" % i) for i in range(2)])
            grot = Rot([sb.alloc(TP, F32, "pg%d" % i) for i in range(2)])
            sA = sb.alloc(TP + H, F32, "sA")
            sB = sb.alloc(TP + H, F32, "sB")
            dbf = sb.alloc(TP, BF16, "dbf")
            sgp = sb.alloc(TP, F32, "sgp")
            ybr = Rot([sb.alloc(TP, BF16, "yb%d" % i) for i in range(2)])
            pyr = Rot([ps.bank("pc%d" % i) for i in range(2)])
            for pc in range(npiece):
                t0 = pc * TP
                for g, win in enumerate((2, 4, 8, 16)):
                    u = urot.next()
                    row = A_COLS + g * 128
                    if t0 > 0:
                        P.dma("sp", u, pj[row:row + 128, t0 - H:t0 + TP])
                    else:
                        P.memset("pool", u[:, 0:H], 0.0)
                        P.dma("sp", u[:, H:H + TP], pj[row:row + 128, 0:TP])
                    pg_ = grot.next()
                    P.dma("act", pg_, pj[A_COLS + PW + g * 128:A_COLS + PW + (g + 1) * 128, t0:t0 + TP])
                    src = u
                    sh = 1
                    dsts = [sA, sB]
                    lo = 0
                    for lev in range(g + 1):
                        dst = dsts[lev % 2]
                        lo += sh
                        P.tt("pool" if lev % 2 else "dve", dst[:, lo:TP + H], src[:, lo:TP + H], src[:, lo - sh:TP + H - sh],
                             ALU.add)
                        src = dst
                        sh *= 2
                    ssum = src
                    if t0 == 0:
                        P.tt("dve", ssum[:, H:H + 16], ssum[:, H:H + 16], corr[:, g * 16:(g + 1) * 16], ALU.mult)
                    P.stt(dbf, ssum[:, H:H + TP], 1.0 / win, u[:, H:H + TP], ALU.mult, ALU.subtract)
                    P.act(sgp, pg_, AF.Silu)
                    yb = ybr.next()
                    for hf in range(TP // 512):
                        py = pyr.next()
                        P.mm(py, lhsT=pwb[:, g * 128:(g + 1) * 128], rhs=dbf[:, hf * 512:(hf + 1) * 512])
                        P.stt(yb[:, hf * 512:(hf + 1) * 512], py, prm[:, PM_PS + g:PM_PS + g + 1],
                              sgp[:, hf * 512:(hf + 1) * 512], ALU.mult, ALU.mult)
                    P.dma("sp", ymix[RW + g * 128:RW + (g + 1) * 128, t0:t0 + TP], yb)

        def phase_D(l):
            sb.off = const_off
            ps.off = 0
            TPq = min(1024, S)
            QH = sb.alloc(S, BF16, "QH")
            KHt = sb.alloc(S, BF16, "KHt")
            V = sb.alloc(NB * 128, BF16, "V")
            V3 = V.re("p (n c) -> p n c", c=128)
            sgc = sb.alloc(S, BF16, "sgc")
            ycst = sb.alloc(S, BF16, "ycst")
            SGr = Rot([sb.alloc(S, F32, "SG%d" % i) for i in range(3)])
            EZr = Rot([sb.alloc(S, BF16, "EZ%d" % i) for i in range(3)])
            Pnr = Rot([sb.alloc(S, F32, "Pn%d" % i) for i in range(2)])
            Wtr = Rot([sb.alloc(S, BF16, "Wt%d" % i) for i in range(2)])
            WTr = Rot([sb.alloc(S, BF16, "WT%d" % i) for i in range(2)])
            qf = Rot([sb.alloc(TPq, F32, "qf%d" % i) for i in range(2)])
            sqq = sb.alloc(TPq, F32, "sqq")
            rt = sb.alloc(TPq, F32, "rt")
            zr = Rot([ps.bank("z%d" % i) for i in range(3)])
            ptwr = Rot([ps.bank("ptw%d" % i) for i in range(2)])
            por = Rot([ps.bank("po%d" % i) for i in range(2)])
            pgen = ps.bank("pgenD")
            for hp in range(6):
                P.dma("sp", V3, vtok[:, hp * 128:(hp + 1) * 128].rearrange("(n p) c -> p n c", p=128))
                for (dst, row0, gcol) in ((QH, C0 + hp * 128, qgs), (KHt, C0 + SW + hp * 128, prm[:, PM_KN:PM_KN + 1])):
                    for pc in range(S // TPq):
                        q = qf.next()
                        P.dma("sp", q, pj[row0:row0 + 128, pc * TPq:(pc + 1) * TPq])
                        P.act(sqq, q, AF.Square)
                        for hf in range(TPq // 512):
                            hs = slice(hf * 512, (hf + 1) * 512)
                            P.mm(pgen, lhsT=bones, rhs=sqq[:, hs])
                            P.act(rt[:, hs], pgen, AF.Sqrt, bias=epsr, scale=1.0 / 64)
                        P.recip(rt, rt)
                        P.stt(dst[:, pc * TPq:(pc + 1) * TPq], q, gcol, rt, ALU.mult, ALU.mult)
                for pc in range(S // TPq):
                    q = qf.next()
                    r0 = C0 + 3 * SW + hp * 128
                    P.dma("act", q, pj[r0:r0 + 128, pc * TPq:(pc + 1) * TPq])
                    P.act(sgc[:, pc * TPq:(pc + 1) * TPq], q, AF.Silu)
                items = [(T, h) for T in range(NB) for h in range(2)]
                state = {}

                def stage1a(T, h):
                    sl = slice(64 * h, 64 * h + 64)
                    kend = (T + 1) * 128
                    SG = SGr.next()
                    EZ = EZr.next()
                    for ck in range((kend + 511) // 512):
                        w_ = min(512, kend - ck * 512)
                        pz = zr.next()
                        P.mm(pz[:, 0:w_], lhsT=QH[sl, T * 128:(T + 1) * 128], rhs=KHt[sl, ck * 512:ck * 512 + w_])
                        P.act(SG[:, ck * 512:ck * 512 + w_], pz[:, 0:w_], AF.Sigmoid, scale=-1.0)
                        P.act(EZ[:, ck * 512:ck * 512 + w_], pz[:, 0:w_], AF.Sigmoid)
                    dsl = slice(T * 128, kend)
                    P.tt("pool", SG[:, dsl], SG[:, dsl], MLs, ALU.mult)
                    P.tt("pool", SG[:, dsl], SG[:, dsl], MGE, ALU.add)
                    P.tt("pool", EZ[:, dsl], EZ[:, dsl], MLs, ALU.mult)
                    state[("a", T, h)] = (SG, EZ)

                def stage1b(T, h):
                    kend = (T + 1) * 128
                    SG, EZ = state.pop(("a", T, h))
                    Pn = Pnr.next()
                    P.scan(Pn[:, 0:kend][:, ::-1], SG[:, 0:kend][:, ::-1], zero1.bc([128, kend]), 1.0, ALU.mult, ALU.add)
                    state[("b", T, h)] = (EZ, Pn)

                def stage1c(T, h):
                    kend = (T + 1) * 128
                    EZ, Pn = state.pop(("b", T, h))
                    Wt = Wtr.next()
                    npl = ((kend - 1) * 3 // 8) // 64 * 64
                    if npl > 0:
                        P.tt("pool", Wt[:, 0:npl], EZ[:, 0:npl], Pn[:, 1:npl + 1], ALU.mult)
                    P.tt("dve", Wt[:, npl:kend - 1], EZ[:, npl:kend - 1], Pn[:, npl + 1:kend], ALU.mult)
                    P.memset("pool", Wt[:, kend - 1:kend], 0.0)
                    state[(T, h)] = Wt

                def stage2(T, h):
                    sl = slice(64 * h, 64 * h + 64)
                    Wt = state.pop((T, h))
                    WT = WTr.next()
                    nsb = T + 1
                    for g0 in range(0, nsb, 4):
                        n = min(4, nsb - g0)
                        pt = ptwr.next()
                        ptb = Tl(pt.ap.bitcast(BF16)[:, 0:512], pt.b)
                        for j in range(n):
                            P.tr(ptb[:, j * 128:(j + 1) * 128], Wt[:, (g0 + j) * 128:(g0 + j + 1) * 128], ident)
                        P.copy("act", WT[:, g0 * 128:(g0 + n) * 128], ptb[:, 0:n * 128])
                    if h == 0:
                        state["po"] = por.next()
                    po = state["po"]
                    for sbk in range(nsb):
                        P.mm(po[sl, 0:128], lhsT=V3[:, sbk, 64 * h:64 * h + 64], rhs=WT[:, sbk * 128:(sbk + 1) * 128],
                             start=(sbk == 0), stop=(sbk == nsb - 1))
                    if h == 1:
                        P.tt("dve", ycst[:, T * 128:(T + 1) * 128], po[:, 0:128], sgc[:, T * 128:(T + 1) * 128], ALU.mult)

                n_it = len(items)
                for i in range(n_it + 3):
                    if i < n_it:
                        stage1a(*items[i])
                    if 0 <= i - 1 < n_it:
                        stage1b(*items[i - 1])
                    if 0 <= i - 2 < n_it:
                        stage1c(*items[i - 2])
                    if 0 <= i - 3 < n_it:
                        stage2(*items[i - 3])
                P.dma("sp", ymix[RW + PW + hp * 128:RW + PW + (hp + 1) * 128, :], ycst)

        def phase_E(l, xsrc):
            sb.off = const_off
            ps.off = 0
            wo = sb.alloc(16 * D, BF16, "wo")
            wo3 = wo.re("p (k c) -> p k c", c=D)
            wst = Rot([sb.alloc(4 * D, F32, "wst%d" % i) for i in range(2)])
            for kq in range(4):
                w = wst.next()
                w3 = w.re("p (k c) -> p k c", c=D)
                P.dma("sp" if kq % 2 == 0 else "act", w3,
                      w_out[l, kq * 512:(kq + 1) * 512, :].rearrange("(k p) c -> p k c", p=128))
                P.copy("pool", wo3[:, kq * 4:kq * 4 + 2, :], w3[:, 0:2, :])
                P.copy("dve", wo3[:, kq * 4 + 2:kq * 4 + 4, :], w3[:, 2:4, :])
            TQ = 512
            yr = Rot([sb.alloc(16 * TQ, BF16, "ymx%d" % i) for i in range(2)])
            xr = Rot([sb.alloc(D, F32, "xe%d" % i) for i in range(3)])
            pr = Rot([ps.bank("pe%d" % i) for i in range(8)])
            for tq in range(S // TQ):
                ym = yr.next()
                ym3 = ym.re("p (k t) -> p k t", t=TQ)
                P.dma("sp", ym3, ymix[:, tq * TQ:(tq + 1) * TQ].rearrange("(k p) t -> p k t", p=128))
                for tb in range(TQ // 128):
                    r0 = tq * TQ + tb * 128
                    xe = xr.next()
                    P.dma("sp", xe, xsrc[r0:r0 + 128, :])
                    for cq in range(4):
                        po = pr.next()
                        for k in range(16):
                            P.mm(po, lhsT=ym3[:, k, tb * 128:(tb + 1) * 128], rhs=wo3[:, k, cq * 512:(cq + 1) * 512],
                                 start=(k == 0), stop=(k == 15))
                        P.tt("dve", xe[:, cq * 512:(cq + 1) * 512], po, xe[:, cq * 512:(cq + 1) * 512], ALU.add)
                    P.dma("act", out[r0:r0 + 128, :], xe)

        for l in range(L):
            xsrc = x_in if l == 0 else out
            load_params(l)
            P.barrier()
            phase_A(l, xsrc)
            P.barrier()
            phase_B(l)
            P.barrier()
            phase_C(l)
            P.barrier()
            phase_D(l)
            P.barrier()
            phase_E(l, xsrc)
            P.barrier()
        P.emit()
    import os
    if os.environ.get("KSTATS"):
        print("op counts", {e: len(P.ops[e]) for e in ENGS}, flush=True)
    return nc


_CACHE = {}


def run(inputs, S, L, n_cores, dbg=False, trace=False):
    key = (S, L, dbg)
    if key not in _CACHE:
        _CACHE[key] = build(S, L, dbg)
    nc = _CACHE[key]
    x = np.ascontiguousarray(np.asarray(inputs["x"], dtype=np.float32))
    B = x.shape[0]
    shared = {}
    for k, v in inputs.items():
        if k == "x":
            continue
        a = np.ascontiguousarray(np.asarray(v, dtype=np.float32))
        if k == "r_k":
            a = a.reshape(a.shape[0], -1)
        shared[k] = a
    in_maps = []
    for c in range(n_cores):
        m = dict(shared)
        m["x"] = x[c % B]
        in_maps.append(m)
    res = run_bass_kernel_spmd(nc, in_maps, core_ids=list(range(n_cores)))
    return res


def kernel(**inputs):
    x = np.asarray(inputs["x"])
    B, S, _ = x.shape
    L = np.asarray(inputs["norm_g"]).shape[0]
    res = run(inputs, S, L, 8)
    return np.stack([np.asarray(res.results[b]["out"], dtype=np.float32) for b in range(B)], axis=0)
```

```python
import contextlib
import math
import numpy as np
import concourse.bass as bass
import concourse.mybir as mybir
from concourse.bass_utils import run_bass_kernel_spmd

F32 = mybir.dt.float32
BF16 = mybir.dt.bfloat16
ALU = mybir.AluOpType
AF = mybir.ActivationFunctionType

ENGS = ("pe", "act", "dve", "pool", "sp")

D = 2048
RW = 768
PW = 512
SW = 768
A_COLS = 4 * RW + 128
B_COLS = 2 * PW
C_COLS = 4 * SW
IN_COLS = A_COLS + B_COLS + C_COLS
C0 = A_COLS + B_COLS
RMS_EPS = 1e-6
GN_EPS = 64e-5
CDEC = math.exp(-0.5)


class Buf:
    __slots__ = ("name", "w", "r", "rd", "excl")

    def __init__(self, name="", excl=False):
        self.name = name
        self.excl = excl
        self.w = None
        self.r = {}
        self.rd = []


class Op:
    __slots__ = ("eng", "fn", "dma", "deps", "seq", "needs_inc", "sem", "val", "prev_val", "xw")

    def __init__(self, eng, fn, dma):
        self.eng = eng
        self.fn = fn
        self.dma = dma
        self.deps = []
        self.needs_inc = False
        self.seq = None
        self.sem = None
        self.val = None
        self.prev_val = 0
        self.xw = None


class Tl:
    __slots__ = ("ap", "b")

    def __init__(self, ap, b):
        self.ap = ap
        self.b = b

    def __getitem__(self, k):
        return Tl(self.ap[k], self.b)

    def re(self, pat, **kw):
        return Tl(self.ap.rearrange(pat, **kw), self.b)

    def bc(self, shape):
        return Tl(self.ap.broadcast_to(shape), self.b)

    def wb(self, b):
        return Tl(self.ap, b)


def _ap(x):
    return x.ap if isinstance(x, Tl) else x


def _bufs(*xs):
    return [x.b for x in xs if isinstance(x, Tl) and x.b is not None]


class Prog:
    def __init__(self, nc, n_dma_sems=8):
        self.nc = nc
        self.ops = {e: [] for e in ENGS}
        self.n_dma_sems = n_dma_sems
        self.dma_rr = {e: 0 for e in ENGS}
        self.dma_cnt = {}

    def _dep(self, op, p, kind):
        if p is None or p is op:
            return
        if (not p.dma) and (not op.dma) and p.eng == op.eng:
            if kind != "raw" or op.eng == "pe":
                return
        op.deps.append(p)
        if not p.dma:
            p.needs_inc = True

    def op(self, eng, fn, reads=(), writes=(), dma=False):
        o = Op(eng, fn, dma)
        writes = list(writes) + [b for b in reads if b.excl and b not in writes]
        reads = [b for b in reads if not b.excl]
        for b in reads:
            self._dep(o, b.w, "raw")
        for b in writes:
            self._dep(o, b.w, "waw")
            for p in b.r.values():
                self._dep(o, p, "war")
            for p in b.rd:
                self._dep(o, p, "war")
        for b in reads:
            if dma:
                b.rd.append(o)
            else:
                b.r[eng] = o
        for b in writes:
            b.w = o
            b.r = {}
            b.rd = []
        if dma:
            k = (eng, self.dma_rr[eng] % self.n_dma_sems)
            self.dma_rr[eng] += 1
            o.sem = k
            o.prev_val = self.dma_cnt.get(k, 0)
            o.val = o.prev_val + 16
            self.dma_cnt[k] = o.val
        self.ops[eng].append(o)
        return o

    def barrier(self):
        lasts = []
        for e in ENGS:
            for o in reversed(self.ops[e]):
                if not o.dma and o.fn is not None:
                    lasts.append(o)
                    break
        snap = dict(self.dma_cnt)
        for e in ENGS:
            o = Op(e, None, False)
            for p in lasts:
                if p.eng != e:
                    o.deps.append(p)
                    p.needs_inc = True
            o.xw = snap
            self.ops[e].append(o)

    def dma(self, eng, out, in_, **kw):
        o_, i_ = _ap(out), _ap(in_)
        return self.op(eng, lambda e: e.dma_start(out=o_, in_=i_, **kw), _bufs(in_), _bufs(out), dma=True)

    def mm(self, out, lhsT, rhs, start=True, stop=True):
        o_, l_, r_ = _ap(out), _ap(lhsT), _ap(rhs)
        return self.op("pe", lambda e: e.matmul(o_, lhsT=l_, rhs=r_, start=start, stop=stop),
                       _bufs(lhsT, rhs), _bufs(out))

    def tr(self, out, in_, ident):
        o_, i_, d_ = _ap(out), _ap(in_), _ap(ident)
        return self.op("pe", lambda e: e.transpose(out=o_, in_=i_, identity=d_), _bufs(in_, ident), _bufs(out))

    def act(self, out, in_, func, bias=None, scale=1.0, accum=None):
        o_, i_ = _ap(out), _ap(in_)
        kw = {"scale": _ap(scale)}
        if bias is not None:
            kw["bias"] = _ap(bias)
        if accum is not None:
            kw["accum_out"] = _ap(accum)
        return self.op("act", lambda e: e.activation(out=o_, in_=i_, func=func, **kw),
                       _bufs(in_, bias, scale), _bufs(out, accum))

    def tt(self, eng, out, in0, in1, op):
        o_, a_, b_ = _ap(out), _ap(in0), _ap(in1)
        return self.op(eng, lambda e: e.tensor_tensor(out=o_, in0=a_, in1=b_, op=op), _bufs(in0, in1), _bufs(out))

    def ts(self, eng, out, in0, s1, s2, op0, op1=None):
        o_, a_, s1_, s2_ = _ap(out), _ap(in0), _ap(s1), _ap(s2)
        if op1 is None:
            fn = lambda e: e.tensor_scalar(out=o_, in0=a_, scalar1=s1_, scalar2=None, op0=op0)
        else:
            fn = lambda e: e.tensor_scalar(out=o_, in0=a_, scalar1=s1_, scalar2=s2_, op0=op0, op1=op1)
        return self.op(eng, fn, _bufs(in0, s1, s2), _bufs(out))

    def stt(self, out, in0, scalar, in1, op0, op1):
        o_, a_, s_, b_ = _ap(out), _ap(in0), _ap(scalar), _ap(in1)
        return self.op("dve", lambda e: e.scalar_tensor_tensor(out=o_, in0=a_, scalar=s_, in1=b_, op0=op0, op1=op1),
                       _bufs(in0, scalar, in1), _bufs(out))

    def copy(self, eng, out, in_):
        o_, i_ = _ap(out), _ap(in_)
        if eng == "act":
            fn = lambda e: e.copy(out=o_, in_=i_)
        else:
            fn = lambda e: e.tensor_copy(out=o_, in_=i_)
        return self.op(eng, fn, _bufs(in_), _bufs(out))

    def recip(self, out, in_):
        o_, i_ = _ap(out), _ap(in_)
        return self.op("dve", lambda e: e.reciprocal(out=o_, in_=i_), _bufs(in_), _bufs(out))

    def scan(self, out, d0, d1, init, op0, op1):
        o_, a_, b_, i_ = _ap(out), _ap(d0), _ap(d1), _ap(init)
        return self.op("dve", lambda e: e.tensor_tensor_scan(out=o_, data0=a_, data1=b_, initial=i_, op0=op0, op1=op1),
                       _bufs(d0, d1, init), _bufs(out))

    def memset(self, eng, out, val):
        o_ = _ap(out)
        return self.op(eng, lambda e: e.memset(o_, val), (), _bufs(out))

    def asel(self, out, in_, pattern, cmp, fill, base, cm):
        o_, i_ = _ap(out), _ap(in_)
        return self.op("pool", lambda e: e.affine_select(out=o_, in_=i_, pattern=pattern, compare_op=cmp, fill=fill,
                                                          base=base, channel_multiplier=cm), _bufs(in_), _bufs(out))

    def emit(self):
        nc = self.nc
        with contextlib.ExitStack() as st:
            esem = {e: st.enter_context(nc.semaphore("s_" + e)) for e in ENGS}
            dsem = {}
            for k in self.dma_cnt:
                dsem[k] = st.enter_context(nc.semaphore("d_%s_%d" % k))
            for e in ENGS:
                n = 0
                for o in self.ops[e]:
                    if o.needs_inc:
                        n += 1
                        o.seq = n
            final_waits = dict(self.dma_cnt)
            block = st.enter_context(nc.Block())

            def run(e, eng):
                waited = {}

                def wait(sem, key, val):
                    if waited.get(key, 0) >= val:
                        return
                    eng.wait_ge(sem, val)
                    waited[key] = val

                for o in self.ops[e]:
                    for p in o.deps:
                        if p.dma:
                            wait(dsem[p.sem], p.sem, p.val)
                        else:
                            wait(esem[p.eng], p.eng, p.seq)
                    if o.xw:
                        for k, v in o.xw.items():
                            wait(dsem[k], k, v)
                    if o.fn is None:
                        continue
                    if o.dma:
                        if o.prev_val:
                            wait(dsem[o.sem], o.sem, o.prev_val)
                        o.fn(eng).then_inc(dsem[o.sem], 16)
                    else:
                        ins = o.fn(eng)
                        if o.needs_inc:
                            ins.then_inc(esem[e], 1)
                if e == "sp":
                    for k, v in final_waits.items():
                        wait(dsem[k], k, v)

            @block.tensor
            def _(eng):
                run("pe", eng)

            @block.scalar
            def _(eng):
                run("act", eng)

            @block.vector
            def _(eng):
                run("dve", eng)

            @block.gpsimd
            def _(eng):
                run("pool", eng)

            @block.sync
            def _(eng):
                run("sp", eng)


class Arena:
    def __init__(self, t, nbytes):
        self.t = t
        self.nbytes = nbytes
        self.off = 0

    def alloc(self, cols, dtype=F32, name="", align=4):
        sz = cols * (2 if dtype == BF16 else 4)
        sz = (sz + 3) // 4 * 4
        self.off = (self.off + align - 1) // align * align
        a = self.off
        self.off += sz
        assert self.off <= self.nbytes, ("arena overflow", name, self.off, self.nbytes)
        ap = self.t[:, a // 4:(a + sz) // 4]
        if dtype == BF16:
            ap = ap.bitcast(BF16)[:, 0:cols]
        return Tl(ap, Buf(name))

    def bank(self, name=""):
        t = self.alloc(512, F32, name, align=2048)
        t.b.excl = True
        return t


class Rot:
    def __init__(self, items):
        self.items = items
        self.i = 0

    def next(self):
        x = self.items[self.i % len(self.items)]
        self.i += 1
        return x


def build(S, L, dbg=False):
    nc = bass.Bass("TRN2", target_bir_lowering=False)
    P = Prog(nc)
    NB = S // 128

    def din(name, shape):
        return nc.dram_tensor(name, shape, F32, kind="ExternalInput").ap()

    x_in = din("x", [S, D])
    norm_g = din("norm_g", [L, D])
    w_in = din("w_in", [L, D, IN_COLS])
    mu_a = din("mu_a", [L, A_COLS])
    w_up = din("w_up", [L, 64, RW])
    w0 = din("w0", [L, RW])
    a_up = din("a_up", [L, 64, RW])
    a0 = din("a0", [L, RW])
    k_k = din("k_k", [L, RW])
    k_a = din("k_a", [L, RW])
    r_k = din("r_k", [L, RW])
    gn_g = din("gn_g", [L, RW])
    gn_b = din("gn_b", [L, RW])
    pool_w = din("pool_w", [L, 4, 128, 128])
    pool_scale = din("pool_scale", [L, PW])
    qn_g = din("qn_g", [L, 64])
    kn_g = din("kn_g", [L, 64])
    w_out = din("w_out", [L, D, D])
    out = nc.dram_tensor("out", [S, D], F32, kind="ExternalOutput").ap()
    skind = "ExternalOutput" if dbg else "Internal"
    dbg_outs = {}

    def dump(name, tile, cols, dt=F32):
        if not dbg or name in dbg_outs:
            return
        dbg_outs[name] = nc.dram_tensor("dbg_" + name, [128, cols], dt, kind="ExternalOutput").ap()
        P.dma("sp", dbg_outs[name], tile)
    pj = nc.dram_tensor("pj", [IN_COLS, S], F32, kind=skind).ap()
    vtok = nc.dram_tensor("vtok", [S, SW], BF16, kind=skind).ap()
    ymix = nc.dram_tensor("ymix", [D, S], BF16, kind=skind).ap()

    with contextlib.ExitStack() as st:
        SB_BYTES = 206 * 1024
        sb_t = st.enter_context(nc.sbuf_tensor("arena", [128, SB_BYTES // 4], F32))
        ps_t = st.enter_context(nc.psum_tensor("psarena", [128, 4096], F32))
        sb = Arena(sb_t, SB_BYTES)
        ps = Arena(ps_t, 16384)

        identf = sb.alloc(128, F32, "identf")
        ident = sb.alloc(128, BF16, "ident")
        M2 = sb.alloc(256, F32, "M2")
        MLs = sb.alloc(128, F32, "MLs")
        MGE = sb.alloc(128, F32, "MGE")
        bones = sb.alloc(128, F32, "bones")
        zero1 = sb.alloc(1, F32, "zero1")
        epsr = sb.alloc(1, F32, "epsr")
        epsg = sb.alloc(1, F32, "epsg")
        TPB = min(512, S)
        reset = sb.alloc(TPB, F32, "reset")
        prm = sb.alloc(80, F32, "prm")
        qgs = sb.alloc(1, F32, "qgs")

        P.memset("pool", identf, 1.0)
        P.asel(identf, identf, [[-1, 128]], ALU.is_equal, 0.0, 0, 1)
        P.copy("dve", ident, identf)
        P.memset("pool", M2, 1.0)
        P.asel(M2[:, 0:128], M2[:, 0:128], [[1, 128]], ALU.is_gt, 0.0, 0, -1)
        P.asel(M2[:, 128:256], M2[:, 128:256], [[1, 128]], ALU.is_ge, 0.0, 0, -1)
        P.memset("pool", MLs, 1.0)
        P.asel(MLs, MLs, [[-1, 128]], ALU.is_gt, 0.0, 0, 1)
        P.ts("dve", MGE, MLs, -1.0, 1.0, ALU.mult, ALU.add)
        P.memset("dve", bones, 0.0)
        P.memset("dve", bones[0:64, 0:64], 1.0)
        P.memset("dve", bones[64:128, 64:128], 1.0)
        P.memset("dve", zero1, 0.0)
        P.memset("dve", epsr, RMS_EPS)
        P.memset("dve", epsg, GN_EPS)
        P.memset("dve", reset, 1.0)
        P.memset("dve", reset.re("p (c n) -> p c n", n=128)[:, :, 0:1], 0.0)
        const_off = sb.off

        PM_MU, PM_W0, PM_A0, PM_KK, PM_KA, PM_RK, PM_GG, PM_GB, PM_PS, PM_QN, PM_KN = 0, 25, 31, 37, 43, 49, 55, 61, 67, 71, 72

        def load_params(l):
            sb.off = const_off
            ps.off = 0
            stg = sb.alloc(128, F32, "prm_stage")
            P.memset("dve", stg, 0.0)
            stg_w = stg

            def ld(row0, src, n):
                P.dma("sp", stg_w[row0:row0 + n, :], src)
            ld(PM_MU, mu_a[l].rearrange("(t p) -> t p", p=128), 25)
            for r0, src in ((PM_W0, w0), (PM_A0, a0), (PM_KK, k_k), (PM_KA, k_a), (PM_RK, r_k), (PM_GG, gn_g),
                            (PM_GB, gn_b)):
                ld(r0, src[l].rearrange("(t p) -> t p", p=128), 6)
            ld(PM_PS, pool_scale[l].rearrange("(t p) -> t p", p=128), 4)
            P.dma("sp", stg_w[PM_QN:PM_QN + 1, 0:64], qn_g[l:l + 1, :])
            P.dma("sp", stg_w[PM_QN:PM_QN + 1, 64:128], qn_g[l:l + 1, :])
            P.dma("sp", stg_w[PM_KN:PM_KN + 1, 0:64], kn_g[l:l + 1, :])
            P.dma("sp", stg_w[PM_KN:PM_KN + 1, 64:128], kn_g[l:l + 1, :])
            pt = ps.bank("prm_ps")
            P.mm(pt[:, 0:73], lhsT=stg[0:73, :], rhs=identf[0:73, 0:73])
            P.copy("dve", prm[:, 0:73], pt[:, 0:73])
            P.ts("dve", qgs, prm[:, PM_QN:PM_QN + 1], 0.125, None, ALU.mult)

        def phase_A(l, xsrc):
            sb.off = const_off
            ps.off = 0
            TOKP = min(2048, S)
            npass = S // TOKP
            nb = TOKP // 128
            gbc = sb.alloc(D, F32, "gbc")
            P.dma("sp", gbc, norm_g[l].partition_broadcast(128))
            hT = sb.alloc(16 * TOKP, BF16, "hT")
            hT3 = hT.re("p (k t) -> p k t", t=TOKP)
            hbufs = [Buf("hT%d" % i) for i in range(TOKP // 512)]
            xrot = Rot([sb.alloc(D, F32, "x%d" % i) for i in range(2)])
            hrot = Rot([sb.alloc(D, BF16, "h%d" % i) for i in range(2)])
            junk = sb.alloc(D, BF16, "junk")
            ssr = Rot([sb.alloc(1, F32, "ss%d" % i) for i in range(4)])
            rsr = Rot([sb.alloc(1, F32, "rs%d" % i) for i in range(4)])
            CG = 384
            wfrot = Rot([sb.alloc(16 * CG, F32, "wf%d" % i) for i in range(2)])
            wbrot = Rot([sb.alloc(16 * CG, BF16, "wb%d" % i) for i in range(2)])
            ostrot = Rot([sb.alloc(512, F32, "ost%d" % i) for i in range(3)])
            vstrot = Rot([sb.alloc(CG, BF16, "vst%d" % i) for i in range(3)])
            trps = Rot([ps.bank("trp%d" % i) for i in range(2)])
            mops = Rot([ps.bank("mo%d" % i) for i in range(4)])
            for pa in range(npass):
                t0 = pa * TOKP
                for tb in range(nb):
                    xt = xrot.next()
                    P.dma("sp", xt, xsrc[t0 + tb * 128:t0 + (tb + 1) * 128, :])
                    ss = ssr.next()
                    rs = rsr.next()
                    P.memset("pool", ss, 0.0)
                    P.act(junk, xt, AF.Square, accum=ss)
                    P.act(rs, ss, AF.Sqrt, bias=epsr, scale=1.0 / D)
                    P.recip(rs, rs)
                    hb = hrot.next()
                    P.stt(hb, xt, rs, gbc, ALU.mult, ALU.mult)
                    hbuf = hbufs[tb // 4]
                    for q in range(4):
                        tp = trps.next()
                        tpb = Tl(tp.ap.bitcast(BF16)[:, 0:512], tp.b)
                        for j in range(4):
                            kc = 4 * q + j
                            P.tr(tpb[:, j * 128:(j + 1) * 128], hb[:, kc * 128:(kc + 1) * 128], ident)
                        dst = Tl(hT3.ap[:, 4 * q:4 * q + 4, tb * 128:(tb + 1) * 128], hbuf)
                        P.copy("act" if q % 2 == 0 else "dve", dst, tpb.re("p (j t) -> p j t", t=128))
                for cg in range(IN_COLS // CG):
                    wf = wfrot.next()
                    wsrc = w_in[l, :, cg * CG:(cg + 1) * CG].rearrange("(k p) c -> p k c", p=128)
                    wf3 = wf.re("p (k c) -> p k c", c=CG)
                    P.dma("sp", wf3[:, 0:8, :], wsrc[:, 0:8, :])
                    P.dma("pool", wf3[:, 8:16, :], wsrc[:, 8:16, :])
                    wb = wbrot.next()
                    wb3 = wb.re("p (k c) -> p k c", c=CG)
                    P.copy("pool", wb3[:, 0:6, :], wf3[:, 0:6, :])
                    P.copy("dve", wb3[:, 6:16, :], wf3[:, 6:16, :])
                    if cg not in (15, 16):
                        for ci in range(3):
                            ct = cg * 3 + ci
                            for tc in range(TOKP // 512):
                                po = mops.next()
                                for k in range(16):
                                    rhs = Tl(hT3.ap[:, k, tc * 512:(tc + 1) * 512], hbufs[tc])
                                    P.mm(po, lhsT=wb3[:, k, ci * 128:(ci + 1) * 128], rhs=rhs, start=(k == 0),
                                         stop=(k == 15))
                                ost = ostrot.next()
                                P.copy("act", ost, po)
                                P.dma("act", pj[ct * 128:(ct + 1) * 128, t0 + tc * 512:t0 + (tc + 1) * 512], ost)
                    else:
                        for tb in range(nb):
                            po = mops.next()
                            for k in range(16):
                                lhsT = Tl(hT3.ap[:, k, tb * 128:(tb + 1) * 128], hbufs[tb // 4])
                                P.mm(po[:, 0:CG], lhsT=lhsT, rhs=wb3[:, k, :], start=(k == 0), stop=(k == 15))
                            vst = vstrot.next()
                            P.copy("act", vst, po[:, 0:CG])
                            P.dma("act", vtok[t0 + tb * 128:t0 + (tb + 1) * 128, (cg - 15) * CG:(cg - 14) * CG], vst)

        def phase_B(l):
            sb.off = const_off
            ps.off = 0
            TP = TPB
            NCH = TP // 128
            npiece = S // TP
            NIT = 2 * NCH
            wup = sb.alloc(RW, F32, "wup")
            P.dma("sp", wup[0:64, :], w_up[l])
            P.dma("sp", wup[64:128, :], a_up[l])
            Sf = [[sb.alloc(64, F32, "Sf%d_%d" % (hp, i)) for i in range(2)] for hp in range(6)]
            Sb = [[sb.alloc(64, BF16, "Sb%d_%d" % (hp, i)) for i in range(2)] for hp in range(6)]
            cur = [0] * 6
            for hp in range(6):
                P.memset("dve", Sf[hp][0], 0.0)
                P.memset("dve", Sb[hp][0], 0.0)

            def f32t(name, n=TP):
                return sb.alloc(n, F32, name)

            def bft(name, n=TP):
                return sb.alloc(n, BF16, name)
            lda, wa, wa2 = f32t("lda", TP + 1), f32t("wa"), f32t("wa2")
            ldr = Rot([[f32t("ld%s%d" % (nm, i), TP + 1) for nm in "rkvg"] for i in range(2)])
            tmpd = f32t("tmpd")
            rs_, ks_, vs_, gs_ = f32t("rs"), f32t("ks"), f32t("vs"), f32t("gs")
            sig, a_, cs, Winv, csm, Wm, e2, E2 = (f32t(n) for n in ("sig", "a", "cs", "Winv", "csm", "Wm", "e2", "E2"))
            kk, kk2, nrm, kkn, tmpk, kp, b_ = (f32t(n) for n in ("kk", "kk2", "nrm", "kkn", "tmpk", "kp", "b"))
            rkr, sgt = f32t("rkr"), f32t("sgt")
            yc, sq, rstd, yn = f32t("yc"), f32t("sq"), f32t("rstd"), f32t("yn")
            KH, BH = bft("KH"), bft("BH")
            sets = []
            for i in range(3):
                d = {"AR": bft("AR%d" % i, 2 * TP), "KB": bft("KB%d" % i), "BB": bft("BB%d" % i), "VB": bft("VB%d" % i),
                     "W": f32t("W%d" % i), "bon": f32t("bon%d" % i), "sg": f32t("sg%d" % i), "ysb": f32t("ysb%d" % i),
                     "SA1": [bft("SA1_%d_%d" % (i, j), 256) for j in range(NIT)],
                     "SA2": [bft("SA2_%d_%d" % (i, j), 256) for j in range(NIT)],
                     "TT": [bft("TT_%d_%d" % (i, j), 128) for j in range(NIT)]}
                d["AR4"] = d["AR"].re("p (c w n) -> p c w n", w=2, n=128)
                sets.append(d)
            yarot = Rot([bft("ya%d" % i) for i in range(2)])
            TOKr = Rot([bft("TOK%d" % i, 384) for i in range(2)])
            Xr = [Rot([bft("X%d_%d" % (i, j), 128) for j in range(2)]) for i in range(NIT)]
            XTr = [Rot([bft("XT%d_%d" % (i, j), 128) for j in range(2)]) for i in range(NIT)]
            Pr = [Rot([bft("P%d_%d" % (i, j), 128) for j in range(2)]) for i in range(NIT)]
            Zbr = Rot([bft("Zb%d" % i, 64) for i in range(2)])
            Ubr = Rot([bft("Ub%d" % i, 64) for i in range(2)])
            pgen = ps.bank("pgenP")
            pgc = ps.bank("pgenC")
            pxb = [ps.bank("pxb%d" % i) for i in range(3)]
            ppb = [ps.bank("ppb%d" % i) for i in range(2)]
            pmisc = ps.bank("pmisc")
            ptr = Tl(pmisc.ap.bitcast(BF16)[:, 0:384], pmisc.b)
            pzr = Rot([Tl(pmisc.ap[:, 192:256], pmisc.b), Tl(pmisc.ap[:, 256:320], pmisc.b)])
            pur = Rot([Tl(pmisc.ap[:, 320:384], pmisc.b), Tl(pmisc.ap[:, 384:448], pmisc.b)])
            psn = Tl(pmisc.ap[:, 448:512], pmisc.b)
            items = [(c, h) for c in range(NCH) for h in range(2)]
            TPv = "p (c n) -> p c n"

            def px_tiles(it):
                bk = pxb[(it // 2) % 3]
                o = (it % 2) * 256
                return bk[:, o:o + 128], bk[:, o + 128:o + 256]

            def prepA(pc, hp, st):
                t0 = pc * TP
                AR, AR4, KB, BB, VB, W, bon, sg = (st[k] for k in ("AR", "AR4", "KB", "BB", "VB", "W", "bon", "sg"))

                def load_shift(dst, row0, eng="sp"):
                    if t0 > 0:
                        P.dma(eng, dst, pj[row0:row0 + 128, t0 - 1:t0 + TP])
                    else:
                        P.memset("pool", dst[:, 0:1], 0.0)
                        P.dma(eng, dst[:, 1:TP + 1], pj[row0:row0 + 128, 0:TP])

                def lerp(dst, ld, mucol):
                    P.tt("pool", tmpd, ld[:, 0:TP], ld[:, 1:TP + 1], ALU.subtract)
                    P.stt(dst, tmpd, prm[:, mucol:mucol + 1], ld[:, 1:TP + 1], ALU.mult, ALU.add)
                if hp == 0:
                    load_shift(lda, 3072)
                    lerp(wa, lda, PM_MU + 24)
                    P.act(wa2[0:64, :], wa[0:64, :], AF.Tanh)
                    P.copy("pool", wa2[64:128, :], wa[64:128, :])
                lds = ldr.next()
                for i, (ld, r0) in enumerate(zip(lds, (0, 768, 1536, 2304))):
                    load_shift(ld, r0 + hp * 128, "sp" if i % 2 == 0 else "act")
                lerp(rs_, lds[0], PM_MU + hp)
                lerp(ks_, lds[1], PM_MU + 6 + hp)
                lerp(vs_, lds[2], PM_MU + 12 + hp)
                lerp(gs_, lds[3], PM_MU + 18 + hp)
                P.mm(pgen[:, 0:TP], lhsT=wup[0:64, hp * 128:(hp + 1) * 128], rhs=wa2[0:64, :])
                P.act(sig, pgen[:, 0:TP], AF.Sigmoid, bias=prm[:, PM_W0 + hp:PM_W0 + hp + 1])
                P.mm(pgen[:, 0:TP], lhsT=wup[64:128, hp * 128:(hp + 1) * 128], rhs=wa2[64:128, :])
                P.act(a_, pgen[:, 0:TP], AF.Sigmoid, bias=prm[:, PM_A0 + hp:PM_A0 + hp + 1])
                P.act(sgt, gs_, AF.Sigmoid)
                P.tt("pool", sg, sgt, gs_, ALU.mult)
                P.scan(cs, reset, sig, 0.0, ALU.mult, ALU.add)
                P.act(W, cs, AF.Exp, scale=-CDEC)
                P.act(Winv, cs, AF.Exp, scale=CDEC)
                P.tt("pool", csm, cs, sig, ALU.subtract)
                P.act(Wm, csm, AF.Exp, scale=-CDEC)
                cs3 = cs.re(TPv, n=128)
                P.tt("dve", e2.re(TPv, n=128), cs3[:, :, 127:128].bc([128, NCH, 128]), cs3, ALU.subtract)
                P.act(E2, e2, AF.Exp, scale=-CDEC)
                P.ts("dve", kk, ks_, prm[:, PM_KK + hp:PM_KK + hp + 1], None, ALU.mult)
                P.act(kk2, kk, AF.Square)
                return {"st": st, "hp": hp, "t0": t0, "X": {}, "XT": {}, "P": {}}

            def prepB(ctx):
                st, hp = ctx["st"], ctx["hp"]
                AR, AR4, KB, BB, VB, W, bon, sg = (st[k] for k in ("AR", "AR4", "KB", "BB", "VB", "W", "bon", "sg"))
                P.mm(pgen[:, 0:TP], lhsT=bones, rhs=kk2)
                P.act(nrm, pgen[:, 0:TP], AF.Sqrt)
                P.ts("dve", nrm, nrm, 1e-12, None, ALU.max)
                P.recip(nrm, nrm)
                P.tt("dve", kkn, kk, nrm, ALU.mult)
                P.ts("dve", tmpk, a_, -1.0, prm[:, PM_KA + hp:PM_KA + hp + 1], ALU.add, ALU.mult)
                P.stt(kp, tmpk, 1.0, ks_, ALU.add, ALU.mult)
                P.tt("dve", b_, kkn, a_, ALU.mult)
                P.stt(AR4[:, :, 0, :], kkn.re(TPv, n=128), -1.0, Wm.re(TPv, n=128), ALU.mult, ALU.mult)
                P.tt("pool", AR4[:, :, 1, :], rs_.re(TPv, n=128), W.re(TPv, n=128), ALU.mult)
                P.tt("dve", KH, kp, Winv, ALU.mult)
                P.tt("dve", BH, b_, Winv, ALU.mult)
                P.tt("dve", KB, kp, E2, ALU.mult)
                P.tt("pool", BB, b_, E2, ALU.mult)
                P.copy("act", VB, vs_)
                P.stt(rkr, rs_, prm[:, PM_RK + hp:PM_RK + hp + 1], kp, ALU.mult, ALU.mult)

            def prepC(ctx):
                st, hp = ctx["st"], ctx["hp"]
                AR, AR4, KB, BB, VB, W, bon, sg = (st[k] for k in ("AR", "AR4", "KB", "BB", "VB", "W", "bon", "sg"))
                SA1, SA2 = st["SA1"], st["SA2"]
                P.mm(pgen[:, 0:TP], lhsT=bones, rhs=rkr)
                P.tt("dve", bon, pgen[:, 0:TP], vs_, ALU.mult)
                for it, (c, h) in enumerate(items):
                    sl = slice(64 * h, 64 * h + 64)
                    csl = slice(c * 128, (c + 1) * 128)
                    pscb = pxb[it % 3]
                    ps1 = pscb[:, 0:256]
                    ps2 = pscb[:, 256:512]
                    arc = Tl(AR4.ap[sl, c].rearrange("p w n -> p (w n)"), AR.b)
                    P.mm(ps1, lhsT=BH[sl, csl], rhs=arc)
                    P.mm(ps2, lhsT=KH[sl, csl], rhs=arc)
                    ps3 = ppb[it // 4][:, (it % 4) * 128:(it % 4 + 1) * 128]
                    P.mm(ps3, lhsT=AR4[sl, c, 0, :], rhs=BH[sl, csl])
                    P.tt("dve", SA1[it], ps1, M2, ALU.mult)
                    P.tt("dve", SA2[it], ps2, M2, ALU.mult)
                    xt0 = XTr[it].next()
                    P.tt("dve", xt0, ps3, MLs, ALU.mult)
                    p0 = Pr[it].next()
                    P.tt("pool", p0, SA1[it][:, 0:128], ident, ALU.add)
                    ctx["X"][it], ctx["XT"][it], ctx["P"][it] = SA1[it][:, 0:128], xt0, p0

            def level(ctx, k):
                Xs, XTs, Ps, st = ctx["X"], ctx["XT"], ctx["P"], ctx["st"]
                for half in range(0, NIT, 4):
                    its = list(range(half, min(half + 4, NIT)))
                    pend = {}
                    for it in its:
                        pp = ppb[it // 4][:, (it % 4) * 128:(it % 4 + 1) * 128]
                        px, pxt = px_tiles(it)
                        if k >= 1:
                            P.mm(pp, lhsT=XTs[it], rhs=Ps[it])
                        if k < 5:
                            P.mm(px, lhsT=XTs[it], rhs=Xs[it])
                        if k < 6:
                            P.mm(pxt, lhsT=Xs[it], rhs=XTs[it])
                        pend[it] = (pp, px, pxt)
                    for it in its:
                        pp, px, pxt = pend[it]
                        if k >= 1:
                            pn_ = st["TT"][it] if k == 6 else Pr[it].next()
                            P.tt("dve", pn_, pp, Ps[it], ALU.add)
                            Ps[it] = pn_
                        if k < 5:
                            xn = Xr[it].next()
                            P.copy("act", xn, px)
                            Xs[it] = xn
                        if k < 6:
                            xtn = XTr[it].next()
                            P.copy("act", xtn, pxt)
                            XTs[it] = xtn

            def chain_step(ctx, c):
                st, hp = ctx["st"], ctx["hp"]
                AR4, KB, BB, VB, W, ysb = (st[k] for k in ("AR4", "KB", "BB", "VB", "W", "ysb"))
                SA1, SA2, Ps = st["SA1"], st["SA2"], ctx["P"]
                csl = slice(c * 128, (c + 1) * 128)
                TOK = TOKr.next()
                P.tr(ptr[:, 0:128], VB[:, csl], ident)
                P.tr(ptr[:, 128:256], KB[:, csl], ident)
                P.tr(ptr[:, 256:384], BB[:, csl], ident)
                P.copy("act", TOK, ptr)
                sfo, sbo = Sf[hp][cur[hp]], Sb[hp][cur[hp]]
                sfn, sbn = Sf[hp][1 - cur[hp]], Sb[hp][1 - cur[hp]]
                hv = []
                for h in range(2):
                    it = items.index((c, h))
                    sl = slice(64 * h, 64 * h + 64)
                    hv.append((it, sl, TOK[:, 64 * h:64 * h + 64], TOK[:, 128 + 64 * h:128 + 64 * h + 64],
                               TOK[:, 256 + 64 * h:256 + 64 * h + 64]))
                pzs, zbs, pus, ubs = [], [], [], []
                for (it, sl, vt, kbt, bbt) in hv:
                    pz = pzr.next()
                    P.mm(pz, lhsT=AR4[sl, c, 0, :], rhs=sbo[sl, :], start=True, stop=False)
                    P.mm(pz, lhsT=SA2[it][:, 0:128], rhs=vt, start=False, stop=True)
                    pzs.append(pz)
                for pz in pzs:
                    zb = Zbr.next()
                    P.copy("act", zb, pz)
                    zbs.append(zb)
                for (it, sl, vt, kbt, bbt), zb in zip(hv, zbs):
                    pu = pur.next()
                    P.mm(pu, lhsT=Ps[it], rhs=zb)
                    pus.append(pu)
                for pu in pus:
                    ub = Ubr.next()
                    P.copy("act", ub, pu)
                    ubs.append(ub)
                for (it, sl, vt, kbt, bbt), ub in zip(hv, ubs):
                    P.mm(pgc[sl, 0:128], lhsT=sbo[sl, :], rhs=AR4[sl, c, 1, :], start=True, stop=False)
                    P.mm(pgc[sl, 0:128], lhsT=ub, rhs=SA1[it][:, 128:256], start=False, stop=False)
                    P.mm(pgc[sl, 0:128], lhsT=vt, rhs=SA2[it][:, 128:256], start=False, stop=True)
                for (it, sl, vt, kbt, bbt), ub in zip(hv, ubs):
                    P.mm(psn[sl, :], lhsT=bbt, rhs=ub, start=True, stop=False)
                    P.mm(psn[sl, :], lhsT=kbt, rhs=vt, start=False, stop=True)
                P.copy("act", ysb[:, csl], pgc[:, 0:128])
                wc = W[:, c * 128 + 127:c * 128 + 128]
                P.stt(sfn, sfo, wc, psn, ALU.mult, ALU.add)
                P.stt(sbn, sfo, wc, psn, ALU.mult, ALU.add)
                cur[hp] = 1 - cur[hp]

            def post(ctx):
                st, hp, t0 = ctx["st"], ctx["hp"], ctx["t0"]
                ysb, bon, sg = st["ysb"], st["bon"], st["sg"]
                P.mm(pgc[:, 0:TP], lhsT=bones, rhs=ysb)
                P.stt(yc, pgc[:, 0:TP], -1.0 / 64, ysb, ALU.mult, ALU.add)
                P.act(sq, yc, AF.Square)
                P.mm(pgc[:, 0:TP], lhsT=bones, rhs=sq)
                P.act(rstd, pgc[:, 0:TP], AF.Sqrt, bias=epsg, scale=1.0 / 64)
                P.recip(rstd, rstd)
                P.tt("pool", yn, yc, rstd, ALU.mult)
                P.ts("dve", yn, yn, prm[:, PM_GG + hp:PM_GG + hp + 1], prm[:, PM_GB + hp:PM_GB + hp + 1], ALU.mult,
                     ALU.add)
                P.tt("pool", yn, yn, bon, ALU.add)
                ya = yarot.next()
                P.tt("dve", ya, yn, sg, ALU.mult)
                P.dma("sp", ymix[hp * 128:(hp + 1) * 128, t0:t0 + TP], ya)

            units = [(pc, hp) for pc in range(npiece) for hp in range(6)]

            def start_unit(ui):
                pc, hp = units[ui]
                return prepA(pc, hp, sets[ui % 3])
            ctxs = {0: start_unit(0)}
            prepB(ctxs[0])
            prepC(ctxs[0])
            for ui in range(len(units)):
                ctx = ctxs[ui]
                prev = ctxs.get(ui - 1)
                nxt = None
                for k in range(7):
                    level(ctx, k)
                    if k == 0 and ui + 1 < len(units):
                        nxt = ctxs[ui + 1] = start_unit(ui + 1)
                    if prev is not None and 1 <= k <= 4 and (k - 1) < NCH:
                        chain_step(prev, k - 1)
                    if k == 2 and nxt is not None:
                        prepB(nxt)
                    if k == 5 and prev is not None:
                        for c in range(4, NCH):
                            chain_step(prev, c)
                        post(prev)
                    if k == 6 and nxt is not None:
                        prepC(nxt)
                ctxs.pop(ui - 1, None)
            last = ctxs[len(units) - 1]
            for c in range(NCH):
                chain_step(last, c)
            post(last)

        def phase_C(l):
            sb.off = const_off
            ps.off = 0
            TP = min(1024, S)
            npiece = S // TP
            H = 16
            pwf = sb.alloc(512, F32, "pwf")
            pwb = sb.alloc(512, BF16, "pwb")
            P.dma("sp", pwf.re("p (g d) -> p g d", d=128), pool_w[l].rearrange("g c d -> c g d"))
            P.copy("dve", pwb, pwf)
            corr = sb.alloc(4 * 16, F32, "corr")
            for g, win in enumerate((2, 4, 8, 16)):
                for t in range(16):
                    P.memset("pool", corr[:, g * 16 + t:g * 16 + t + 1], float(win) / min(t + 1, win))
            urot = Rot([sb.alloc(TP + H, F32, "pu%d" % i) for i in range(2)])
            grot = Rot([sb.alloc(TP, F32, "pg%d" % i) for i in range(2)])
            sA = sb.alloc(TP + H, F32, "sA")
            sB = sb.alloc(TP + H, F32, "sB")
            dbf = sb.alloc(TP, BF16, "dbf")
            sgp = sb.alloc(TP, F32, "sgp")
            ybr = Rot([sb.alloc(TP, BF16, "yb%d" % i) for i in range(2)])
            pyr = Rot([ps.bank("pc%d" % i) for i in range(2)])
            for pc in range(npiece):
                t0 = pc * TP
                for g, win in enumerate((2, 4, 8, 16)):
                    u = urot.next()
                    row = A_COLS + g * 128
                    if t0 > 0:
                        P.dma("sp", u, pj[row:row + 128, t0 - H:t0 + TP])
                    else:
                        P.memset("pool", u[:, 0:H], 0.0)
                        P.dma("sp", u[:, H:H + TP], pj[row:row + 128, 0:TP])
                    pg_ = grot.next()
                    P.dma("act", pg_, pj[A_COLS + PW + g * 128:A_COLS + PW + (g + 1) * 128, t0:t0 + TP])
                    src = u
                    sh = 1
                    dsts = [sA, sB]
                    lo = 0
                    for lev in range(g + 1):
                        dst = dsts[lev % 2]
                        lo += sh
                        P.tt("pool" if lev % 2 else "dve", dst[:, lo:TP + H], src[:, lo:TP + H], src[:, lo - sh:TP + H - sh],
                             ALU.add)
                        src = dst
                        sh *= 2
                    ssum = src
                    if t0 == 0:
                        P.tt("dve", ssum[:, H:H + 16], ssum[:, H:H + 16], corr[:, g * 16:(g + 1) * 16], ALU.mult)
                    P.stt(dbf, ssum[:, H:H + TP], 1.0 / win, u[:, H:H + TP], ALU.mult, ALU.subtract)
                    P.act(sgp, pg_, AF.Silu)
                    yb = ybr.next()
                    for hf in range(TP // 512):
                        py = pyr.next()
                        P.mm(py, lhsT=pwb[:, g * 128:(g + 1) * 128], rhs=dbf[:, hf * 512:(hf + 1) * 512])
                        P.stt(yb[:, hf * 512:(hf + 1) * 512], py, prm[:, PM_PS + g:PM_PS + g + 1],
                              sgp[:, hf * 512:(hf + 1) * 512], ALU.mult, ALU.mult)
                    P.dma("sp", ymix[RW + g * 128:RW + (g + 1) * 128, t0:t0 + TP], yb)

        def phase_D(l):
            sb.off = const_off
            ps.off = 0
            TPq = min(1024, S)
            QH = sb.alloc(S, BF16, "QH")
            KHt = sb.alloc(S, BF16, "KHt")
            V = sb.alloc(NB * 128, BF16, "V")
            V3 = V.re("p (n c) -> p n c", c=128)
            sgc = sb.alloc(S, BF16, "sgc")
            ycst = sb.alloc(S, BF16, "ycst")
            SGr = Rot([sb.alloc(S, F32, "SG%d" % i) for i in range(3)])
            EZr = Rot([sb.alloc(S, BF16, "EZ%d" % i) for i in range(3)])
            Pnr = Rot([sb.alloc(S, F32, "Pn%d" % i) for i in range(2)])
            Wtr = Rot([sb.alloc(S, BF16, "Wt%d" % i) for i in range(2)])
            WTr = Rot([sb.alloc(S, BF16, "WT%d" % i) for i in range(2)])
            qf = Rot([sb.alloc(TPq, F32, "qf%d" % i) for i in range(2)])
            sqqr = Rot([sb.alloc(TPq, F32, "sqq%d" % i) for i in range(2)])
            rtr = Rot([sb.alloc(TPq, F32, "rt%d" % i) for i in range(2)])
            zr = Rot([ps.bank("z%d" % i) for i in range(3)])
            ptwr = Rot([ps.bank("ptw%d" % i) for i in range(2)])
            por = Rot([ps.bank("po%d" % i) for i in range(2)])
            pgen = ps.bank("pgenD")
            for hp in range(6):
                P.dma("sp", V3, vtok[:, hp * 128:(hp + 1) * 128].rearrange("(n p) c -> p n c", p=128))
                for (dst, row0, gcol) in ((QH, C0 + hp * 128, qgs), (KHt, C0 + SW + hp * 128, prm[:, PM_KN:PM_KN + 1])):
                    for pc in range(S // TPq):
                        q = qf.next()
                        P.dma("sp", q, pj[row0:row0 + 128, pc * TPq:(pc + 1) * TPq])
                        sqq = sqqr.next()
                        rt = rtr.next()
                        P.act(sqq, q, AF.Square)
                        for hf in range(TPq // 512):
                            hs = slice(hf * 512, (hf + 1) * 512)
                            P.mm(pgen, lhsT=bones, rhs=sqq[:, hs])
                            P.act(rt[:, hs], pgen, AF.Sqrt, bias=epsr, scale=1.0 / 64)
                        P.recip(rt, rt)
                        P.stt(dst[:, pc * TPq:(pc + 1) * TPq], q, gcol, rt, ALU.mult, ALU.mult)
                for pc in range(S // TPq):
                    q = qf.next()
                    r0 = C0 + 3 * SW + hp * 128
                    P.dma("act", q, pj[r0:r0 + 128, pc * TPq:(pc + 1) * TPq])
                    P.act(sgc[:, pc * TPq:(pc + 1) * TPq], q, AF.Silu)
                items = [(T, h) for T in range(NB) for h in range(2)]
                state = {}

                def stage1a(T, h):
                    sl = slice(64 * h, 64 * h + 64)
                    kend = (T + 1) * 128
                    SG = SGr.next()
                    EZ = EZr.next()
                    for ck in range((kend + 511) // 512):
                        w_ = min(512, kend - ck * 512)
                        pz = zr.next()
                        P.mm(pz[:, 0:w_], lhsT=QH[sl, T * 128:(T + 1) * 128], rhs=KHt[sl, ck * 512:ck * 512 + w_])
                        P.act(SG[:, ck * 512:ck * 512 + w_], pz[:, 0:w_], AF.Sigmoid, scale=-1.0)
                        P.act(EZ[:, ck * 512:ck * 512 + w_], pz[:, 0:w_], AF.Sigmoid)
                    dsl = slice(T * 128, kend)
                    P.tt("pool", SG[:, dsl], SG[:, dsl], MLs, ALU.mult)
                    P.tt("pool", SG[:, dsl], SG[:, dsl], MGE, ALU.add)
                    P.tt("pool", EZ[:, dsl], EZ[:, dsl], MLs, ALU.mult)
                    state[("a", T, h)] = (SG, EZ)

                def stage1b(T, h):
                    kend = (T + 1) * 128
                    SG, EZ = state.pop(("a", T, h))
                    Pn = Pnr.next()
                    P.scan(Pn[:, 0:kend][:, ::-1], SG[:, 0:kend][:, ::-1], zero1.bc([128, kend]), 1.0, ALU.mult, ALU.add)
                    state[("b", T, h)] = (EZ, Pn)

                def stage1c(T, h):
                    kend = (T + 1) * 128
                    EZ, Pn = state.pop(("b", T, h))
                    Wt = Wtr.next()
                    npl = ((kend - 1) * 3 // 8) // 64 * 64
                    if npl > 0:
                        P.tt("pool", Wt[:, 0:npl], EZ[:, 0:npl], Pn[:, 1:npl + 1], ALU.mult)
                    P.tt("dve", Wt[:, npl:kend - 1], EZ[:, npl:kend - 1], Pn[:, npl + 1:kend], ALU.mult)
                    P.memset("pool", Wt[:, kend - 1:kend], 0.0)
                    state[(T, h)] = Wt

                def stage2(T, h):
                    sl = slice(64 * h, 64 * h + 64)
                    Wt = state.pop((T, h))
                    WT = WTr.next()
                    nsb = T + 1
                    for g0 in range(0, nsb, 4):
                        n = min(4, nsb - g0)
                        pt = ptwr.next()
                        ptb = Tl(pt.ap.bitcast(BF16)[:, 0:512], pt.b)
                        for j in range(n):
                            P.tr(ptb[:, j * 128:(j + 1) * 128], Wt[:, (g0 + j) * 128:(g0 + j + 1) * 128], ident)
                        P.copy("act", WT[:, g0 * 128:(g0 + n) * 128], ptb[:, 0:n * 128])
                    if h == 0:
                        state["po"] = por.next()
                    po = state["po"]
                    for sbk in range(nsb):
                        P.mm(po[sl, 0:128], lhsT=V3[:, sbk, 64 * h:64 * h + 64], rhs=WT[:, sbk * 128:(sbk + 1) * 128],
                             start=(sbk == 0), stop=(sbk == nsb - 1))
                    if h == 1:
                        P.tt("dve", ycst[:, T * 128:(T + 1) * 128], po[:, 0:128], sgc[:, T * 128:(T + 1) * 128], ALU.mult)

                n_it = len(items)
                for i in range(n_it + 3):
                    if i < n_it:
                        stage1a(*items[i])
                    if 0 <= i - 1 < n_it:
                        stage1b(*items[i - 1])
                    if 0 <= i - 2 < n_it:
                        stage1c(*items[i - 2])
                    if 0 <= i - 3 < n_it:
                        stage2(*items[i - 3])
                P.dma("sp", ymix[RW + PW + hp * 128:RW + PW + (hp + 1) * 128, :], ycst)

        def phase_E(l, xsrc):
            sb.off = const_off
            ps.off = 0
            wo = sb.alloc(16 * D, BF16, "wo")
            wo3 = wo.re("p (k c) -> p k c", c=D)
            wst = Rot([sb.alloc(4 * D, F32, "wst%d" % i) for i in range(2)])
            for kq in range(4):
                w = wst.next()
                w3 = w.re("p (k c) -> p k c", c=D)
                P.dma("sp" if kq % 2 == 0 else "act", w3,
                      w_out[l, kq * 512:(kq + 1) * 512, :].rearrange("(k p) c -> p k c", p=128))
                P.copy("pool", wo3[:, kq * 4:kq * 4 + 2, :], w3[:, 0:2, :])
                P.copy("dve", wo3[:, kq * 4 + 2:kq * 4 + 4, :], w3[:, 2:4, :])
            TQ = 512
            yr = Rot([sb.alloc(16 * TQ, BF16, "ymx%d" % i) for i in range(2)])
            xr = Rot([sb.alloc(D, F32, "xe%d" % i) for i in range(3)])
            pr = Rot([ps.bank("pe%d" % i) for i in range(8)])
            for tq in range(S // TQ):
                ym = yr.next()
                ym3 = ym.re("p (k t) -> p k t", t=TQ)
                P.dma("sp", ym3, ymix[:, tq * TQ:(tq + 1) * TQ].rearrange("(k p) t -> p k t", p=128))
                for tb in range(TQ // 128):
                    r0 = tq * TQ + tb * 128
                    xe = xr.next()
                    P.dma("sp", xe, xsrc[r0:r0 + 128, :])
                    for cq in range(4):
                        po = pr.next()
                        for k in range(16):
                            P.mm(po, lhsT=ym3[:, k, tb * 128:(tb + 1) * 128], rhs=wo3[:, k, cq * 512:(cq + 1) * 512],
                                 start=(k == 0), stop=(k == 15))
                        P.tt("dve", xe[:, cq * 512:(cq + 1) * 512], po, xe[:, cq * 512:(cq + 1) * 512], ALU.add)
                    P.dma("act", out[r0:r0 + 128, :], xe)

        for l in range(L):
            xsrc = x_in if l == 0 else out
            load_params(l)
            P.barrier()
            phase_A(l, xsrc)
            P.barrier()
            phase_B(l)
            P.barrier()
            phase_C(l)
            P.barrier()
            phase_D(l)
            P.barrier()
            phase_E(l, xsrc)
            P.barrier()
        P.emit()
    import os
    if os.environ.get("KSTATS"):
        print("op counts", {e: len(P.ops[e]) for e in ENGS}, flush=True)
    return nc


_CACHE = {}


def run(inputs, S, L, n_cores, dbg=False, trace=False):
    key = (S, L, dbg)
    if key not in _CACHE:
        _CACHE[key] = build(S, L, dbg)
    nc = _CACHE[key]
    x = np.ascontiguousarray(np.asarray(inputs["x"], dtype=np.float32))
    B = x.shape[0]
    shared = {}
    for k, v in inputs.items():
        if k == "x":
            continue
        a = np.ascontiguousarray(np.asarray(v, dtype=np.float32))
        if k == "r_k":
            a = a.reshape(a.shape[0], -1)
        shared[k] = a
    in_maps = []
    for c in range(n_cores):
        m = dict(shared)
        m["x"] = x[c % B]
        in_maps.append(m)
    res = run_bass_kernel_spmd(nc, in_maps, core_ids=list(range(n_cores)))
    return res


def kernel(**inputs):
    x = np.asarray(inputs["x"])
    B, S, _ = x.shape
    L = np.asarray(inputs["norm_g"]).shape[0]
    res = run(inputs, S, L, 8)
    return np.stack([np.asarray(res.results[b]["out"], dtype=np.float32) for b in range(B)], axis=0)
```

```python
import contextlib
import math
import numpy as np
import concourse.bass as bass
import concourse.mybir as mybir
from concourse.bass_utils import run_bass_kernel_spmd

F32 = mybir.dt.float32
BF16 = mybir.dt.bfloat16
ALU = mybir.AluOpType
AF = mybir.ActivationFunctionType

ENGS = ("pe", "act", "dve", "pool", "sp")

D = 2048
RW = 768
PW = 512
SW = 768
A_COLS = 4 * RW + 128
B_COLS = 2 * PW
C_COLS = 4 * SW
IN_COLS = A_COLS + B_COLS + C_COLS
C0 = A_COLS + B_COLS
RMS_EPS = 1e-6
GN_EPS = 64e-5
CDEC = math.exp(-0.5)


class Buf:
    __slots__ = ("name", "w", "r", "rd", "excl")

    def __init__(self, name="", excl=False):
        self.name = name
        self.excl = excl
        self.w = None
        self.r = {}
        self.rd = []


class Op:
    __slots__ = ("eng", "fn", "dma", "deps", "seq", "needs_inc", "sem", "val", "prev_val", "xw")

    def __init__(self, eng, fn, dma):
        self.eng = eng
        self.fn = fn
        self.dma = dma
        self.deps = []
        self.needs_inc = False
        self.seq = None
        self.sem = None
        self.val = None
        self.prev_val = 0
        self.xw = None


class Tl:
    __slots__ = ("ap", "b")

    def __init__(self, ap, b):
        self.ap = ap
        self.b = b

    def __getitem__(self, k):
        return Tl(self.ap[k], self.b)

    def re(self, pat, **kw):
        return Tl(self.ap.rearrange(pat, **kw), self.b)

    def bc(self, shape):
        return Tl(self.ap.broadcast_to(shape), self.b)

    def wb(self, b):
        return Tl(self.ap, b)


def _ap(x):
    return x.ap if isinstance(x, Tl) else x


def _bufs(*xs):
    return [x.b for x in xs if isinstance(x, Tl) and x.b is not None]


class Prog:
    def __init__(self, nc, n_dma_sems=8):
        self.nc = nc
        self.ops = {e: [] for e in ENGS}
        self.n_dma_sems = n_dma_sems
        self.dma_rr = {e: 0 for e in ENGS}
        self.dma_cnt = {}

    def _dep(self, op, p, kind):
        if p is None or p is op:
            return
        if (not p.dma) and (not op.dma) and p.eng == op.eng:
            if kind != "raw" or op.eng == "pe":
                return
        op.deps.append(p)
        if not p.dma:
            p.needs_inc = True

    def op(self, eng, fn, reads=(), writes=(), dma=False):
        o = Op(eng, fn, dma)
        writes = list(writes) + [b for b in reads if b.excl and b not in writes]
        reads = [b for b in reads if not b.excl]
        for b in reads:
            self._dep(o, b.w, "raw")
        for b in writes:
            self._dep(o, b.w, "waw")
            for p in b.r.values():
                self._dep(o, p, "war")
            for p in b.rd:
                self._dep(o, p, "war")
        for b in reads:
            if dma:
                b.rd.append(o)
            else:
                b.r[eng] = o
        for b in writes:
            b.w = o
            b.r = {}
            b.rd = []
        if dma:
            k = (eng, self.dma_rr[eng] % self.n_dma_sems)
            self.dma_rr[eng] += 1
            o.sem = k
            o.prev_val = self.dma_cnt.get(k, 0)
            o.val = o.prev_val + 16
            self.dma_cnt[k] = o.val
        self.ops[eng].append(o)
        return o

    def barrier(self):
        lasts = []
        for e in ENGS:
            for o in reversed(self.ops[e]):
                if not o.dma and o.fn is not None:
                    lasts.append(o)
                    break
        snap = dict(self.dma_cnt)
        for e in ENGS:
            o = Op(e, None, False)
            for p in lasts:
                if p.eng != e:
                    o.deps.append(p)
                    p.needs_inc = True
            o.xw = snap
            self.ops[e].append(o)

    def dma(self, eng, out, in_, **kw):
        o_, i_ = _ap(out), _ap(in_)
        return self.op(eng, lambda e: e.dma_start(out=o_, in_=i_, **kw), _bufs(in_), _bufs(out), dma=True)

    def mm(self, out, lhsT, rhs, start=True, stop=True):
        o_, l_, r_ = _ap(out), _ap(lhsT), _ap(rhs)
        return self.op("pe", lambda e: e.matmul(o_, lhsT=l_, rhs=r_, start=start, stop=stop),
                       _bufs(lhsT, rhs), _bufs(out))

    def tr(self, out, in_, ident):
        o_, i_, d_ = _ap(out), _ap(in_), _ap(ident)
        return self.op("pe", lambda e: e.transpose(out=o_, in_=i_, identity=d_), _bufs(in_, ident), _bufs(out))

    def act(self, out, in_, func, bias=None, scale=1.0, accum=None):
        o_, i_ = _ap(out), _ap(in_)
        kw = {"scale": _ap(scale)}
        if bias is not None:
            kw["bias"] = _ap(bias)
        if accum is not None:
            kw["accum_out"] = _ap(accum)
        return self.op("act", lambda e: e.activation(out=o_, in_=i_, func=func, **kw),
                       _bufs(in_, bias, scale), _bufs(out, accum))

    def tt(self, eng, out, in0, in1, op):
        o_, a_, b_ = _ap(out), _ap(in0), _ap(in1)
        return self.op(eng, lambda e: e.tensor_tensor(out=o_, in0=a_, in1=b_, op=op), _bufs(in0, in1), _bufs(out))

    def ts(self, eng, out, in0, s1, s2, op0, op1=None):
        o_, a_, s1_, s2_ = _ap(out), _ap(in0), _ap(s1), _ap(s2)
        if op1 is None:
            fn = lambda e: e.tensor_scalar(out=o_, in0=a_, scalar1=s1_, scalar2=None, op0=op0)
        else:
            fn = lambda e: e.tensor_scalar(out=o_, in0=a_, scalar1=s1_, scalar2=s2_, op0=op0, op1=op1)
        return self.op(eng, fn, _bufs(in0, s1, s2), _bufs(out))

    def stt(self, out, in0, scalar, in1, op0, op1):
        o_, a_, s_, b_ = _ap(out), _ap(in0), _ap(scalar), _ap(in1)
        return self.op("dve", lambda e: e.scalar_tensor_tensor(out=o_, in0=a_, scalar=s_, in1=b_, op0=op0, op1=op1),
                       _bufs(in0, scalar, in1), _bufs(out))

    def copy(self, eng, out, in_):
        o_, i_ = _ap(out), _ap(in_)
        if eng == "act":
            fn = lambda e: e.copy(out=o_, in_=i_)
        else:
            fn = lambda e: e.tensor_copy(out=o_, in_=i_)
        return self.op(eng, fn, _bufs(in_), _bufs(out))

    def recip(self, out, in_):
        o_, i_ = _ap(out), _ap(in_)
        return self.op("dve", lambda e: e.reciprocal(out=o_, in_=i_), _bufs(in_), _bufs(out))

    def scan(self, out, d0, d1, init, op0, op1):
        o_, a_, b_, i_ = _ap(out), _ap(d0), _ap(d1), _ap(init)
        return self.op("dve", lambda e: e.tensor_tensor_scan(out=o_, data0=a_, data1=b_, initial=i_, op0=op0, op1=op1),
                       _bufs(d0, d1, init), _bufs(out))

    def memset(self, eng, out, val):
        o_ = _ap(out)
        return self.op(eng, lambda e: e.memset(o_, val), (), _bufs(out))

    def asel(self, out, in_, pattern, cmp, fill, base, cm):
        o_, i_ = _ap(out), _ap(in_)
        return self.op("pool", lambda e: e.affine_select(out=o_, in_=i_, pattern=pattern, compare_op=cmp, fill=fill,
                                                          base=base, channel_multiplier=cm), _bufs(in_), _bufs(out))

    def emit(self):
        nc = self.nc
        with contextlib.ExitStack() as st:
            esem = {e: st.enter_context(nc.semaphore("s_" + e)) for e in ENGS}
            dsem = {}
            for k in self.dma_cnt:
                dsem[k] = st.enter_context(nc.semaphore("d_%s_%d" % k))
            for e in ENGS:
                n = 0
                for o in self.ops[e]:
                    if o.needs_inc:
                        n += 1
                        o.seq = n
            final_waits = dict(self.dma_cnt)
            block = st.enter_context(nc.Block())

            def run(e, eng):
                waited = {}

                def wait(sem, key, val):
                    if waited.get(key, 0) >= val:
                        return
                    eng.wait_ge(sem, val)
                    waited[key] = val

                for o in self.ops[e]:
                    for p in o.deps:
                        if p.dma:
                            wait(dsem[p.sem], p.sem, p.val)
                        else:
                            wait(esem[p.eng], p.eng, p.seq)
                    if o.xw:
                        for k, v in o.xw.items():
                            wait(dsem[k], k, v)
                    if o.fn is None:
                        continue
                    if o.dma:
                        if o.prev_val:
                            wait(dsem[o.sem], o.sem, o.prev_val)
                        o.fn(eng).then_inc(dsem[o.sem], 16)
                    else:
                        ins = o.fn(eng)
                        if o.needs_inc:
                            ins.then_inc(esem[e], 1)
                if e == "sp":
                    for k, v in final_waits.items():
                        wait(dsem[k], k, v)

            @block.tensor
            def _(eng):
                run("pe", eng)

            @block.scalar
            def _(eng):
                run("act", eng)

            @block.vector
            def _(eng):
                run("dve", eng)

            @block.gpsimd
            def _(eng):
                run("pool", eng)

            @block.sync
            def _(eng):
                run("sp", eng)


class Arena:
    def __init__(self, t, nbytes):
        self.t = t
        self.nbytes = nbytes
        self.off = 0

    def alloc(self, cols, dtype=F32, name="", align=4):
        sz = cols * (2 if dtype == BF16 else 4)
        sz = (sz + 3) // 4 * 4
        self.off = (self.off + align - 1) // align * align
        a = self.off
        self.off += sz
        assert self.off <= self.nbytes, ("arena overflow", name, self.off, self.nbytes)
        ap = self.t[:, a // 4:(a + sz) // 4]
        if dtype == BF16:
            ap = ap.bitcast(BF16)[:, 0:cols]
        return Tl(ap, Buf(name))

    def bank(self, name=""):
        t = self.alloc(512, F32, name, align=2048)
        t.b.excl = True
        return t


class Rot:
    def __init__(self, items):
        self.items = items
        self.i = 0

    def next(self):
        x = self.items[self.i % len(self.items)]
        self.i += 1
        return x


def build(S, L, dbg=False):
    nc = bass.Bass("TRN2", target_bir_lowering=False)
    P = Prog(nc)
    NB = S // 128

    def din(name, shape):
        return nc.dram_tensor(name, shape, F32, kind="ExternalInput").ap()

    x_in = din("x", [S, D])
    norm_g = din("norm_g", [L, D])
    w_in = din("w_in", [L, D, IN_COLS])
    mu_a = din("mu_a", [L, A_COLS])
    w_up = din("w_up", [L, 64, RW])
    w0 = din("w0", [L, RW])
    a_up = din("a_up", [L, 64, RW])
    a0 = din("a0", [L, RW])
    k_k = din("k_k", [L, RW])
    k_a = din("k_a", [L, RW])
    r_k = din("r_k", [L, RW])
    gn_g = din("gn_g", [L, RW])
    gn_b = din("gn_b", [L, RW])
    pool_w = din("pool_w", [L, 4, 128, 128])
    pool_scale = din("pool_scale", [L, PW])
    qn_g = din("qn_g", [L, 64])
    kn_g = din("kn_g", [L, 64])
    w_out = din("w_out", [L, D, D])
    out = nc.dram_tensor("out", [S, D], F32, kind="ExternalOutput").ap()
    skind = "ExternalOutput" if dbg else "Internal"
    dbg_outs = {}

    def dump(name, tile, cols, dt=F32):
        if not dbg or name in dbg_outs:
            return
        dbg_outs[name] = nc.dram_tensor("dbg_" + name, [128, cols], dt, kind="ExternalOutput").ap()
        P.dma("sp", dbg_outs[name], tile)
    pj = nc.dram_tensor("pj", [IN_COLS, S], F32, kind=skind).ap()
    vtok = nc.dram_tensor("vtok", [S, SW], BF16, kind=skind).ap()
    ymix = nc.dram_tensor("ymix", [D, S], BF16, kind=skind).ap()

    with contextlib.ExitStack() as st:
        SB_BYTES = 206 * 1024
        sb_t = st.enter_context(nc.sbuf_tensor("arena", [128, SB_BYTES // 4], F32))
        ps_t = st.enter_context(nc.psum_tensor("psarena", [128, 4096], F32))
        sb = Arena(sb_t, SB_BYTES)
        ps = Arena(ps_t, 16384)

        identf = sb.alloc(128, F32, "identf")
        ident = sb.alloc(128, BF16, "ident")
        M2 = sb.alloc(256, F32, "M2")
        MLs = sb.alloc(128, F32, "MLs")
        MGE = sb.alloc(128, F32, "MGE")
        bones = sb.alloc(128, F32, "bones")
        zero1 = sb.alloc(1, F32, "zero1")
        epsr = sb.alloc(1, F32, "epsr")
        epsg = sb.alloc(1, F32, "epsg")
        TPB = min(512, S)
        reset = sb.alloc(TPB, F32, "reset")
        prm = sb.alloc(80, F32, "prm")
        qgs = sb.alloc(1, F32, "qgs")

        P.memset("pool", identf, 1.0)
        P.asel(identf, identf, [[-1, 128]], ALU.is_equal, 0.0, 0, 1)
        P.copy("dve", ident, identf)
        P.memset("pool", M2, 1.0)
        P.asel(M2[:, 0:128], M2[:, 0:128], [[1, 128]], ALU.is_gt, 0.0, 0, -1)
        P.asel(M2[:, 128:256], M2[:, 128:256], [[1, 128]], ALU.is_ge, 0.0, 0, -1)
        P.memset("pool", MLs, 1.0)
        P.asel(MLs, MLs, [[-1, 128]], ALU.is_gt, 0.0, 0, 1)
        P.ts("dve", MGE, MLs, -1.0, 1.0, ALU.mult, ALU.add)
        P.memset("dve", bones, 0.0)
        P.memset("dve", bones[0:64, 0:64], 1.0)
        P.memset("dve", bones[64:128, 64:128], 1.0)
        P.memset("dve", zero1, 0.0)
        P.memset("dve", epsr, RMS_EPS)
        P.memset("dve", epsg, GN_EPS)
        P.memset("dve", reset, 1.0)
        P.memset("dve", reset.re("p (c n) -> p c n", n=128)[:, :, 0:1], 0.0)
        const_off = sb.off

        PM_MU, PM_W0, PM_A0, PM_KK, PM_KA, PM_RK, PM_GG, PM_GB, PM_PS, PM_QN, PM_KN = 0, 25, 31, 37, 43, 49, 55, 61, 67, 71, 72

        def load_params(l):
            sb.off = const_off
            ps.off = 0
            stg = sb.alloc(128, F32, "prm_stage")
            P.memset("dve", stg, 0.0)
            stg_w = stg

            def ld(row0, src, n):
                P.dma("sp", stg_w[row0:row0 + n, :], src)
            ld(PM_MU, mu_a[l].rearrange("(t p) -> t p", p=128), 25)
            for r0, src in ((PM_W0, w0), (PM_A0, a0), (PM_KK, k_k), (PM_KA, k_a), (PM_RK, r_k), (PM_GG, gn_g),
                            (PM_GB, gn_b)):
                ld(r0, src[l].rearrange("(t p) -> t p", p=128), 6)
            ld(PM_PS, pool_scale[l].rearrange("(t p) -> t p", p=128), 4)
            P.dma("sp", stg_w[PM_QN:PM_QN + 1, 0:64], qn_g[l:l + 1, :])
            P.dma("sp", stg_w[PM_QN:PM_QN + 1, 64:128], qn_g[l:l + 1, :])
            P.dma("sp", stg_w[PM_KN:PM_KN + 1, 0:64], kn_g[l:l + 1, :])
            P.dma("sp", stg_w[PM_KN:PM_KN + 1, 64:128], kn_g[l:l + 1, :])
            pt = ps.bank("prm_ps")
            P.mm(pt[:, 0:73], lhsT=stg[0:73, :], rhs=identf[0:73, 0:73])
            P.copy("dve", prm[:, 0:73], pt[:, 0:73])
            P.ts("dve", qgs, prm[:, PM_QN:PM_QN + 1], 0.125, None, ALU.mult)

        def phase_A(l, xsrc):
            sb.off = const_off
            ps.off = 0
            TOKP = min(2048, S)
            npass = S // TOKP
            nb = TOKP // 128
            gbc = sb.alloc(D, F32, "gbc")
            P.dma("sp", gbc, norm_g[l].partition_broadcast(128))
            hT = sb.alloc(16 * TOKP, BF16, "hT")
            hT3 = hT.re("p (k t) -> p k t", t=TOKP)
            hbufs = [Buf("hT%d" % i) for i in range(TOKP // 512)]
            xrot = Rot([sb.alloc(D, F32, "x%d" % i) for i in range(2)])
            hrot = Rot([sb.alloc(D, BF16, "h%d" % i) for i in range(2)])
            junk = sb.alloc(D, BF16, "junk")
            ssr = Rot([sb.alloc(1, F32, "ss%d" % i) for i in range(4)])
            rsr = Rot([sb.alloc(1, F32, "rs%d" % i) for i in range(4)])
            CG = 384
            wfrot = Rot([sb.alloc(16 * CG, F32, "wf%d" % i) for i in range(2)])
            wbrot = Rot([sb.alloc(16 * CG, BF16, "wb%d" % i) for i in range(2)])
            ostrot = Rot([sb.alloc(512, F32, "ost%d" % i) for i in range(3)])
            vstrot = Rot([sb.alloc(CG, BF16, "vst%d" % i) for i in range(3)])
            trps = Rot([ps.bank("trp%d" % i) for i in range(2)])
            mops = Rot([ps.bank("mo%d" % i) for i in range(4)])
            for pa in range(npass):
                t0 = pa * TOKP
                for tb in range(nb):
                    xt = xrot.next()
                    P.dma("sp", xt, xsrc[t0 + tb * 128:t0 + (tb + 1) * 128, :])
                    ss = ssr.next()
                    rs = rsr.next()
                    P.memset("pool", ss, 0.0)
                    P.act(junk, xt, AF.Square, accum=ss)
                    P.act(rs, ss, AF.Sqrt, bias=epsr, scale=1.0 / D)
                    P.recip(rs, rs)
                    hb = hrot.next()
                    P.stt(hb, xt, rs, gbc, ALU.mult, ALU.mult)
                    hbuf = hbufs[tb // 4]
                    for q in range(4):
                        tp = trps.next()
                        tpb = Tl(tp.ap.bitcast(BF16)[:, 0:512], tp.b)
                        for j in range(4):
                            kc = 4 * q + j
                            P.tr(tpb[:, j * 128:(j + 1) * 128], hb[:, kc * 128:(kc + 1) * 128], ident)
                        dst = Tl(hT3.ap[:, 4 * q:4 * q + 4, tb * 128:(tb + 1) * 128], hbuf)
                        P.copy("act" if q % 2 == 0 else "dve", dst, tpb.re("p (j t) -> p j t", t=128))
                for cg in range(IN_COLS // CG):
                    wf = wfrot.next()
                    wsrc = w_in[l, :, cg * CG:(cg + 1) * CG].rearrange("(k p) c -> p k c", p=128)
                    wf3 = wf.re("p (k c) -> p k c", c=CG)
                    P.dma("sp", wf3[:, 0:8, :], wsrc[:, 0:8, :])
                    P.dma("pool", wf3[:, 8:16, :], wsrc[:, 8:16, :])
                    wb = wbrot.next()
                    wb3 = wb.re("p (k c) -> p k c", c=CG)
                    P.copy("pool", wb3[:, 0:6, :], wf3[:, 0:6, :])
                    P.copy("dve", wb3[:, 6:16, :], wf3[:, 6:16, :])
                    if cg not in (15, 16):
                        for ci in range(3):
                            ct = cg * 3 + ci
                            for tc in range(TOKP // 512):
                                po = mops.next()
                                for k in range(16):
                                    rhs = Tl(hT3.ap[:, k, tc * 512:(tc + 1) * 512], hbufs[tc])
                                    P.mm(po, lhsT=wb3[:, k, ci * 128:(ci + 1) * 128], rhs=rhs, start=(k == 0),
                                         stop=(k == 15))
                                ost = ostrot.next()
                                P.copy("act", ost, po)
                                P.dma("act", pj[ct * 128:(ct + 1) * 128, t0 + tc * 512:t0 + (tc + 1) * 512], ost)
                    else:
                        for tb in range(nb):
                            po = mops.next()
                            for k in range(16):
                                lhsT = Tl(hT3.ap[:, k, tb * 128:(tb + 1) * 128], hbufs[tb // 4])
                                P.mm(po[:, 0:CG], lhsT=lhsT, rhs=wb3[:, k, :], start=(k == 0), stop=(k == 15))
                            vst = vstrot.next()
                            P.copy("act", vst, po[:, 0:CG])
                            P.dma("act", vtok[t0 + tb * 128:t0 + (tb + 1) * 128, (cg - 15) * CG:(cg - 14) * CG], vst)

        def phase_B(l):
            sb.off = const_off
            ps.off = 0
            TP = TPB
            NCH = TP // 128
            npiece = S // TP
            NIT = 2 * NCH
            wup = sb.alloc(RW, F32, "wup")
            P.dma("sp", wup[0:64, :], w_up[l])
            P.dma("sp", wup[64:128, :], a_up[l])
            Sf = [[sb.alloc(64, F32, "Sf%d_%d" % (hp, i)) for i in range(2)] for hp in range(6)]
            Sb = [[sb.alloc(64, BF16, "Sb%d_%d" % (hp, i)) for i in range(2)] for hp in range(6)]
            cur = [0] * 6
            for hp in range(6):
                P.memset("dve", Sf[hp][0], 0.0)
                P.memset("dve", Sb[hp][0], 0.0)

            def f32t(name, n=TP):
                return sb.alloc(n, F32, name)

            def bft(name, n=TP):
                return sb.alloc(n, BF16, name)
            lda, wa, wa2 = f32t("lda", TP + 1), f32t("wa"), f32t("wa2")
            ldr = Rot([[f32t("ld%s%d" % (nm, i), TP + 1) for nm in "rkvg"] for i in range(2)])
            tmpd = f32t("tmpd")
            rs_, ks_, vs_, gs_ = f32t("rs"), f32t("ks"), f32t("vs"), f32t("gs")
            sig, a_, cs, Winv, csm, Wm, e2, E2 = (f32t(n) for n in ("sig", "a", "cs", "Winv", "csm", "Wm", "e2", "E2"))
            kk, kk2, nrm, kkn, tmpk, kp, b_ = (f32t(n) for n in ("kk", "kk2", "nrm", "kkn", "tmpk", "kp", "b"))
            rkr, sgt = f32t("rkr"), f32t("sgt")
            yc, sq, rstd, yn = f32t("yc"), f32t("sq"), f32t("rstd"), f32t("yn")
            KH, BH = bft("KH"), bft("BH")
            sets = []
            for i in range(3):
                d = {"AR": bft("AR%d" % i, 2 * TP), "KB": bft("KB%d" % i), "BB": bft("BB%d" % i), "VB": bft("VB%d" % i),
                     "W": f32t("W%d" % i), "bon": f32t("bon%d" % i), "sg": f32t("sg%d" % i), "ysb": f32t("ysb%d" % i),
                     "SA1": [bft("SA1_%d_%d" % (i, j), 256) for j in range(NIT)],
                     "SA2": [bft("SA2_%d_%d" % (i, j), 256) for j in range(NIT)],
                     "TT": [bft("TT_%d_%d" % (i, j), 128) for j in range(NIT)]}
                d["AR4"] = d["AR"].re("p (c w n) -> p c w n", w=2, n=128)
                sets.append(d)
            yarot = Rot([bft("ya%d" % i) for i in range(2)])
            TOKr = Rot([bft("TOK%d" % i, 384) for i in range(2)])
            Xr = [Rot([bft("X%d_%d" % (i, j), 128) for j in range(2)]) for i in range(NIT)]
            XTr = [Rot([bft("XT%d_%d" % (i, j), 128) for j in range(2)]) for i in range(NIT)]
            Pr = [Rot([bft("P%d_%d" % (i, j), 128) for j in range(2)]) for i in range(NIT)]
            Zbr = Rot([bft("Zb%d" % i, 64) for i in range(2)])
            Ubr = Rot([bft("Ub%d" % i, 64) for i in range(2)])
            pgen = ps.bank("pgenP")
            pgc = ps.bank("pgenC")
            pxb = [ps.bank("pxb%d" % i) for i in range(3)]
            ppb = [ps.bank("ppb%d" % i) for i in range(2)]
            pmisc = ps.bank("pmisc")
            ptr = Tl(pmisc.ap.bitcast(BF16)[:, 0:384], pmisc.b)
            pzr = Rot([Tl(pmisc.ap[:, 192:256], pmisc.b), Tl(pmisc.ap[:, 256:320], pmisc.b)])
            pur = Rot([Tl(pmisc.ap[:, 320:384], pmisc.b), Tl(pmisc.ap[:, 384:448], pmisc.b)])
            psn = Tl(pmisc.ap[:, 448:512], pmisc.b)
            items = [(c, h) for c in range(NCH) for h in range(2)]
            TPv = "p (c n) -> p c n"

            def px_tiles(it):
                bk = pxb[(it // 2) % 3]
                o = (it % 2) * 256
                return bk[:, o:o + 128], bk[:, o + 128:o + 256]

            def prepA(pc, hp, st):
                t0 = pc * TP
                AR, AR4, KB, BB, VB, W, bon, sg = (st[k] for k in ("AR", "AR4", "KB", "BB", "VB", "W", "bon", "sg"))

                def load_shift(dst, row0, eng="sp"):
                    if t0 > 0:
                        P.dma(eng, dst, pj[row0:row0 + 128, t0 - 1:t0 + TP])
                    else:
                        P.memset("pool", dst[:, 0:1], 0.0)
                        P.dma(eng, dst[:, 1:TP + 1], pj[row0:row0 + 128, 0:TP])

                def lerp(dst, ld, mucol):
                    P.tt("pool", tmpd, ld[:, 0:TP], ld[:, 1:TP + 1], ALU.subtract)
                    P.stt(dst, tmpd, prm[:, mucol:mucol + 1], ld[:, 1:TP + 1], ALU.mult, ALU.add)
                if hp == 0:
                    load_shift(lda, 3072)
                    lerp(wa, lda, PM_MU + 24)
                    P.act(wa2[0:64, :], wa[0:64, :], AF.Tanh)
                    P.copy("pool", wa2[64:128, :], wa[64:128, :])
                lds = ldr.next()
                for i, (ld, r0) in enumerate(zip(lds, (0, 768, 1536, 2304))):
                    load_shift(ld, r0 + hp * 128, "sp" if i % 2 == 0 else "act")
                lerp(rs_, lds[0], PM_MU + hp)
                lerp(ks_, lds[1], PM_MU + 6 + hp)
                lerp(vs_, lds[2], PM_MU + 12 + hp)
                lerp(gs_, lds[3], PM_MU + 18 + hp)
                P.mm(pgen[:, 0:TP], lhsT=wup[0:64, hp * 128:(hp + 1) * 128], rhs=wa2[0:64, :])
                P.act(sig, pgen[:, 0:TP], AF.Sigmoid, bias=prm[:, PM_W0 + hp:PM_W0 + hp + 1])
                P.mm(pgen[:, 0:TP], lhsT=wup[64:128, hp * 128:(hp + 1) * 128], rhs=wa2[64:128, :])
                P.act(a_, pgen[:, 0:TP], AF.Sigmoid, bias=prm[:, PM_A0 + hp:PM_A0 + hp + 1])
                P.act(sgt, gs_, AF.Sigmoid)
                P.tt("pool", sg, sgt, gs_, ALU.mult)
                P.scan(cs, reset, sig, 0.0, ALU.mult, ALU.add)
                P.act(W, cs, AF.Exp, scale=-CDEC)
                P.act(Winv, cs, AF.Exp, scale=CDEC)
                P.tt("pool", csm, cs, sig, ALU.subtract)
                P.act(Wm, csm, AF.Exp, scale=-CDEC)
                cs3 = cs.re(TPv, n=128)
                P.tt("dve", e2.re(TPv, n=128), cs3[:, :, 127:128].bc([128, NCH, 128]), cs3, ALU.subtract)
                P.act(E2, e2, AF.Exp, scale=-CDEC)
                P.ts("dve", kk, ks_, prm[:, PM_KK + hp:PM_KK + hp + 1], None, ALU.mult)
                P.act(kk2, kk, AF.Square)
                return {"st": st, "hp": hp, "t0": t0, "X": {}, "XT": {}, "P": {}}

            def prepB(ctx):
                st, hp = ctx["st"], ctx["hp"]
                AR, AR4, KB, BB, VB, W, bon, sg = (st[k] for k in ("AR", "AR4", "KB", "BB", "VB", "W", "bon", "sg"))
                P.mm(pgen[:, 0:TP], lhsT=bones, rhs=kk2)
                P.act(nrm, pgen[:, 0:TP], AF.Sqrt)
                P.ts("dve", nrm, nrm, 1e-12, None, ALU.max)
                P.recip(nrm, nrm)
                P.tt("dve", kkn, kk, nrm, ALU.mult)
                P.ts("dve", tmpk, a_, -1.0, prm[:, PM_KA + hp:PM_KA + hp + 1], ALU.add, ALU.mult)
                P.stt(kp, tmpk, 1.0, ks_, ALU.add, ALU.mult)
                P.tt("dve", b_, kkn, a_, ALU.mult)
                P.stt(AR4[:, :, 0, :], kkn.re(TPv, n=128), -1.0, Wm.re(TPv, n=128), ALU.mult, ALU.mult)
                P.tt("pool", AR4[:, :, 1, :], rs_.re(TPv, n=128), W.re(TPv, n=128), ALU.mult)
                P.tt("dve", KH, kp, Winv, ALU.mult)
                P.tt("dve", BH, b_, Winv, ALU.mult)
                P.tt("dve", KB, kp, E2, ALU.mult)
                P.tt("pool", BB, b_, E2, ALU.mult)
                P.copy("act", VB, vs_)
                P.stt(rkr, rs_, prm[:, PM_RK + hp:PM_RK + hp + 1], kp, ALU.mult, ALU.mult)

            def prepC(ctx):
                st, hp = ctx["st"], ctx["hp"]
                AR, AR4, KB, BB, VB, W, bon, sg = (st[k] for k in ("AR", "AR4", "KB", "BB", "VB", "W", "bon", "sg"))
                SA1, SA2 = st["SA1"], st["SA2"]
                P.mm(pgen[:, 0:TP], lhsT=bones, rhs=rkr)
                P.tt("dve", bon, pgen[:, 0:TP], vs_, ALU.mult)
                for it, (c, h) in enumerate(items):
                    sl = slice(64 * h, 64 * h + 64)
                    csl = slice(c * 128, (c + 1) * 128)
                    pscb = pxb[it % 3]
                    ps1 = pscb[:, 0:256]
                    ps2 = pscb[:, 256:512]
                    arc = Tl(AR4.ap[sl, c].rearrange("p w n -> p (w n)"), AR.b)
                    P.mm(ps1, lhsT=BH[sl, csl], rhs=arc)
                    P.mm(ps2, lhsT=KH[sl, csl], rhs=arc)
                    ps3 = ppb[it // 4][:, (it % 4) * 128:(it % 4 + 1) * 128]
                    P.mm(ps3, lhsT=AR4[sl, c, 0, :], rhs=BH[sl, csl])
                    P.tt("dve", SA1[it], ps1, M2, ALU.mult)
                    P.tt("dve", SA2[it], ps2, M2, ALU.mult)
                    xt0 = XTr[it].next()
                    P.tt("dve", xt0, ps3, MLs, ALU.mult)
                    p0 = Pr[it].next()
                    P.tt("pool", p0, SA1[it][:, 0:128], ident, ALU.add)
                    ctx["X"][it], ctx["XT"][it], ctx["P"][it] = SA1[it][:, 0:128], xt0, p0

            def level(ctx, k):
                Xs, XTs, Ps, st = ctx["X"], ctx["XT"], ctx["P"], ctx["st"]
                for half in range(0, NIT, 4):
                    its = list(range(half, min(half + 4, NIT)))
                    pend = {}
                    for it in its:
                        pp = ppb[it // 4][:, (it % 4) * 128:(it % 4 + 1) * 128]
                        px, pxt = px_tiles(it)
                        if k >= 1:
                            P.mm(pp, lhsT=XTs[it], rhs=Ps[it])
                        if k < 5:
                            P.mm(px, lhsT=XTs[it], rhs=Xs[it])
                        if k < 6:
                            P.mm(pxt, lhsT=Xs[it], rhs=XTs[it])
                        pend[it] = (pp, px, pxt)
                    for it in its:
                        pp, px, pxt = pend[it]
                        if k >= 1:
                            pn_ = st["TT"][it] if k == 6 else Pr[it].next()
                            P.tt("dve", pn_, pp, Ps[it], ALU.add)
                            Ps[it] = pn_
                        if k < 5:
                            xn = Xr[it].next()
                            P.copy("act", xn, px)
                            Xs[it] = xn
                        if k < 6:
                            xtn = XTr[it].next()
                            P.copy("act", xtn, pxt)
                            XTs[it] = xtn

            def chain_step(ctx, c):
                st, hp = ctx["st"], ctx["hp"]
                AR4, KB, BB, VB, W, ysb = (st[k] for k in ("AR4", "KB", "BB", "VB", "W", "ysb"))
                SA1, SA2, Ps = st["SA1"], st["SA2"], ctx["P"]
                csl = slice(c * 128, (c + 1) * 128)
                TOK = TOKr.next()
                P.tr(ptr[:, 0:128], VB[:, csl], ident)
                P.tr(ptr[:, 128:256], KB[:, csl], ident)
                P.tr(ptr[:, 256:384], BB[:, csl], ident)
                P.copy("act", TOK, ptr)
                sfo, sbo = Sf[hp][cur[hp]], Sb[hp][cur[hp]]
                sfn, sbn = Sf[hp][1 - cur[hp]], Sb[hp][1 - cur[hp]]
                hv = []
                for h in range(2):
                    it = items.index((c, h))
                    sl = slice(64 * h, 64 * h + 64)
                    hv.append((it, sl, TOK[:, 64 * h:64 * h + 64], TOK[:, 128 + 64 * h:128 + 64 * h + 64],
                               TOK[:, 256 + 64 * h:256 + 64 * h + 64]))
                pzs, zbs, pus, ubs = [], [], [], []
                for (it, sl, vt, kbt, bbt) in hv:
                    pz = pzr.next()
                    P.mm(pz, lhsT=AR4[sl, c, 0, :], rhs=sbo[sl, :], start=True, stop=False)
                    P.mm(pz, lhsT=SA2[it][:, 0:128], rhs=vt, start=False, stop=True)
                    pzs.append(pz)
                for pz in pzs:
                    zb = Zbr.next()
                    P.copy("act", zb, pz)
                    zbs.append(zb)
                for (it, sl, vt, kbt, bbt), zb in zip(hv, zbs):
                    pu = pur.next()
                    P.mm(pu, lhsT=Ps[it], rhs=zb)
                    pus.append(pu)
                for pu in pus:
                    ub = Ubr.next()
                    P.copy("act", ub, pu)
                    ubs.append(ub)
                for (it, sl, vt, kbt, bbt), ub in zip(hv, ubs):
                    P.mm(pgc[sl, 0:128], lhsT=sbo[sl, :], rhs=AR4[sl, c, 1, :], start=True, stop=False)
                    P.mm(pgc[sl, 0:128], lhsT=ub, rhs=SA1[it][:, 128:256], start=False, stop=False)
                    P.mm(pgc[sl, 0:128], lhsT=vt, rhs=SA2[it][:, 128:256], start=False, stop=True)
                for (it, sl, vt, kbt, bbt), ub in zip(hv, ubs):
                    P.mm(psn[sl, :], lhsT=bbt, rhs=ub, start=True, stop=False)
                    P.mm(psn[sl, :], lhsT=kbt, rhs=vt, start=False, stop=True)
                P.copy("act", ysb[:, csl], pgc[:, 0:128])
                wc = W[:, c * 128 + 127:c * 128 + 128]
                P.stt(sfn, sfo, wc, psn, ALU.mult, ALU.add)
                P.stt(sbn, sfo, wc, psn, ALU.mult, ALU.add)
                cur[hp] = 1 - cur[hp]

            def post(ctx):
                st, hp, t0 = ctx["st"], ctx["hp"], ctx["t0"]
                ysb, bon, sg = st["ysb"], st["bon"], st["sg"]
                P.mm(pgc[:, 0:TP], lhsT=bones, rhs=ysb)
                P.stt(yc, pgc[:, 0:TP], -1.0 / 64, ysb, ALU.mult, ALU.add)
                P.act(sq, yc, AF.Square)
                P.mm(pgc[:, 0:TP], lhsT=bones, rhs=sq)
                P.act(rstd, pgc[:, 0:TP], AF.Sqrt, bias=epsg, scale=1.0 / 64)
                P.recip(rstd, rstd)
                P.tt("pool", yn, yc, rstd, ALU.mult)
                P.ts("dve", yn, yn, prm[:, PM_GG + hp:PM_GG + hp + 1], prm[:, PM_GB + hp:PM_GB + hp + 1], ALU.mult,
                     ALU.add)
                P.tt("pool", yn, yn, bon, ALU.add)
                ya = yarot.next()
                P.tt("dve", ya, yn, sg, ALU.mult)
                P.dma("sp", ymix[hp * 128:(hp + 1) * 128, t0:t0 + TP], ya)

            units = [(pc, hp) for pc in range(npiece) for hp in range(6)]

            def start_unit(ui):
                pc, hp = units[ui]
                return prepA(pc, hp, sets[ui % 3])
            ctxs = {0: start_unit(0)}
            prepB(ctxs[0])
            prepC(ctxs[0])
            for ui in range(len(units)):
                ctx = ctxs[ui]
                prev = ctxs.get(ui - 1)
                nxt = None
                for k in range(7):
                    level(ctx, k)
                    if k == 0 and ui + 1 < len(units):
                        nxt = ctxs[ui + 1] = start_unit(ui + 1)
                    if prev is not None and 1 <= k <= 4 and (k - 1) < NCH:
                        chain_step(prev, k - 1)
                    if k == 2 and nxt is not None:
                        prepB(nxt)
                    if k == 5 and prev is not None:
                        for c in range(4, NCH):
                            chain_step(prev, c)
                        post(prev)
                    if k == 6 and nxt is not None:
                        prepC(nxt)
                ctxs.pop(ui - 1, None)
            last = ctxs[len(units) - 1]
            for c in range(NCH):
                chain_step(last, c)
            post(last)

        def phase_C(l):
            sb.off = const_off
            ps.off = 0
            TP = min(1024, S)
            npiece = S // TP
            H = 16
            pwf = sb.alloc(512, F32, "pwf")
            pwb = sb.alloc(512, BF16, "pwb")
            P.dma("sp", pwf.re("p (g d) -> p g d", d=128), pool_w[l].rearrange("g c d -> c g d"))
            P.copy("dve", pwb, pwf)
            corr = sb.alloc(4 * 16, F32, "corr")
            for g, win in enumerate((2, 4, 8, 16)):
                for t in range(16):
                    P.memset("pool", corr[:, g * 16 + t:g * 16 + t + 1], float(win) / min(t + 1, win))
            urot = Rot([sb.alloc(TP + H, F32, "pu%d" % i) for i in range(2)])
            grot = Rot([sb.alloc(TP, F32, "pg%d" % i) for i in range(2)])
            sABr = Rot([[sb.alloc(TP + H, F32, "sA%d" % i), sb.alloc(TP + H, F32, "sB%d" % i)] for i in range(2)])
            dbfr = Rot([sb.alloc(TP, BF16, "dbf%d" % i) for i in range(2)])
            sgpr = Rot([sb.alloc(TP, F32, "sgp%d" % i) for i in range(2)])
            ybr = Rot([sb.alloc(TP, BF16, "yb%d" % i) for i in range(2)])
            pyr = Rot([ps.bank("pc%d" % i) for i in range(2)])
            for pc in range(npiece):
                t0 = pc * TP
                for g, win in enumerate((2, 4, 8, 16)):
                    u = urot.next()
                    row = A_COLS + g * 128
                    if t0 > 0:
                        P.dma("sp", u, pj[row:row + 128, t0 - H:t0 + TP])
                    else:
                        P.memset("pool", u[:, 0:H], 0.0)
                        P.dma("sp", u[:, H:H + TP], pj[row:row + 128, 0:TP])
                    pg_ = grot.next()
                    P.dma("act", pg_, pj[A_COLS + PW + g * 128:A_COLS + PW + (g + 1) * 128, t0:t0 + TP])
                    src = u
                    sh = 1
                    dsts = sABr.next()
                    dbf = dbfr.next()
                    sgp = sgpr.next()
                    lo = 0
                    for lev in range(g + 1):
                        dst = dsts[lev % 2]
                        lo += sh
                        P.tt("pool" if lev % 2 else "dve", dst[:, lo:TP + H], src[:, lo:TP + H], src[:, lo - sh:TP + H - sh],
                             ALU.add)
                        src = dst
                        sh *= 2
                    ssum = src
                    if t0 == 0:
                        P.tt("dve", ssum[:, H:H + 16], ssum[:, H:H + 16], corr[:, g * 16:(g + 1) * 16], ALU.mult)
                    P.stt(dbf, ssum[:, H:H + TP], 1.0 / win, u[:, H:H + TP], ALU.mult, ALU.subtract)
                    P.act(sgp, pg_, AF.Silu)
                    yb = ybr.next()
                    for hf in range(TP // 512):
                        py = pyr.next()
                        P.mm(py, lhsT=pwb[:, g * 128:(g + 1) * 128], rhs=dbf[:, hf * 512:(hf + 1) * 512])
                        P.stt(yb[:, hf * 512:(hf + 1) * 512], py, prm[:, PM_PS + g:PM_PS + g + 1],
                              sgp[:, hf * 512:(hf + 1) * 512], ALU.mult, ALU.mult)
                    P.dma("sp", ymix[RW + g * 128:RW + (g + 1) * 128, t0:t0 + TP], yb)

        def phase_D(l):
            sb.off = const_off
            ps.off = 0
            TPq = min(1024, S)
            QH = sb.alloc(S, BF16, "QH")
            KHt = sb.alloc(S, BF16, "KHt")
            V = sb.alloc(NB * 128, BF16, "V")
            V3 = V.re("p (n c) -> p n c", c=128)
            sgc = sb.alloc(S, BF16, "sgc")
            ycst = sb.alloc(S, BF16, "ycst")
            SGr = Rot([sb.alloc(S, F32, "SG%d" % i) for i in range(3)])
            EZr = Rot([sb.alloc(S, BF16, "EZ%d" % i) for i in range(3)])
            Pnr = Rot([sb.alloc(S, F32, "Pn%d" % i) for i in range(2)])
            Wtr = Rot([sb.alloc(S, BF16, "Wt%d" % i) for i in range(2)])
            WTr = Rot([sb.alloc(S, BF16, "WT%d" % i) for i in range(2)])
            qf = Rot([sb.alloc(TPq, F32, "qf%d" % i) for i in range(2)])
            sqqr = Rot([sb.alloc(TPq, F32, "sqq%d" % i) for i in range(2)])
            rtr = Rot([sb.alloc(TPq, F32, "rt%d" % i) for i in range(2)])
            zr = Rot([ps.bank("z%d" % i) for i in range(3)])
            ptwr = Rot([ps.bank("ptw%d" % i) for i in range(2)])
            por = Rot([ps.bank("po%d" % i) for i in range(2)])
            pgen = ps.bank("pgenD")
            for hp in range(6):
                P.dma("sp", V3, vtok[:, hp * 128:(hp + 1) * 128].rearrange("(n p) c -> p n c", p=128))
                for (dst, row0, gcol) in ((QH, C0 + hp * 128, qgs), (KHt, C0 + SW + hp * 128, prm[:, PM_KN:PM_KN + 1])):
                    for pc in range(S // TPq):
                        q = qf.next()
                        P.dma("sp", q, pj[row0:row0 + 128, pc * TPq:(pc + 1) * TPq])
                        sqq = sqqr.next()
                        rt = rtr.next()
                        P.act(sqq, q, AF.Square)
                        for hf in range(TPq // 512):
                            hs = slice(hf * 512, (hf + 1) * 512)
                            P.mm(pgen, lhsT=bones, rhs=sqq[:, hs])
                            P.act(rt[:, hs], pgen, AF.Sqrt, bias=epsr, scale=1.0 / 64)
                        P.recip(rt, rt)
                        P.stt(dst[:, pc * TPq:(pc + 1) * TPq], q, gcol, rt, ALU.mult, ALU.mult)
                for pc in range(S // TPq):
                    q = qf.next()
                    r0 = C0 + 3 * SW + hp * 128
                    P.dma("act", q, pj[r0:r0 + 128, pc * TPq:(pc + 1) * TPq])
                    P.act(sgc[:, pc * TPq:(pc + 1) * TPq], q, AF.Silu)
                items = [(T, h) for T in range(NB) for h in range(2)]
                state = {}

                def stage1a(T, h):
                    sl = slice(64 * h, 64 * h + 64)
                    kend = (T + 1) * 128
                    SG = SGr.next()
                    EZ = EZr.next()
                    for ck in range((kend + 511) // 512):
                        w_ = min(512, kend - ck * 512)
                        pz = zr.next()
                        P.mm(pz[:, 0:w_], lhsT=QH[sl, T * 128:(T + 1) * 128], rhs=KHt[sl, ck * 512:ck * 512 + w_])
                        P.act(SG[:, ck * 512:ck * 512 + w_], pz[:, 0:w_], AF.Sigmoid, scale=-1.0)
                        P.act(EZ[:, ck * 512:ck * 512 + w_], pz[:, 0:w_], AF.Sigmoid)
                    dsl = slice(T * 128, kend)
                    P.tt("pool", SG[:, dsl], SG[:, dsl], MLs, ALU.mult)
                    P.tt("pool", SG[:, dsl], SG[:, dsl], MGE, ALU.add)
                    P.tt("pool", EZ[:, dsl], EZ[:, dsl], MLs, ALU.mult)
                    state[("a", T, h)] = (SG, EZ)

                def stage1b(T, h):
                    kend = (T + 1) * 128
                    SG, EZ = state.pop(("a", T, h))
                    Pn = Pnr.next()
                    P.scan(Pn[:, 0:kend][:, ::-1], SG[:, 0:kend][:, ::-1], zero1.bc([128, kend]), 1.0, ALU.mult, ALU.add)
                    state[("b", T, h)] = (EZ, Pn)

                def stage1c(T, h):
                    kend = (T + 1) * 128
                    EZ, Pn = state.pop(("b", T, h))
                    Wt = Wtr.next()
                    npl = ((kend - 1) * 3 // 8) // 64 * 64
                    if npl > 0:
                        P.tt("pool", Wt[:, 0:npl], EZ[:, 0:npl], Pn[:, 1:npl + 1], ALU.mult)
                    P.tt("dve", Wt[:, npl:kend - 1], EZ[:, npl:kend - 1], Pn[:, npl + 1:kend], ALU.mult)
                    P.memset("pool", Wt[:, kend - 1:kend], 0.0)
                    state[(T, h)] = Wt

                def stage2(T, h):
                    sl = slice(64 * h, 64 * h + 64)
                    Wt = state.pop((T, h))
                    WT = WTr.next()
                    nsb = T + 1
                    for g0 in range(0, nsb, 4):
                        n = min(4, nsb - g0)
                        pt = ptwr.next()
                        ptb = Tl(pt.ap.bitcast(BF16)[:, 0:512], pt.b)
                        for j in range(n):
                            P.tr(ptb[:, j * 128:(j + 1) * 128], Wt[:, (g0 + j) * 128:(g0 + j + 1) * 128], ident)
                        P.copy("act", WT[:, g0 * 128:(g0 + n) * 128], ptb[:, 0:n * 128])
                    if h == 0:
                        state["po"] = por.next()
                    po = state["po"]
                    for sbk in range(nsb):
                        P.mm(po[sl, 0:128], lhsT=V3[:, sbk, 64 * h:64 * h + 64], rhs=WT[:, sbk * 128:(sbk + 1) * 128],
                             start=(sbk == 0), stop=(sbk == nsb - 1))
                    if h == 1:
                        P.tt("dve", ycst[:, T * 128:(T + 1) * 128], po[:, 0:128], sgc[:, T * 128:(T + 1) * 128], ALU.mult)

                n_it = len(items)
                for i in range(n_it + 3):
                    if i < n_it:
                        stage1a(*items[i])
                    if 0 <= i - 1 < n_it:
                        stage1b(*items[i - 1])
                    if 0 <= i - 2 < n_it:
                        stage1c(*items[i - 2])
                    if 0 <= i - 3 < n_it:
                        stage2(*items[i - 3])
                P.dma("sp", ymix[RW + PW + hp * 128:RW + PW + (hp + 1) * 128, :], ycst)

        def phase_E(l, xsrc):
            sb.off = const_off
            ps.off = 0
            wo = sb.alloc(16 * D, BF16, "wo")
            wo3 = wo.re("p (k c) -> p k c", c=D)
            wst = Rot([sb.alloc(4 * D, F32, "wst%d" % i) for i in range(2)])
            for kq in range(4):
                w = wst.next()
                w3 = w.re("p (k c) -> p k c", c=D)
                P.dma("sp" if kq % 2 == 0 else "act", w3,
                      w_out[l, kq * 512:(kq + 1) * 512, :].rearrange("(k p) c -> p k c", p=128))
                P.copy("pool", wo3[:, kq * 4:kq * 4 + 2, :], w3[:, 0:2, :])
                P.copy("dve", wo3[:, kq * 4 + 2:kq * 4 + 4, :], w3[:, 2:4, :])
            TQ = 512
            yr = Rot([sb.alloc(16 * TQ, BF16, "ymx%d" % i) for i in range(2)])
            xr = Rot([sb.alloc(D, F32, "xe%d" % i) for i in range(3)])
            pr = Rot([ps.bank("pe%d" % i) for i in range(8)])
            for tq in range(S // TQ):
                ym = yr.next()
                ym3 = ym.re("p (k t) -> p k t", t=TQ)
                P.dma("sp", ym3, ymix[:, tq * TQ:(tq + 1) * TQ].rearrange("(k p) t -> p k t", p=128))
                for tb in range(TQ // 128):
                    r0 = tq * TQ + tb * 128
                    xe = xr.next()
                    P.dma("sp", xe, xsrc[r0:r0 + 128, :])
                    for cq in range(4):
                        po = pr.next()
                        for k in range(16):
                            P.mm(po, lhsT=ym3[:, k, tb * 128:(tb + 1) * 128], rhs=wo3[:, k, cq * 512:(cq + 1) * 512],
                                 start=(k == 0), stop=(k == 15))
                        P.tt("dve", xe[:, cq * 512:(cq + 1) * 512], po, xe[:, cq * 512:(cq + 1) * 512], ALU.add)
                    P.dma("act", out[r0:r0 + 128, :], xe)

        for l in range(L):
            xsrc = x_in if l == 0 else out
            load_params(l)
            P.barrier()
            phase_A(l, xsrc)
            P.barrier()
            phase_B(l)
            P.barrier()
            phase_C(l)
            P.barrier()
            phase_D(l)
            P.barrier()
            phase_E(l, xsrc)
            P.barrier()
        P.emit()
    import os
    if os.environ.get("KSTATS"):
        print("op counts", {e: len(P.ops[e]) for e in ENGS}, flush=True)
    return nc


_CACHE = {}


def run(inputs, S, L, n_cores, dbg=False, trace=False):
    key = (S, L, dbg)
    if key not in _CACHE:
        _CACHE[key] = build(S, L, dbg)
    nc = _CACHE[key]
    x = np.ascontiguousarray(np.asarray(inputs["x"], dtype=np.float32))
    B = x.shape[0]
    shared = {}
    for k, v in inputs.items():
        if k == "x":
            continue
        a = np.ascontiguousarray(np.asarray(v, dtype=np.float32))
        if k == "r_k":
            a = a.reshape(a.shape[0], -1)
        shared[k] = a
    in_maps = []
    for c in range(n_cores):
        m = dict(shared)
        m["x"] = x[c % B]
        in_maps.append(m)
    res = run_bass_kernel_spmd(nc, in_maps, core_ids=list(range(n_cores)))
    return res


def kernel(**inputs):
    x = np.asarray(inputs["x"])
    B, S, _ = x.shape
    L = np.asarray(inputs["norm_g"]).shape[0]
    res = run(inputs, S, L, 8)
    return np.stack([np.asarray(res.results[b]["out"], dtype=np.float32) for b in range(B)], axis=0)
```
